# Optimizing a Trainium2 kernel written in Bass

```python
import math
import jax, jax.numpy as jnp
from jax import lax
import numpy as np

D_MODEL = 2048
BATCH = 2
SEQ = 8192
DEPTH = 2

N_MIXERS = 2
N_HEADS = 16
HEAD_DIM = D_MODEL // N_HEADS
DILATED_GROUPS = ((128, 1), (512, 4), (2048, 16))
N_GROUPS = len(DILATED_GROUPS)
D_FF = 5632
CONV_W = 3
NORM_EPS = 1e-5
ALIBI_MAX = 8.0
NEG_INF = -1e30

kernel_name = "hybrid_shortconv_dilated_alibi_encoder"


def alibi_slopes(n_heads):
    return jnp.asarray(2.0 ** (-ALIBI_MAX * np.arange(1, n_heads + 1) / n_heads), dtype=jnp.float32)


def rmsnorm(x, g):
    xf = x.astype(jnp.float32)
    y = xf * lax.rsqrt(jnp.mean(xf * xf, axis=-1, keepdims=True) + NORM_EPS)
    return (y * g.astype(jnp.float32)).astype(x.dtype)


def dwconv3(u, w, b):
    up = jnp.pad(u, ((0, 0), (1, 1), (0, 0)))
    return up[:, :-2] * w[0] + up[:, 1:-1] * w[1] + up[:, 2:] * w[2] + b


def short_conv_mixer(h, w_in, conv_w, conv_b, w_out):
    z = h @ w_in
    u, gate_b, gate_c = jnp.split(z, 3, axis=-1)
    y = gate_b * dwconv3(gate_c * u, conv_w, conv_b)
    return y @ w_out


def to_strided(x, d):
    B, S = x.shape[:2]
    rest = x.shape[2:]
    y = x.reshape((B, S // d, d) + rest)
    y = jnp.swapaxes(y, 1, 2)
    return y.reshape((B * d, S // d) + rest)


def from_strided(y, d, B):
    N, L = y.shape[:2]
    rest = y.shape[2:]
    z = y.reshape((B, d, L) + rest)
    z = jnp.swapaxes(z, 1, 2)
    return z.reshape((B, L * d) + rest)


def banded_attention(q, k, v, slopes, dil, half):
    N, L, H, Dh = q.shape
    blk = half
    nb = -(-L // blk)
    Lp = nb * blk
    qb = jnp.pad(q, ((0, 0), (0, Lp - L), (0, 0), (0, 0))).reshape(N, nb, blk, H, Dh)
    kp = jnp.pad(k, ((0, 0), (blk, Lp - L + blk), (0, 0), (0, 0))).reshape(N, nb + 2, blk, H, Dh)
    vp = jnp.pad(v, ((0, 0), (blk, Lp - L + blk), (0, 0), (0, 0))).reshape(N, nb + 2, blk, H, Dh)
    kw = jnp.concatenate([kp[:, :-2], kp[:, 1:-1], kp[:, 2:]], axis=2)
    vw = jnp.concatenate([vp[:, :-2], vp[:, 1:-1], vp[:, 2:]], axis=2)
    qi = jnp.arange(Lp).reshape(nb, blk)
    kj = jnp.arange(nb)[:, None] * blk + jnp.arange(3 * blk)[None, :] - blk
    delta = jnp.abs(kj[:, None, :] - qi[:, :, None])
    valid = (delta <= half) & ((kj >= 0) & (kj < L))[:, None, :]
    s = jnp.einsum('nbqhd,nbkhd->nbhqk', qb, kw).astype(jnp.float32) * (Dh ** -0.5)
    bias = -slopes[None, :, None, None] * (dil * delta).astype(jnp.float32)[:, None, :, :]
    s = jnp.where(valid[:, None, :, :][None], s + bias[None], NEG_INF)
    m = jnp.max(s, axis=-1, keepdims=True)
    p = jnp.exp(s - m)
    den = jnp.sum(p, axis=-1)
    o = jnp.einsum('nbhqk,nbkhd->nbqhd', p, vw.astype(jnp.float32))
    den_t = jnp.swapaxes(den, 2, 3)
    o = o / den_t[..., None]
    lse = jnp.swapaxes(m[..., 0], 2, 3) + jnp.log(den_t)
    return o.reshape(N, Lp, H, Dh)[:, :L], lse.reshape(N, Lp, H)[:, :L]


def dilated_attention_mixer(h, w_qkv, w_out):
    B, S, D = h.shape
    qkv = (h @ w_qkv).reshape(B, S, N_GROUPS, 3, N_HEADS, HEAD_DIM)
    slopes = alibi_slopes(N_HEADS)
    outs, lses = [], []
    for g, (window, dil) in enumerate(DILATED_GROUPS):
        half = (window // 2) // dil
        q = to_strided(qkv[:, :, g, 0], dil)
        k = to_strided(qkv[:, :, g, 1], dil)
        v = to_strided(qkv[:, :, g, 2], dil)
        o, lse = banded_attention(q, k, v, slopes, dil, half)
        outs.append(from_strided(o, dil, B))
        lses.append(from_strided(lse, dil, B))
    wts = jax.nn.softmax(jnp.stack(lses, axis=0), axis=0)
    o = jnp.sum(wts[..., None] * jnp.stack(outs, axis=0), axis=0)
    return o.reshape(B, S, D).astype(h.dtype) @ w_out


def conv_ffn(h, w_up, conv_w, conv_b, w_down):
    u = dwconv3(h @ w_up, conv_w, conv_b)
    a, b = jnp.split(u, 2, axis=-1)
    return (jax.nn.silu(a) * b) @ w_down


def setup_inputs(seed: int = 0) -> dict:
    key = jax.random.key(seed)
    ks = jax.random.split(key, 16)
    D, F = D_MODEL, D_FF
    n_a = (DEPTH + N_MIXERS - 1) // N_MIXERS
    n_b = DEPTH // N_MIXERS
    nrm = lambda k, shape, s: jax.random.normal(k, shape, jnp.float32) * s
    return {
        "x": nrm(ks[0], (BATCH, SEQ, D), 1.0),
        "mix_norm_g": 1.0 + nrm(ks[1], (DEPTH, D), 0.02),
        "ffn_norm_g": 1.0 + nrm(ks[2], (DEPTH, D), 0.02),
        "final_norm_g": 1.0 + nrm(ks[3], (D,), 0.02),
        "sc_w_in": nrm(ks[4], (n_a, D, 3 * D), D ** -0.5),
        "sc_conv_w": nrm(ks[5], (n_a, CONV_W, D), CONV_W ** -0.5),
        "sc_conv_b": nrm(ks[6], (n_a, D), 0.01),
        "sc_w_out": nrm(ks[7], (n_a, D, D), D ** -0.5),
        "attn_w_qkv": nrm(ks[8], (n_b, D, N_GROUPS * 3 * N_HEADS * HEAD_DIM), D ** -0.5),
        "attn_w_out": nrm(ks[9], (n_b, N_HEADS * HEAD_DIM, D), (N_HEADS * HEAD_DIM) ** -0.5),
        "ffn_w_up": nrm(ks[10], (DEPTH, D, 2 * F), D ** -0.5),
        "ffn_conv_w": nrm(ks[11], (DEPTH, CONV_W, 2 * F), CONV_W ** -0.5),
        "ffn_conv_b": nrm(ks[12], (DEPTH, 2 * F), 0.01),
        "ffn_w_down": nrm(ks[13], (DEPTH, F, D), F ** -0.5),
    }


def reference(x, mix_norm_g, ffn_norm_g, final_norm_g, sc_w_in, sc_conv_w, sc_conv_b, sc_w_out,
              attn_w_qkv, attn_w_out, ffn_w_up, ffn_conv_w, ffn_conv_b, ffn_w_down):
    for i in range(DEPTH):
        h = rmsnorm(x, mix_norm_g[i])
        j = i // N_MIXERS
        if i % N_MIXERS == 0:
            x = x + short_conv_mixer(h, sc_w_in[j], sc_conv_w[j], sc_conv_b[j], sc_w_out[j])
        else:
            x = x + dilated_attention_mixer(h, attn_w_qkv[j], attn_w_out[j])
        h = rmsnorm(x, ffn_norm_g[i])
        x = x + conv_ffn(h, ffn_w_up[i], ffn_conv_w[i], ffn_conv_b[i], ffn_w_down[i])
    return rmsnorm(x, final_norm_g)
```

```python
import numpy as np
from contextlib import ExitStack
import concourse.bass as bass
import concourse.mybir as mybir
from concourse.bass_utils import run_bass_kernel_spmd

F32 = mybir.dt.float32
BF16 = mybir.dt.bfloat16
AF = mybir.ActivationFunctionType
ALU = mybir.AluOpType

D = 2048
KC = 16
T = 2048
NCORES = 8
SEQ = 8192
F = 5632
FC = 44
NH = 16
NG = 3
DILS = (1, 4, 16)
HALO = 1024
EPS = 1e-5
FG_SIZES = [15, 15, 14]
FG_OFF = [0, 15, 30]
FGROUPS = len(FG_SIZES)
FGC = max(FG_SIZES)

ENGS = ["sync", "scalar", "vector", "gpsimd", "tensor"]


class Sem:
    def __init__(self, nc, stack, name):
        self.h = stack.enter_context(nc.semaphore(uid(name)))
        self.n = 0

    def inc(self, k):
        self.n += k
        return (self, self.n)


_SEMS = {}


def get_sem(nc, name):
    key = (id(nc), name)
    if key not in _SEMS:
        stack = _SEMS.setdefault((id(nc), "__stack__"), ExitStack())
        _SEMS[key] = Sem(nc, stack, name)
    return _SEMS[key]


def release_sems(nc):
    st = _SEMS.pop((id(nc), "__stack__"), None)
    for k in [k for k in _SEMS if k[0] == id(nc)]:
        del _SEMS[k]
    if st is not None:
        st.close()


class Prog:
    def __init__(self, nc, stack):
        self.nc = nc
        self.stack = stack
        self.ops = {e: [] for e in ENGS}
        self.done = {e: get_sem(nc, "done_" + e) for e in ["scalar", "vector", "gpsimd", "tensor"]}
        self.waited = {}
        self.last_w = {}
        self.readers = {}
        self.dma_sems = []

    def sem(self, name):
        s = get_sem(self.nc, name)
        if s not in self.dma_sems:
            self.dma_sems.append(s)
        return s

    def wait(self, eng, tok):
        if tok is None:
            return
        s, v = tok
        key = (eng, id(s))
        if self.waited.get(key, 0) >= v:
            return
        self.waited[key] = v
        self.ops[eng].append(lambda e, s=s, v=v: e.wait_ge(s.h, v))

    def _deps(self, eng, reads, writes, extra):
        for w in extra:
            self.wait(eng, w)
        for k in reads:
            if k in self.last_w:
                self.wait(eng, self.last_w[k])
        for k in writes:
            if k in self.last_w:
                self.wait(eng, self.last_w[k])
            for tok in self.readers.get(k, {}).values():
                self.wait(eng, tok)

    def _track(self, tok, reads, writes):
        s, v = tok
        for k in reads:
            self.readers.setdefault(k, {})[id(s)] = tok
        for k in writes:
            self.last_w[k] = tok
            self.readers[k] = {}

    def op(self, eng, fn, reads=(), writes=(), extra=(), signal=True):
        self._deps(eng, reads, writes, extra)
        if not signal:
            self.ops[eng].append(lambda e, fn=fn: fn(e))
            return None
        s = self.done[eng]
        tok = s.inc(1)
        self.ops[eng].append(lambda e, fn=fn, s=s: fn(e).then_inc(s.h, 1))
        self._track(tok, reads, writes)
        return tok

    def dma(self, eng, out, in_, sem, reads=(), writes=(), extra=(), slow=False):
        self._deps(eng, reads, writes, extra)
        tok = sem.inc(16)
        kw = {"allow_slow_non_contiguous": True} if slow else {}
        self.ops[eng].append(
            lambda e, out=out, in_=in_, sem=sem, kw=kw: e.dma_start(
                out=(out(e) if callable(out) else out),
                in_=(in_(e) if callable(in_) else in_), **kw).then_inc(sem.h, 16))
        self._track(tok, reads, writes)
        return tok

    def finish(self):
        for s in self.dma_sems:
            if s.n > 0:
                self.wait("sync", (s, s.n))

    def emit(self):
        self.finish()
        nc = self.nc
        with nc.Block() as block:
            for name in ENGS:
                ops = self.ops[name]

                def body(e, ops=ops):
                    for f in ops:
                        f(e)
                getattr(block, name)(body)


_UID = [0]


def uid(name):
    _UID[0] += 1
    return f"{name}_{_UID[0]}"


class Stage:
    def __init__(self, nc):
        self.nc = nc
        self.st = ExitStack()
        self.P = Prog(nc, self.st)
        self.ps = [self.st.enter_context(nc.psum_tensor(uid(f"ps{i}"), [128, 512], F32)) for i in range(8)]
        self.psn = 0

    def sb(self, shape, dt, name=None):
        return self.st.enter_context(self.nc.sbuf_tensor(uid(name or "t"), shape, dt))

    def get(self, key, fn):
        if not hasattr(self, "cache"):
            self.cache = {}
        if key not in self.cache:
            self.cache[key] = fn()
        return self.cache[key]

    def wpool(self, nslots=3, wsize=6144):
        def mk():
            return {"buf": [self.sb([128, wsize], BF16, f"wraw{i}") for i in range(nslots)],
                    "sem": [self.P.sem(f"s_w{i}") for i in range(nslots)], "n": 0, "size": wsize}
        return self.get("wpool", mk)

    def note(self, name, tok):
        if tok is None:
            return
        d = self.get(("prod", name), dict)
        d[id(tok[0])] = tok

    def prod(self, name):
        return list(self.get(("prod", name), dict).values())

    def bank(self):
        b = self.psn % 8
        self.psn += 1
        return b

    def close(self):
        self.P.emit()
        self.st.close()


def stage_norm(nc, xT, tok0, ntok, gcol, h=None, hoff=0, outT=None, out0=0, xsrc=None):
    S = Stage(nc)
    P = S.P
    TN = 512
    nt = (ntok + TN - 1) // TN
    NSET = 2
    xs = [S.sb([128, KC, TN], F32, f"xs{i}") for i in range(NSET)]
    sq = [S.sb([128, KC, TN], BF16, f"sq{i}") for i in range(NSET)]
    rs = [S.sb([128, TN], F32, f"rs{i}") for i in range(NSET)]
    ones = S.sb([128, 128], BF16, "ones")
    s_x = [P.sem(f"s_x{i}") for i in range(NSET)]
    s_o = [P.sem(f"s_o{i}") for i in range(NSET)]
    ob = None
    if outT is not None:
        ob = [S.sb([128, KC, TN], F32, f"ob{i}") for i in range(NSET)]
    epsc = S.sb([128, 1], F32, "epsc")
    P.op("vector", lambda e: e.memset(ones[:, :], 1.0), writes=["ones"])
    P.op("vector", lambda e: e.memset(epsc[:, :], EPS), writes=["epsc"])
    xv = xT.rearrange("(kc p) t -> p kc t", p=128) if xT is not None else None
    ov = outT.rearrange("(kc p) t -> p kc t", p=128) if outT is not None else None
    for t in range(nt):
        s = t % NSET
        a = t * TN
        n = min(TN, ntok - a)
        src = xsrc(a, n) if xsrc is not None else xv[:, :, tok0 + a:tok0 + a + n]
        if isinstance(src, list):
            tokp = None
            for pi, (off, m, sp) in enumerate(src):
                tokp = P.dma("sync", xs[s][:, :, off:off + m], sp, s_x[s], writes=([("xs", s)] if pi == 0 else []))
            P.last_w[("xs", s)] = tokp
        else:
            P.dma("sync", xs[s][:, :, 0:n], src, s_x[s], writes=[("xs", s)])
        P.op("scalar", lambda e, s=s, n=n: e.activation(out=sq[s][:, :, 0:n], in_=xs[s][:, :, 0:n], func=AF.Square),
             reads=[("xs", s)], writes=[("sq", s)])
        b = S.bank()
        for kc in range(KC):
            last = kc == KC - 1
            fn = (lambda e, b=b, s=s, kc=kc, n=n: e.matmul(S.ps[b][:, 0:n], ones[:, :], sq[s][:, kc, 0:n],
                                                          start=(kc == 0), stop=(kc == KC - 1)))
            if kc == 0:
                P._deps("tensor", ["ones", ("sq", s)], [("ps", b)], [])
            if last:
                P.op("tensor", fn, reads=["ones", ("sq", s)], writes=[("ps", b)])
            else:
                P.op("tensor", fn, signal=False)
        P.op("scalar", lambda e, s=s, b=b, n=n: e.activation(
            out=rs[s][:, 0:n], in_=S.ps[b][:, 0:n], func=AF.Sqrt, bias=epsc[:, 0:1], scale=1.0 / D),
            reads=[("ps", b), "epsc"], writes=[("rs", s)])
        P.op("vector", lambda e, s=s, n=n: e.reciprocal(out=rs[s][:, 0:n], in_=rs[s][:, 0:n]),
             reads=[("rs", s)], writes=[("rs", s)])
        for kc in range(KC):
            eng = "vector"
            if outT is None:
                dst = h[:, kc, hoff + a:hoff + a + n]
                wr = []
            else:
                dst = ob[s][:, kc, 0:n]
                wr = [("ob", s, kc)]
            P.op(eng, lambda e, s=s, kc=kc, n=n, dst=dst: e.scalar_tensor_tensor(
                out=dst, in0=xs[s][:, kc, 0:n], scalar=gcol[:, kc:kc + 1], in1=rs[s][:, 0:n],
                op0=ALU.mult, op1=ALU.mult),
                reads=[("xs", s), ("rs", s)], writes=wr)
        if outT is not None:
            P.dma("sync", ov[:, :, out0 + a:out0 + a + n], ob[s][:, :, 0:n], s_o[s],
                  reads=[("ob", s, kc) for kc in range(KC)])
    S.close()


def run_linear(S, Wv, KCn, groups, rhs, tiles, epilogue, pre=None, nslots=3, extra=(), wsize=6144):
    P = S.P
    G = max(len(g) for g in groups)
    pool = S.wpool(nslots, wsize)
    assert KCn * G * 128 <= pool["size"]
    nsl = len(pool["buf"])
    first = True
    for gi, cols in enumerate(groups):
        slot = pool["n"] % nsl
        pool["n"] += 1
        wb = pool["buf"][slot][:, 0:KCn * G * 128].rearrange("p (k g c) -> p k g c", g=G, c=128)
        wtok = None
        for ci, c0 in enumerate(cols):
            wtok = P.dma("gpsimd", wb[:, :, ci, :], Wv[:, :, c0:c0 + 128], pool["sem"][slot],
                         writes=([("wb", slot)] if ci == 0 else []))
        P.last_w[("wb", slot)] = wtok
        for ti, n in enumerate(tiles):
            if pre is not None:
                pre(gi, ti)
            banks = []
            for ci in range(len(cols)):
                b = S.bank()
                banks.append(b)
                P._deps("tensor", [("wb", slot)], [("ps", b)], extra if first else [])
                first = False
                for kc in range(KCn):
                    fn = (lambda e, b=b, wb=wb, ci=ci, kc=kc, ti=ti, n=n: e.matmul(
                        S.ps[b][:, 0:n], wb[:, kc, ci, :], rhs(kc, ti),
                        start=(kc == 0), stop=(kc == KCn - 1)))
                    if kc == KCn - 1:
                        P.op("tensor", fn, reads=[("wb", slot)], writes=[("ps", b)])
                    else:
                        P.op("tensor", fn, signal=False)
            epilogue(gi, ti, banks, n)


def conv_tiles(nout, tn=410):
    tiles = []
    a = 0
    while a < nout:
        n = min(tn, nout - a)
        tiles.append((a, n))
        a += n
    return tiles


def stage_sconv_in(nc, w_in, convp, h, y, S=None):
    own = S is None
    S = S or Stage(nc)
    P = S.P
    ct = conv_tiles(T + 2)
    Wv = w_in.rearrange("(kc p) m -> p kc m", p=128)
    groups = [[j * 128, D + j * 128, 2 * D + j * 128] for j in range(KC)]
    NS = 3
    cs = [S.sb([128, 412], F32, f"cs{i}") for i in range(NS)]
    cu = [S.sb([128, 412], F32, f"cu{i}") for i in range(NS)]
    a1 = [S.sb([128, 412], F32, f"a1{i}") for i in range(NS)]
    cnt = [0]

    def rhs(kc, ti):
        a, n = ct[ti]
        return h[:, kc, a:a + n + 2]

    def epi(gi, ti, banks, n2):
        a, n = ct[ti]
        bu, bb, bc = banks
        s = cnt[0] % NS
        cnt[0] += 1
        j = gi
        w0 = convp[:, j, 0:1]
        w1 = convp[:, j, 1:2]
        w2 = convp[:, j, 2:3]
        bia = convp[:, j, 3:4]
        P.op("scalar", lambda e: e.activation(out=cs[s][:, 0:n + 2], in_=S.ps[bc][:, 0:n + 2], func=AF.Copy),
             reads=[("ps", bc)], writes=[("cs", s)])
        P.op("vector", lambda e: e.tensor_tensor(out=cu[s][:, 0:n + 2], in0=S.ps[bu][:, 0:n + 2],
                                                 in1=cs[s][:, 0:n + 2], op=ALU.mult),
             reads=[("ps", bu), ("cs", s)], writes=[("cu", s)])
        P.op("scalar", lambda e: e.activation(out=a1[s][:, 0:n], in_=cu[s][:, 1:n + 1], func=AF.Identity,
                                              bias=bia, scale=w1),
             reads=[("cu", s)], writes=[("a1", s)])
        P.op("vector", lambda e: e.scalar_tensor_tensor(out=a1[s][:, 0:n], in0=cu[s][:, 0:n], scalar=w0,
                                                        in1=a1[s][:, 0:n], op0=ALU.mult, op1=ALU.add),
             reads=[("cu", s), ("a1", s)], writes=[("a1", s)])
        P.op("vector", lambda e: e.scalar_tensor_tensor(out=a1[s][:, 0:n], in0=cu[s][:, 2:n + 2], scalar=w2,
                                                        in1=a1[s][:, 0:n], op0=ALU.mult, op1=ALU.add),
             reads=[("cu", s), ("a1", s)], writes=[("a1", s)])
        S.note("y", P.op("vector", lambda e: e.tensor_tensor(out=y[:, j, a:a + n], in0=S.ps[bb][:, 1:n + 1],
                                                             in1=a1[s][:, 0:n], op=ALU.mult),
                         reads=[("ps", bb), ("a1", s)], writes=[]))

    run_linear(S, Wv, KC, groups, rhs, [n + 2 for (_, n) in ct], epi, nslots=3)
    if own:
        S.close()


def stage_proj_res(nc, Wv, KCn, rhs_t, ntok, xin, xin0, xout, xout0, tn=512, edge=None, S=None, extra=(), tag="x"):
    own = S is None
    S = S or Stage(nc)
    P = S.P
    tl = conv_tiles(ntok, tn)
    groups = [[m * 128] for m in range(KC)]
    NX = 8
    xb = S.get("xb", lambda: [S.sb([128, 512], F32, f"xb{i}") for i in range(NX)])
    s_l = [P.sem(f"s_l{i}") for i in range(NX)]
    s_s = [P.sem(f"s_s{i}") for i in range(NX)]
    order = [(gi, ti) for gi in range(KC) for ti in range(len(tl))]
    base = S.get("xbn", lambda: [0])
    i_base = base[0]
    base[0] += len(order)
    idx = {k: i for i, k in enumerate(order)}
    loaded = [0]

    def load(i):
        gi, ti = order[i]
        a, n = tl[ti]
        s = (i_base + i) % NX
        P.dma("sync", xb[s][:, 0:n], xin[gi * 128:(gi + 1) * 128, xin0 + a:xin0 + a + n], s_l[s],
              reads=[("xd", tag, gi, ti)], writes=[("xb", s)])

    def pre(gi, ti):
        i = idx[(gi, ti)]
        while loaded[0] <= min(i + 5, len(order) - 1):
            load(loaded[0])
            loaded[0] += 1

    def rhs(kc, ti):
        a, n = tl[ti]
        return rhs_t[:, kc, a:a + n]

    def epi(gi, ti, banks, n):
        i = idx[(gi, ti)]
        a, n = tl[ti]
        s = (i_base + i) % NX
        b = banks[0]
        P.op("vector", lambda e: e.tensor_tensor(out=xb[s][:, 0:n], in0=S.ps[b][:, 0:n], in1=xb[s][:, 0:n],
                                                 op=ALU.add),
             reads=[("ps", b), ("xb", s)], writes=[("xb", s)])
        P.dma("scalar", xout[gi * 128:(gi + 1) * 128, xout0 + a:xout0 + a + n], xb[s][:, 0:n], s_s[s],
              reads=[("xb", s)], writes=[("xd", tag, gi, ti)])
        if edge is not None and ti == 0:
            P.dma("scalar", edge[gi * 128:(gi + 1) * 128, 0:EW], xb[s][:, 0:EW], s_s[s], reads=[("xb", s)])
        if edge is not None and ti == len(tl) - 1:
            P.dma("scalar", edge[gi * 128:(gi + 1) * 128, EW:2 * EW], xb[s][:, n - EW:n], s_s[s], reads=[("xb", s)])

    run_linear(S, Wv, KCn, groups, rhs, [n for (_, n) in tl], epi, pre=pre, nslots=3, extra=extra)
    if own:
        S.close()


def stage_ffn_up(nc, w_up, convp, h2, g, fg, S=None):
    own = S is None
    S = S or Stage(nc)
    P = S.P
    ct = conv_tiles(T)
    Wv = w_up.rearrange("(kc p) m -> p kc m", p=128)
    groups = [[(FG_OFF[fg] + j) * 128, F + (FG_OFF[fg] + j) * 128] for j in range(FG_SIZES[fg])]
    NS = 3
    tmp = S.get("ffn_tmp", lambda: {nm: [S.sb([128, 412], F32, f"{nm}{i}") for i in range(NS)]
                                    for nm in ["A1", "B1"]})
    A1, B1 = tmp["A1"], tmp["B1"]
    cnt = S.get("ffn_cnt", lambda: [0])

    def rhs(kc, ti):
        a, n = ct[ti]
        return h2[:, kc, a:a + n + 2]

    def epi(gi, ti, banks, n2):
        a, n = ct[ti]
        ba, bb = banks
        s = cnt[0] % NS
        cnt[0] += 1
        ja = FG_OFF[fg] + gi
        jb = FC + FG_OFF[fg] + gi

        def taps(bank, j, X1, nm):
            w0 = convp[:, j, 0:1]
            w1 = convp[:, j, 1:2]
            w2 = convp[:, j, 2:3]
            bia = convp[:, j, 3:4]
            P.op("scalar", lambda e: e.activation(out=X1[s][:, 0:n], in_=S.ps[bank][:, 1:n + 1], func=AF.Identity,
                                                  bias=bia, scale=w1),
                 reads=[("ps", bank)], writes=[(nm, s)])
            P.op("vector", lambda e: e.scalar_tensor_tensor(out=X1[s][:, 0:n], in0=S.ps[bank][:, 0:n], scalar=w0,
                                                            in1=X1[s][:, 0:n], op0=ALU.mult, op1=ALU.add),
                 reads=[("ps", bank), (nm, s)], writes=[(nm, s)])
            P.op("vector", lambda e: e.scalar_tensor_tensor(out=X1[s][:, 0:n], in0=S.ps[bank][:, 2:n + 2], scalar=w2,
                                                            in1=X1[s][:, 0:n], op0=ALU.mult, op1=ALU.add),
                 reads=[("ps", bank), (nm, s)], writes=[(nm, s)])

        taps(ba, ja, A1, "A")
        taps(bb, jb, B1, "B")
        P.op("scalar", lambda e: e.activation(out=A1[s][:, 0:n], in_=A1[s][:, 0:n], func=AF.Silu),
             reads=[("A", s)], writes=[("A", s)])
        S.note("g", P.op("vector", lambda e: e.tensor_tensor(out=g[:, gi, a:a + n], in0=A1[s][:, 0:n],
                                                             in1=B1[s][:, 0:n], op=ALU.mult),
                         reads=[("A", s), ("B", s)], writes=[]))

    run_linear(S, Wv, KC, groups, rhs, [n + 2 for (_, n) in ct], epi, nslots=3)
    if own:
        S.close()


def ffn_layer(nc, per, w_up, w_down, convp, gcol, xin, xin_tok0, xout):
    h2 = per.enter_context(nc.sbuf_tensor(uid("h2"), [128, KC, T + 2], BF16))
    stage_norm(nc, xin, xin_tok0, T + 2, gcol, h=h2, hoff=0)
    g = per.enter_context(nc.sbuf_tensor(uid("g"), [128, FGC, T], BF16))
    Wd = w_down.rearrange("(fc p) m -> p fc m", p=128)
    S = Stage(nc)
    for fg in range(FGROUPS):
        stage_ffn_up(nc, w_up, convp, h2, g, fg, S=S)
        f0, fn = FG_OFF[fg], FG_SIZES[fg]
        if fg == 0:
            stage_proj_res(nc, Wd[:, f0:f0 + fn, :], fn, g, T, xin, xin_tok0 + 1, xout, 0,
                           S=S, extra=S.prod("g"))
        else:
            stage_proj_res(nc, Wd[:, f0:f0 + fn, :], fn, g, T, xout, 0, xout, 0,
                           S=S, extra=S.prod("g"))
    S.close()


def build_A():
    nc = bass.Bass("TRN2", target_bir_lowering=False)
    xin = nc.dram_tensor("xin", [D, T + 4], F32, kind="ExternalInput").ap()
    gmix = nc.dram_tensor("gmix", [128, KC], F32, kind="ExternalInput").ap()
    gffn = nc.dram_tensor("gffn", [128, KC], F32, kind="ExternalInput").ap()
    w_in = nc.dram_tensor("w_in", [D, 3 * D], F32, kind="ExternalInput").ap()
    w_out = nc.dram_tensor("w_out", [D, D], F32, kind="ExternalInput").ap()
    scp = nc.dram_tensor("scp", [128, KC, 4], F32, kind="ExternalInput").ap()
    w_up = nc.dram_tensor("w_up", [D, 2 * F], F32, kind="ExternalInput").ap()
    w_down = nc.dram_tensor("w_down", [F, D], F32, kind="ExternalInput").ap()
    ffp = nc.dram_tensor("ffp", [128, 2 * FC, 4], F32, kind="ExternalInput").ap()
    xa = nc.dram_tensor("xa", [D, T + 2], F32).ap()
    x1 = nc.dram_tensor("x1", [D, T], F32, kind="ExternalOutput").ap()
    with ExitStack() as st:
        consts = st.enter_context(nc.sbuf_tensor("consts", [128, 2 * KC + KC * 4 + 2 * FC * 4], F32))
        gm = consts[:, 0:KC]
        gf = consts[:, KC:2 * KC]
        sc = consts[:, 2 * KC:2 * KC + KC * 4].rearrange("p (j c) -> p j c", c=4)
        fp = consts[:, 2 * KC + KC * 4:].rearrange("p (j c) -> p j c", c=4)
        S = Stage(nc)
        sm = S.P.sem("s_c")
        S.P.dma("sync", gm, gmix, sm)
        S.P.dma("sync", gf, gffn, sm)
        S.P.dma("sync", sc, scp, sm)
        S.P.dma("sync", fp, ffp, sm)
        S.close()
        with ExitStack() as st2:
            h = st2.enter_context(nc.sbuf_tensor("h", [128, KC, T + 4], BF16))
            stage_norm(nc, xin, 0, T + 4, gm, h=h, hoff=0)
            y = st2.enter_context(nc.sbuf_tensor("y", [128, KC, T + 2], BF16))
            SA = Stage(nc)
            stage_sconv_in(nc, w_in, sc, h, y, S=SA)
            stage_proj_res(nc, w_out.rearrange("(kc p) m -> p kc m", p=128), KC, y, T + 2, xin, 1, xa, 0, tn=410,
                           S=SA, extra=SA.prod("y"))
            SA.close()
        with ExitStack() as st2:
            ffn_layer(nc, st2, w_up, w_down, fp, gf, xa, 0, x1)
    release_sems(nc)
    return nc


GH = [64 * d for d in DILS]
TK = [T + 2 * h for h in GH]
NQB = [T // (128 * d) for d in DILS]
NCH = [(NQB[g] + 1) * DILS[g] for g in range(NG)]
CH_OFF = [0, NCH[0], NCH[0] + NCH[1]]
NCH_TOT = sum(NCH)
QKV_G = 3 * D
ATT_SCALE = 128 ** -0.5
NEG = -30000.0
POOL_EVERY = 2


def ss(start, count, step):
    return slice(start, start + step * (count - 1) + 1, step)


def stage_qk(nc, w_qkv, hh, g, which, lo, ntok, dst, dst0, S=None):
    own = S is None
    S = S or Stage(nc)
    P = S.P
    Wv = w_qkv.rearrange("(kc p) m -> p kc m", p=128)
    base = g * QKV_G + which * D
    groups = [[base + (2 * j) * 128, base + (2 * j + 1) * 128] for j in range(KC // 2)]
    tl = conv_tiles(ntok, 512)
    NST = 4
    stg = S.get("stg", lambda: [S.sb([128, 2048], BF16, f"stg{i}") for i in range(NST)])
    s_st = [P.sem(f"s_st{i}") for i in range(NST)]
    cnt = S.get("qk_cnt", lambda: [0])
    cbase = S.get("qk_chunk", lambda: [0])
    chunk0 = cbase[0]
    cbase[0] += KC

    def rhs(kc, ti):
        a, n = tl[ti]
        return hh[:, kc, lo + a:lo + a + n]

    def epi(gi, ti, banks, n):
        a, n = tl[ti]
        for ci, b in enumerate(banks):
            chunk = 2 * gi + ci
            sl = (chunk0 + chunk) % NST
            cnt[0] += 1
            if cnt[0] % 2 == 0:
                P.op("scalar", lambda e, sl=sl, b=b: e.activation(out=stg[sl][:, a:a + n], in_=S.ps[b][:, 0:n], func=AF.Copy),
                     reads=[("ps", b)], writes=[("stg", sl, ti)])
            else:
                P.op("vector", lambda e, sl=sl, b=b: e.tensor_copy(out=stg[sl][:, a:a + n], in_=S.ps[b][:, 0:n]),
                     reads=[("ps", b)], writes=[("stg", sl, ti)])
            if ti == len(tl) - 1:
                P.dma("sync", dst[chunk * 128:(chunk + 1) * 128, dst0:dst0 + ntok], stg[sl][:, 0:ntok], s_st[sl],
                      reads=[("stg", sl, t2) for t2 in range(len(tl))])

    run_linear(S, Wv, KC, groups, rhs, [n for (_, n) in tl], epi, nslots=3, wsize=8192)
    if own:
        S.close()


def stage_v(nc, w_qkv, hh, g, lo, ntok, dstV, row0, S=None):
    own = S is None
    S = S or Stage(nc)
    P = S.P
    Wv = w_qkv.rearrange("(kc p) m -> p kc m", p=128)
    base = g * QKV_G + 2 * D
    pool = S.wpool(3, 8192)
    nsl = len(pool["buf"])
    NST = 4
    vst = S.get("vst", lambda: [S.sb([128, 512], BF16, f"vst{i}") for i in range(NST)])
    s_vs = [P.sem(f"s_vs{i}") for i in range(NST)]
    blocks = conv_tiles(ntok, 128)
    cntl = S.get("v_cnt", lambda: [0])
    for sl in range(4):
        slot = pool["n"] % nsl
        pool["n"] += 1
        wvs = pool["buf"][slot][:, 0:KC * 512].rearrange("p (k c) -> p k c", c=512)
        c0 = base + sl * 512
        P.dma("gpsimd", wvs, Wv[:, :, c0:c0 + 512], pool["sem"][slot], writes=[("wb", slot)])
        for (a, m) in blocks:
            b = S.bank()
            P._deps("tensor", [("wb", slot)], [("ps", b)], [])
            for kc in range(KC):
                fn = (lambda e, b=b, wvs=wvs, kc=kc, a=a, m=m: e.matmul(
                    S.ps[b][0:m, :], hh[:, kc, lo + a:lo + a + m], wvs[:, kc, :],
                    start=(kc == 0), stop=(kc == KC - 1)))
                if kc == KC - 1:
                    P.op("tensor", fn, reads=[("wb", slot)], writes=[("ps", b)])
                else:
                    P.op("tensor", fn, signal=False)
            cntl[0] += 1
            cnt = cntl[0]
            st = cnt % NST
            if cnt % 2 == 0:
                P.op("scalar", lambda e, st=st, b=b, m=m: e.activation(out=vst[st][0:m, :], in_=S.ps[b][0:m, :], func=AF.Copy),
                     reads=[("ps", b)], writes=[("vst", st)])
            else:
                P.op("vector", lambda e, st=st, b=b, m=m: e.tensor_copy(out=vst[st][0:m, :], in_=S.ps[b][0:m, :]),
                     reads=[("ps", b)], writes=[("vst", st)])
            P.dma("sync", dstV[row0 + a:row0 + a + m, sl * 512:(sl + 1) * 512], vst[st][0:m, :], s_vs[st],
                  reads=[("vst", st)])
    if own:
        S.close()


def stage_attn(nc, Qs, Ks, Vs, etab, kbias_d, y):
    S = Stage(nc)
    P = S.P
    NSL = 2
    Qh = [S.sb([128, NG, T], BF16, f"Qh{i}") for i in range(NSL)]
    Kh = [[S.sb([128, TK[g]], BF16, f"Kh{i}_{g}") for g in range(NG)] for i in range(NSL)]
    Vh = [[S.sb([128, NQB[g] + 1, DILS[g], 128], BF16, f"Vh{i}_{g}") for g in range(NG)] for i in range(NSL)]
    Eh = [S.sb([128, NG, 256], F32, f"Eh{i}") for i in range(NSL)]
    kb = S.sb([128, NCH_TOT], F32, "kb")
    ones = S.sb([128, 128], BF16, "ones")
    num = S.sb([128, T], F32, "num")
    den = S.sb([128, T], F32, "den")
    NPT = 7
    pt = [S.sb([128, 256], F32, f"pt{i}") for i in range(NPT)]
    pb = [S.sb([128, 256], BF16, f"pb{i}") for i in range(NPT)]
    s_ld = [P.sem(f"s_ld{i}") for i in range(NSL)]
    s_kb = P.sem("s_kb")
    P.dma("sync", kb[:, :], kbias_d, s_kb, writes=["kb"])
    P.op("vector", lambda e: e.memset(ones[:, :], 1.0), writes=["ones"])
    Vviews = [Vs[g].rearrange("(i p r) c -> p i r c", p=128, r=DILS[g]) for g in range(NG)]

    pending = []

    def load_head(hd):
        s = hd % NSL
        rows = slice(hd * 128, (hd + 1) * 128)
        jobs = []
        for g in range(NG):
            jobs.append((Qh[s][:, g, :], Qs[g][rows, :]))
            jobs.append((Kh[s][g][:, :], Ks[g][rows, :]))
            for i in range(NQB[g] + 1):
                jobs.append((Vh[s][g][:, i, :, :], Vviews[g][:, i, :, rows]))
        jobs.append((Eh[s][:, :, :], etab[:, hd, :, :]))
        for j, (o, i_) in enumerate(jobs):
            pending.append((s, o, i_, j == 0, j == len(jobs) - 1))

    recent = []

    def issue_loads(n):
        for _ in range(n):
            if not pending:
                return
            s, o, i_, first, last = pending.pop(0)
            if len(recent) >= 6:
                P.wait("sync", recent.pop(0))
            tok = P.dma("sync", o, i_, s_ld[s], writes=([("L", s)] if first else []))
            recent.append(tok)
            if last:
                P.last_w[("L", s)] = tok

    units = []
    for hd in range(NH):
        for g in range(NG):
            d = DILS[g]
            for r in range(d):
                for i in range(NQB[g] + 1):
                    units.append((hd, g, r, i))
    nU = len(units)
    LA = 3
    sb_n = [0]
    acc_n = [0]
    acc_of = {}
    sinfo = {}
    state = {"evac_prev": [], "evac_cur": [], "norm": []}

    def s_phase(u):
        hd, g, r, i = units[u]
        s = hd % NSL
        d = DILS[g]
        lo = max(i - 1, 0)
        hi = min(i, NQB[g] - 1)
        N = 128 * (hi - lo + 1)
        bs = sb_n[0] % 4
        sb_n[0] += 1
        k = u % NPT
        kslice = ss(128 * d * i + r, 128, d)
        qslice = ss(128 * d * lo + r, N, d)
        c = CH_OFF[g] + i * d + r
        e0 = 128 if i == 0 else 0
        P.op("tensor", lambda e: e.matmul(S.ps[bs][:, 0:N], Kh[s][g][:, kslice], Qh[s][:, g, qslice],
                                          start=True, stop=True),
             reads=[("L", s)], writes=[("ps", bs)])
        P.op("scalar", lambda e: e.activation(out=pt[k][:, 0:N], in_=S.ps[bs][:, 0:N], func=AF.Exp,
                                              bias=kb[:, c:c + 1], scale=ATT_SCALE),
             reads=[("ps", bs), "kb"], writes=[("pt", k)])
        P.op("vector" if (u % POOL_EVERY) != 0 else "gpsimd",
             lambda e: e.tensor_tensor(out=pb[k][:, 0:N], in0=pt[k][:, 0:N],
                                       in1=Eh[s][:, g, e0:e0 + N], op=ALU.mult),
             reads=[("pt", k), ("L", s)], writes=[("pb", k)])
        sinfo[u] = (lo, hi, k)

    def pv_phase(u):
        hd, g, r, i = units[u]
        s = hd % NSL
        d = DILS[g]
        lo, hi, k = sinfo.pop(u)
        done_blocks = []
        nblk = hi - lo + 1
        for bi, blk in enumerate(range(lo, hi + 1)):
            first = (i == blk)
            last = (i == blk + 1)
            if first:
                acc_of[(hd, g, r, blk)] = acc_n[0] % 2
                acc_n[0] += 1
            par = acc_of[(hd, g, r, blk)]
            bo = 4 + par
            bd = 6 + par
            cols = slice(128 * bi, 128 * (bi + 1))
            if first:
                P._deps("tensor", [], [("ps", bo), ("ps", bd)], [])
            fo = (lambda e, bo=bo, cols=cols, first=first, last=last: e.matmul(
                S.ps[bo][:, 0:128], Vh[s][g][:, i, r, :], pb[k][:, cols], start=first, stop=last))
            fd = (lambda e, bd=bd, cols=cols, first=first, last=last: e.matmul(
                S.ps[bd][:, 0:128], ones[:, :], pb[k][:, cols], start=first, stop=last))
            final = (bi == nblk - 1)
            P.op("tensor", fo, reads=[("pb", k), ("L", s)], signal=False)
            P.op("tensor", fd, reads=[("pb", k), "ones", ("L", s)],
                 writes=([("ps", bo), ("ps", bd)] if last else []), signal=(last or final))
            if last:
                done_blocks.append((blk, bo, bd))
        for (blk, bo, bd) in done_blocks:
            del acc_of[(hd, g, r, blk)]
            tsl = ss(128 * d * blk + r, 128, d)
            if g == 0:
                extra = state["norm"]
                t1 = P.op("scalar", lambda e, bo=bo, tsl=tsl: e.activation(out=num[:, tsl], in_=S.ps[bo][:, 0:128], func=AF.Copy),
                          reads=[("ps", bo)], extra=extra)
                t2 = P.op("vector", lambda e, bd=bd, tsl=tsl: e.tensor_copy(out=den[:, tsl], in_=S.ps[bd][:, 0:128]),
                          reads=[("ps", bd)], extra=extra)
            else:
                extra = state["evac_prev"]
                t1 = P.op("vector", lambda e, bo=bo, tsl=tsl: e.tensor_tensor(out=num[:, tsl], in0=S.ps[bo][:, 0:128],
                                                                               in1=num[:, tsl], op=ALU.add),
                          reads=[("ps", bo)], extra=extra)
                t2 = P.op("vector", lambda e, bd=bd, tsl=tsl: e.tensor_tensor(out=den[:, tsl], in0=S.ps[bd][:, 0:128],
                                                                               in1=den[:, tsl], op=ALU.add),
                          reads=[("ps", bd)], extra=extra)
            state["evac_cur"] = [t1, t2]
        if r == d - 1 and i == NQB[g]:
            state["evac_prev"] = list(state["evac_cur"])
            if g == NG - 1:
                t3 = P.op("vector", lambda e: e.reciprocal(out=den[:, :], in_=den[:, :]), extra=state["evac_prev"])
                t4 = P.op("vector", lambda e, hd=hd: e.tensor_tensor(out=y[:, hd, :], in0=num[:, :], in1=den[:, :],
                                                                       op=ALU.mult), extra=[t3] + state["evac_prev"])
                state["norm"] = [t4]

    load_head(0)
    issue_loads(1000)
    load_head(1)
    for u in range(nU + LA):
        issue_loads(2)
        if u < nU:
            s_phase(u)
        v = u - LA
        if v >= 0:
            pv_phase(v)
            hdv = units[v][0]
            if (v == nU - 1 or units[v + 1][0] != hdv) and hdv + 2 < NH:
                load_head(hdv + 2)
    S.close()


def build_B():
    nc = bass.Bass("TRN2", target_bir_lowering=False)
    x1e = nc.dram_tensor("x1e", [D, T + 2 * HALO], F32, kind="ExternalInput").ap()
    gmix = nc.dram_tensor("gmix", [128, KC], F32, kind="ExternalInput").ap()
    w_qkv = nc.dram_tensor("w_qkv", [D, NG * QKV_G], F32, kind="ExternalInput").ap()
    w_o = nc.dram_tensor("w_o", [D, D], F32, kind="ExternalInput").ap()
    etab = nc.dram_tensor("etab", [128, NH, NG, 256], F32, kind="ExternalInput").ap()
    kbias = nc.dram_tensor("kbias", [128, NCH_TOT], F32, kind="ExternalInput").ap()
    x1p = nc.dram_tensor("x1p", [D, T], F32, kind="ExternalOutput").ap()
    Qs = [nc.dram_tensor(f"Qs{g}", [D, T], BF16).ap() for g in range(NG)]
    Ks = [nc.dram_tensor(f"Ks{g}", [D, TK[g]], BF16).ap() for g in range(NG)]
    Vs = [nc.dram_tensor(f"Vs{g}", [TK[g], D], BF16).ap() for g in range(NG)]
    with ExitStack() as st:
        consts = st.enter_context(nc.sbuf_tensor("consts", [128, KC], F32))
        gm = consts[:, 0:KC]
        S = Stage(nc)
        S.P.dma("sync", gm, gmix, S.P.sem("s_c"))
        S.close()
        with ExitStack() as st2:
            hh = st2.enter_context(nc.sbuf_tensor("hh", [128, KC, 2048], BF16))
            for hf in range(2):
                stage_norm(nc, x1e, 2048 * hf, 2048, gm, h=hh, hoff=0)
                qlo = HALO if hf == 0 else 0
                SQ = Stage(nc)
                SQ.wpool(3, 8192)
                for g in range(NG):
                    klo_ext = HALO - GH[g]
                    khi_ext = HALO + T + GH[g]
                    lo_ext = max(klo_ext, 2048 * hf)
                    hi_ext = min(khi_ext, 2048 * (hf + 1))
                    n = hi_ext - lo_ext
                    stage_qk(nc, w_qkv, hh, g, 0, qlo, T // 2, Qs[g], (T // 2) * hf, S=SQ)
                    stage_qk(nc, w_qkv, hh, g, 1, lo_ext - 2048 * hf, n, Ks[g], lo_ext - klo_ext, S=SQ)
                    stage_v(nc, w_qkv, hh, g, lo_ext - 2048 * hf, n, Vs[g], lo_ext - klo_ext, S=SQ)
                SQ.close()
        with ExitStack() as st2:
            y = st2.enter_context(nc.sbuf_tensor("yatt", [128, NH, T], BF16))
            stage_attn(nc, Qs, Ks, Vs, etab, kbias, y)
            stage_proj_res(nc, w_o.rearrange("(kc p) m -> p kc m", p=128), KC, y, T, x1e, HALO, x1p, 0)
    release_sems(nc)
    return nc


def build_C():
    nc = bass.Bass("TRN2", target_bir_lowering=False)
    xin = nc.dram_tensor("xin", [D, T + 2], F32, kind="ExternalInput").ap()
    gffn = nc.dram_tensor("gffn", [128, KC], F32, kind="ExternalInput").ap()
    gfin = nc.dram_tensor("gfin", [128, KC], F32, kind="ExternalInput").ap()
    w_up = nc.dram_tensor("w_up", [D, 2 * F], F32, kind="ExternalInput").ap()
    w_down = nc.dram_tensor("w_down", [F, D], F32, kind="ExternalInput").ap()
    ffp = nc.dram_tensor("ffp", [128, 2 * FC, 4], F32, kind="ExternalInput").ap()
    x2 = nc.dram_tensor("x2", [D, T], F32).ap()
    outT = nc.dram_tensor("outT", [D, T], F32, kind="ExternalOutput").ap()
    with ExitStack() as st:
        consts = st.enter_context(nc.sbuf_tensor("consts", [128, 2 * KC + 2 * FC * 4], F32))
        gf = consts[:, 0:KC]
        gl = consts[:, KC:2 * KC]
        fp = consts[:, 2 * KC:].rearrange("p (j c) -> p j c", c=4)
        S = Stage(nc)
        sm = S.P.sem("s_c")
        S.P.dma("sync", gf, gffn, sm)
        S.P.dma("sync", gl, gfin, sm)
        S.P.dma("sync", fp, ffp, sm)
        S.close()
        with ExitStack() as st2:
            ffn_layer(nc, st2, w_up, w_down, fp, gf, xin, 0, x2)
        stage_norm(nc, x2, 0, T, gl, outT=outT, out0=0)
    release_sems(nc)
    return nc


NCONST = 3 * KC + 2 * KC + KC * 4 + 2 * (2 * FC * 4) + 2
CC_GROUPS = [[0, 1, 2, 3], [4, 5, 6, 7]]
CCN = 4
PW = 128
NPIECE = T // PW
SERIAL_CC = False
EW = 16


def stage_allgather(nc, src, dst):
    S = Stage(nc)
    cs = get_sem(nc, "s_cc")
    cs.n += 1
    v = cs.n
    S.P.ops["gpsimd"].append(lambda g: g.collective_compute(
        "AllGather", ALU.bypass, replica_groups=CC_GROUPS, ins=[src], outs=[dst]).then_inc(cs.h))
    S.P.ops["gpsimd"].append(lambda g: g.wait_ge(cs.h, v))
    S.close()


def stage_exchange_pieces(nc, x1own, xp, G):
    S = Stage(nc)
    P = S.P
    s_rp = [P.sem(f"s_rp{i}") for i in range(4)]
    toks = []
    for q in range(NPIECE):
        toks.append(P.dma("sync", xp[q], x1own[:, q * PW:(q + 1) * PW], s_rp[q % 4]))
        if q >= 3:
            P.wait("sync", toks[q - 3])
    S.close()
    S = Stage(nc)
    cs = get_sem(nc, "s_cc")
    for q in range(NPIECE):
        cs.n += 1
        S.P.ops["gpsimd"].append(lambda g, q=q: g.collective_compute(
            "AllGather", ALU.bypass, replica_groups=CC_GROUPS, ins=[xp[q]], outs=[G[q]]).then_inc(cs.h))
        if SERIAL_CC or q == NPIECE - 1:
            v = cs.n
            S.P.ops["gpsimd"].append(lambda g, v=v: g.wait_ge(cs.h, v))
    S.close()


def stage_edges(nc, edge_all, xC, vmask):
    S = Stage(nc)
    P = S.P
    ec = S.sb([128, 2, KC, EW], F32, "ec")
    s_e = P.sem("s_e")
    s_e2 = P.sem("s_e2")

    def srcf(side):
        def f(e):
            pid = e.partition_id()
            r = (pid + 3) % 4 if side == 0 else (pid + 1) % 4
            c0 = EW if side == 0 else 0
            return edge_all[bass.ds(r * D, D), c0:c0 + EW].rearrange("(kc p) c -> p kc c", p=128)
        return f

    P.dma("sync", ec[:, 0, :, :], srcf(0), s_e, writes=[("ec", 0)])
    P.dma("sync", ec[:, 1, :, :], srcf(1), s_e2, writes=[("ec", 1)])
    P.op("vector", lambda e: e.tensor_scalar(out=ec[:, 0, :, :], in0=ec[:, 0, :, :], scalar1=vmask[:, 0:1], scalar2=None,
                                             op0=ALU.mult), reads=[("ec", 0)], writes=[("ec", 0)])
    P.op("vector", lambda e: e.tensor_scalar(out=ec[:, 1, :, :], in0=ec[:, 1, :, :], scalar1=vmask[:, 1:2], scalar2=None,
                                             op0=ALU.mult), reads=[("ec", 1)], writes=[("ec", 1)])
    xv = xC.rearrange("(kc p) t -> p kc t", p=128)
    P.dma("sync", xv[:, :, 0:EW], ec[:, 0, :, :], s_e, reads=[("ec", 0)])
    P.dma("sync", xv[:, :, EW + T:EW + T + EW], ec[:, 1, :, :], s_e2, reads=[("ec", 1)])
    S.close()


def build_fused():
    nc = bass.Bass("TRN2", target_bir_lowering=False)
    EI = "ExternalInput"
    xin = nc.dram_tensor("xin", [D, T + 4], F32, kind=EI).ap()
    cst = nc.dram_tensor("cst", [128, NCONST], F32, kind=EI).ap()
    w_in = nc.dram_tensor("w_in", [D, 3 * D], F32, kind=EI).ap()
    w_out = nc.dram_tensor("w_out", [D, D], F32, kind=EI).ap()
    w_up0 = nc.dram_tensor("w_up0", [D, 2 * F], F32, kind=EI).ap()
    w_down0 = nc.dram_tensor("w_down0", [F, D], F32, kind=EI).ap()
    w_qkv = nc.dram_tensor("w_qkv", [D, NG * QKV_G], F32, kind=EI).ap()
    w_o = nc.dram_tensor("w_o", [D, D], F32, kind=EI).ap()
    w_up1 = nc.dram_tensor("w_up1", [D, 2 * F], F32, kind=EI).ap()
    w_down1 = nc.dram_tensor("w_down1", [F, D], F32, kind=EI).ap()
    etab = nc.dram_tensor("etab", [128, NH, NG, 256], F32, kind=EI).ap()
    kbias = nc.dram_tensor("kbias", [128, NCH_TOT], F32, kind=EI).ap()
    outT = nc.dram_tensor("outT", [D, T], F32, kind="ExternalOutput").ap()
    xa = nc.dram_tensor("xa", [D, T + 2], F32).ap()
    x1own = nc.dram_tensor("x1own", [D, T], F32).ap()
    xp = [nc.dram_tensor(f"xp{q}", [D, PW], F32).ap() for q in range(NPIECE)]
    Gp = [nc.dram_tensor(f"Gp{q}", [CCN * D, PW], F32).ap() for q in range(NPIECE)]
    Qs = [nc.dram_tensor(f"Qs{g}", [D, T], BF16).ap() for g in range(NG)]
    Ks = [nc.dram_tensor(f"Ks{g}", [D, TK[g]], BF16).ap() for g in range(NG)]
    Vs = [nc.dram_tensor(f"Vs{g}", [TK[g], D], BF16).ap() for g in range(NG)]
    xC = nc.dram_tensor("xC", [D, T + 2 * EW], F32).ap()
    edge_in = nc.dram_tensor("edge_in", [D, 2 * EW], F32).ap()
    edge_all = nc.dram_tensor("edge_all", [CCN * D, 2 * EW], F32).ap()
    x2 = nc.dram_tensor("x2", [D, T], F32).ap()
    nc.cache_partition_id()
    with ExitStack() as st:
        consts = st.enter_context(nc.sbuf_tensor("consts", [128, NCONST], F32))
        o = 0
        gm0 = consts[:, o:o + KC]; o += KC
        gf0 = consts[:, o:o + KC]; o += KC
        gm1 = consts[:, o:o + KC]; o += KC
        gf1 = consts[:, o:o + KC]; o += KC
        gfin = consts[:, o:o + KC]; o += KC
        sc = consts[:, o:o + KC * 4].rearrange("p (j c) -> p j c", c=4); o += KC * 4
        fp0 = consts[:, o:o + 2 * FC * 4].rearrange("p (j c) -> p j c", c=4); o += 2 * FC * 4
        fp1 = consts[:, o:o + 2 * FC * 4].rearrange("p (j c) -> p j c", c=4); o += 2 * FC * 4
        vmask = consts[:, o:o + 2]; o += 2
        assert o == NCONST
        S = Stage(nc)
        S.P.dma("sync", consts[:, :], cst, S.P.sem("s_c"))
        S.close()
        with ExitStack() as st2:
            h = st2.enter_context(nc.sbuf_tensor("h", [128, KC, T + 4], BF16))
            stage_norm(nc, xin, 0, T + 4, gm0, h=h, hoff=0)
            y = st2.enter_context(nc.sbuf_tensor("y", [128, KC, T + 2], BF16))
            SA = Stage(nc)
            stage_sconv_in(nc, w_in, sc, h, y, S=SA)
            stage_proj_res(nc, w_out.rearrange("(kc p) m -> p kc m", p=128), KC, y, T + 2, xin, 1, xa, 0, tn=410,
                           S=SA, extra=SA.prod("y"))
            SA.close()
        with ExitStack() as st2:
            ffn_layer(nc, st2, w_up0, w_down0, fp0, gf0, xa, 0, x1own)
        stage_exchange_pieces(nc, x1own, xp, Gp)
        x1v = x1own.rearrange("(kc p) t -> p kc t", p=128)

        def xsrc_for(hf):
            def xsrc(a, n):
                e0 = 2048 * hf + a
                if HALO <= e0 < HALO + T:
                    return x1v[:, :, e0 - HALO:e0 - HALO + n]
                pieces = []
                for j in range(n // PW):
                    ee = e0 + j * PW
                    if ee < HALO:
                        q = (T - HALO + ee) // PW
                        sh = 3
                    else:
                        q = (ee - HALO - T) // PW
                        sh = 1

                    def f(e, q=q, sh=sh):
                        r = (e.partition_id() + sh) % 4
                        return Gp[q][bass.ds(r * D, D), :].rearrange("(kc p) t -> p kc t", p=128)
                    pieces.append((j * PW, PW, f))
                return pieces
            return xsrc

        with ExitStack() as st2:
            hh = st2.enter_context(nc.sbuf_tensor("hh", [128, KC, 2048], BF16))
            for hf in range(2):
                stage_norm(nc, None, 0, 2048, gm1, h=hh, hoff=0, xsrc=xsrc_for(hf))
                qlo = HALO if hf == 0 else 0
                SQ = Stage(nc)
                SQ.wpool(3, 8192)
                for g in range(NG):
                    klo_ext = HALO - GH[g]
                    khi_ext = HALO + T + GH[g]
                    lo_ext = max(klo_ext, 2048 * hf)
                    hi_ext = min(khi_ext, 2048 * (hf + 1))
                    n = hi_ext - lo_ext
                    stage_qk(nc, w_qkv, hh, g, 0, qlo, T // 2, Qs[g], (T // 2) * hf, S=SQ)
                    stage_qk(nc, w_qkv, hh, g, 1, lo_ext - 2048 * hf, n, Ks[g], lo_ext - klo_ext, S=SQ)
                    stage_v(nc, w_qkv, hh, g, lo_ext - 2048 * hf, n, Vs[g], lo_ext - klo_ext, S=SQ)
                SQ.close()
        with ExitStack() as st2:
            yat = st2.enter_context(nc.sbuf_tensor("yatt", [128, NH, T], BF16))
            stage_attn(nc, Qs, Ks, Vs, etab, kbias, yat)
            stage_proj_res(nc, w_o.rearrange("(kc p) m -> p kc m", p=128), KC, yat, T, x1own, 0, xC, EW, edge=edge_in)
        stage_allgather(nc, edge_in[:, :], edge_all[:, :])
        stage_edges(nc, edge_all, xC, vmask)
        with ExitStack() as st2:
            ffn_layer(nc, st2, w_up1, w_down1, fp1, gf1, xC, EW - 1, x2)
        stage_norm(nc, x2, 0, T, gfin, outT=outT, out0=0)
    release_sems(nc)
    return nc


def col_layout(v):
    return np.ascontiguousarray(v.reshape(-1, 128).T.astype(np.float32))


def conv_layout(w, b):
    C = w.shape[1] // 128
    out = np.empty((128, C, 4), np.float32)
    for k in range(3):
        out[:, :, k] = w[k].reshape(C, 128).T
    out[:, :, 3] = b.reshape(C, 128).T
    return out


def core_tokens(c):
    b = c // 4
    a = (c % 4) * T
    return b, a


def padded_slice_T(x, b, lo, hi):
    out = np.zeros((x.shape[2], hi - lo), np.float32)
    l2, h2 = max(lo, 0), min(hi, x.shape[1])
    out[:, l2 - lo:h2 - lo] = x[b, l2:h2, :].T
    return out


def alibi_tables():
    slopes = (2.0 ** (-8.0 * np.arange(1, NH + 1) / NH)).astype(np.float64)
    kp = np.arange(128)[:, None]
    qf = np.arange(128)[None, :]
    dB = qf - kp - 64
    dA = qf - kp + 64
    out = np.zeros((128, NH, NG, 256), np.float32)
    for h in range(NH):
        for g in range(NG):
            d = DILS[g]
            eb = np.where(np.abs(dB) <= 64, np.exp(-slopes[h] * d * np.abs(dB)), 0.0)
            ea = np.where(np.abs(dA) <= 64, np.exp(-slopes[h] * d * np.abs(dA)), 0.0)
            out[:, h, g, 0:128] = eb
            out[:, h, g, 128:256] = ea
    return out


def key_bias(a):
    out = np.zeros((128, NCH_TOT), np.float32)
    p = np.arange(128)
    for g in range(NG):
        d = DILS[g]
        for i in range(NQB[g] + 1):
            for r in range(d):
                k = 128 * d * i + d * p + r
                pos = a + k - GH[g]
                out[:, CH_OFF[g] + i * d + r] = np.where((pos >= 0) & (pos < SEQ), 0.0, NEG)
    return out


_NC_CACHE = {}


def get_nc(name, fn):
    if name not in _NC_CACHE:
        _NC_CACHE[name] = fn()
    return _NC_CACHE[name]


def run_A(inp):
    x = inp["x"]
    nc = get_nc("A", build_A)
    in_maps = []
    shared = {
        "gmix": col_layout(inp["mix_norm_g"][0]),
        "gffn": col_layout(inp["ffn_norm_g"][0]),
        "w_in": np.ascontiguousarray(inp["sc_w_in"][0]),
        "w_out": np.ascontiguousarray(inp["sc_w_out"][0]),
        "scp": conv_layout(inp["sc_conv_w"][0], inp["sc_conv_b"][0]),
        "w_up": np.ascontiguousarray(inp["ffn_w_up"][0]),
        "w_down": np.ascontiguousarray(inp["ffn_w_down"][0]),
        "ffp": conv_layout(inp["ffn_conv_w"][0], inp["ffn_conv_b"][0]),
    }
    for c in range(NCORES):
        b, a = core_tokens(c)
        m = dict(shared)
        m["xin"] = padded_slice_T(x, b, a - 2, a + T + 2)
        in_maps.append(m)
    res = run_bass_kernel_spmd(nc, in_maps, core_ids=list(range(NCORES)))
    x1 = np.empty_like(x)
    for c in range(NCORES):
        b, a = core_tokens(c)
        x1[b, a:a + T, :] = res.results[c]["x1"].T
    return x1


def run_B(inp, x1):
    nc = get_nc("B", build_B)
    shared = {
        "gmix": col_layout(inp["mix_norm_g"][1]),
        "w_qkv": np.ascontiguousarray(inp["attn_w_qkv"][0]),
        "w_o": np.ascontiguousarray(inp["attn_w_out"][0]),
        "etab": alibi_tables(),
    }
    in_maps = []
    for c in range(NCORES):
        b, a = core_tokens(c)
        m = dict(shared)
        m["x1e"] = padded_slice_T(x1, b, a - HALO, a + T + HALO)
        m["kbias"] = key_bias(a)
        in_maps.append(m)
    res = run_bass_kernel_spmd(nc, in_maps, core_ids=list(range(NCORES)))
    x1p = np.empty_like(x1)
    for c in range(NCORES):
        b, a = core_tokens(c)
        x1p[b, a:a + T, :] = res.results[c]["x1p"].T
    return x1p


def run_C(inp, x1p):
    nc = get_nc("C", build_C)
    shared = {
        "gffn": col_layout(inp["ffn_norm_g"][1]),
        "gfin": col_layout(inp["final_norm_g"]),
        "w_up": np.ascontiguousarray(inp["ffn_w_up"][1]),
        "w_down": np.ascontiguousarray(inp["ffn_w_down"][1]),
        "ffp": conv_layout(inp["ffn_conv_w"][1], inp["ffn_conv_b"][1]),
    }
    in_maps = []
    for c in range(NCORES):
        b, a = core_tokens(c)
        m = dict(shared)
        m["xin"] = padded_slice_T(x1p, b, a - 1, a + T + 1)
        in_maps.append(m)
    res = run_bass_kernel_spmd(nc, in_maps, core_ids=list(range(NCORES)))
    out = np.empty_like(x1p)
    for c in range(NCORES):
        b, a = core_tokens(c)
        out[b, a:a + T, :] = res.results[c]["outT"].T
    return out


def run_fused(inp):
    x = inp["x"]
    nc = get_nc("F", build_fused)
    cparts = [
        col_layout(inp["mix_norm_g"][0]), col_layout(inp["ffn_norm_g"][0]),
        col_layout(inp["mix_norm_g"][1]), col_layout(inp["ffn_norm_g"][1]),
        col_layout(inp["final_norm_g"]),
        conv_layout(inp["sc_conv_w"][0], inp["sc_conv_b"][0]).reshape(128, -1),
        conv_layout(inp["ffn_conv_w"][0], inp["ffn_conv_b"][0]).reshape(128, -1),
        conv_layout(inp["ffn_conv_w"][1], inp["ffn_conv_b"][1]).reshape(128, -1),
    ]
    shared = {
        "w_in": np.ascontiguousarray(inp["sc_w_in"][0]),
        "w_out": np.ascontiguousarray(inp["sc_w_out"][0]),
        "w_up0": np.ascontiguousarray(inp["ffn_w_up"][0]),
        "w_down0": np.ascontiguousarray(inp["ffn_w_down"][0]),
        "w_qkv": np.ascontiguousarray(inp["attn_w_qkv"][0]),
        "w_o": np.ascontiguousarray(inp["attn_w_out"][0]),
        "w_up1": np.ascontiguousarray(inp["ffn_w_up"][1]),
        "w_down1": np.ascontiguousarray(inp["ffn_w_down"][1]),
        "etab": alibi_tables(),
    }
    in_maps = []
    for c in range(NCORES):
        b, a = core_tokens(c)
        m = dict(shared)
        m["xin"] = padded_slice_T(x, b, a - 2, a + T + 2)
        m["kbias"] = key_bias(a)
        vm = np.zeros((128, 2), np.float32)
        vm[:, 0] = 1.0 if a > 0 else 0.0
        vm[:, 1] = 1.0 if a + T < SEQ else 0.0
        m["cst"] = np.ascontiguousarray(np.concatenate(cparts + [vm], axis=1))
        in_maps.append(m)
    res = run_bass_kernel_spmd(nc, in_maps, core_ids=list(range(NCORES)))
    out = np.empty_like(x)
    for c in range(NCORES):
        b, a = core_tokens(c)
        out[b, a:a + T, :] = res.results[c]["outT"].T
    return out


FUSED = True


def kernel(**inputs):
    inp = {k: np.asarray(v) for k, v in inputs.items()}
    if FUSED:
        return run_fused(inp).astype(np.float32)
    x1 = run_A(inp)
    x1p = run_B(inp, x1)
    return run_C(inp, x1p).astype(np.float32)
```

```python
import numpy as np
from contextlib import ExitStack
import concourse.bass as bass
import concourse.mybir as mybir
from concourse.bass_utils import run_bass_kernel_spmd

F32 = mybir.dt.float32
BF16 = mybir.dt.bfloat16
AF = mybir.ActivationFunctionType
ALU = mybir.AluOpType

D = 2048
KC = 16
T = 2048
NCORES = 8
SEQ = 8192
F = 5632
FC = 44
NH = 16
NG = 3
DILS = (1, 4, 16)
HALO = 1024
EPS = 1e-5
FG_SIZES = [15, 15, 14]
FG_OFF = [0, 15, 30]
FGROUPS = len(FG_SIZES)
FGC = max(FG_SIZES)

ENGS = ["sync", "scalar", "vector", "gpsimd", "tensor"]


class Sem:
    def __init__(self, nc, stack, name):
        self.h = stack.enter_context(nc.semaphore(uid(name)))
        self.n = 0

    def inc(self, k):
        self.n += k
        return (self, self.n)


_SEMS = {}


def get_sem(nc, name):
    key = (id(nc), name)
    if key not in _SEMS:
        stack = _SEMS.setdefault((id(nc), "__stack__"), ExitStack())
        _SEMS[key] = Sem(nc, stack, name)
    return _SEMS[key]


def release_sems(nc):
    st = _SEMS.pop((id(nc), "__stack__"), None)
    for k in [k for k in _SEMS if k[0] == id(nc)]:
        del _SEMS[k]
    if st is not None:
        st.close()


class Prog:
    def __init__(self, nc, stack):
        self.nc = nc
        self.stack = stack
        self.ops = {e: [] for e in ENGS}
        self.done = {e: get_sem(nc, "done_" + e) for e in ["scalar", "vector", "gpsimd", "tensor"]}
        self.waited = {}
        self.last_w = {}
        self.readers = {}
        self.dma_sems = []

    def sem(self, name):
        s = get_sem(self.nc, name)
        if s not in self.dma_sems:
            self.dma_sems.append(s)
        return s

    def wait(self, eng, tok):
        if tok is None:
            return
        s, v = tok
        key = (eng, id(s))
        if self.waited.get(key, 0) >= v:
            return
        self.waited[key] = v
        self.ops[eng].append(lambda e, s=s, v=v: e.wait_ge(s.h, v))

    def _deps(self, eng, reads, writes, extra):
        for w in extra:
            self.wait(eng, w)
        for k in reads:
            if k in self.last_w:
                self.wait(eng, self.last_w[k])
        for k in writes:
            if k in self.last_w:
                self.wait(eng, self.last_w[k])
            for tok in self.readers.get(k, {}).values():
                self.wait(eng, tok)

    def _track(self, tok, reads, writes):
        s, v = tok
        for k in reads:
            self.readers.setdefault(k, {})[id(s)] = tok
        for k in writes:
            self.last_w[k] = tok
            self.readers[k] = {}

    def op(self, eng, fn, reads=(), writes=(), extra=(), signal=True):
        self._deps(eng, reads, writes, extra)
        if not signal:
            self.ops[eng].append(lambda e, fn=fn: fn(e))
            return None
        s = self.done[eng]
        tok = s.inc(1)
        self.ops[eng].append(lambda e, fn=fn, s=s: fn(e).then_inc(s.h, 1))
        self._track(tok, reads, writes)
        return tok

    def dma(self, eng, out, in_, sem, reads=(), writes=(), extra=(), slow=False):
        self._deps(eng, reads, writes, extra)
        tok = sem.inc(16)
        kw = {"allow_slow_non_contiguous": True} if slow else {}
        self.ops[eng].append(
            lambda e, out=out, in_=in_, sem=sem, kw=kw: e.dma_start(
                out=(out(e) if callable(out) else out),
                in_=(in_(e) if callable(in_) else in_), **kw).then_inc(sem.h, 16))
        self._track(tok, reads, writes)
        return tok

    def finish(self):
        for s in self.dma_sems:
            if s.n > 0:
                self.wait("sync", (s, s.n))

    def emit(self):
        self.finish()
        nc = self.nc
        with nc.Block() as block:
            for name in ENGS:
                ops = self.ops[name]

                def body(e, ops=ops):
                    for f in ops:
                        f(e)
                getattr(block, name)(body)


_UID = [0]


def uid(name):
    _UID[0] += 1
    return f"{name}_{_UID[0]}"


class Stage:
    def __init__(self, nc):
        self.nc = nc
        self.st = ExitStack()
        self.P = Prog(nc, self.st)
        self.ps = [self.st.enter_context(nc.psum_tensor(uid(f"ps{i}"), [128, 512], F32)) for i in range(8)]
        self.psn = 0

    def sb(self, shape, dt, name=None):
        return self.st.enter_context(self.nc.sbuf_tensor(uid(name or "t"), shape, dt))

    def get(self, key, fn):
        if not hasattr(self, "cache"):
            self.cache = {}
        if key not in self.cache:
            self.cache[key] = fn()
        return self.cache[key]

    def wpool(self, nslots=3, wsize=6144):
        def mk():
            return {"buf": [self.sb([128, wsize], BF16, f"wraw{i}") for i in range(nslots)],
                    "sem": [self.P.sem(f"s_w{i}") for i in range(nslots)], "n": 0, "size": wsize}
        return self.get("wpool", mk)

    def note(self, name, tok):
        if tok is None:
            return
        d = self.get(("prod", name), dict)
        d[id(tok[0])] = tok

    def prod(self, name):
        return list(self.get(("prod", name), dict).values())

    def bank(self):
        b = self.psn % 8
        self.psn += 1
        return b

    def close(self):
        self.P.emit()
        self.st.close()


def stage_norm(nc, xT, tok0, ntok, gcol, h=None, hoff=0, outT=None, out0=0, xsrc=None, pre_wait=None):
    S = Stage(nc)
    P = S.P
    TN = 512
    nt = (ntok + TN - 1) // TN
    NSET = 2
    xs = [S.sb([128, KC, TN], F32, f"xs{i}") for i in range(NSET)]
    sq = [S.sb([128, KC, TN], BF16, f"sq{i}") for i in range(NSET)]
    rs = [S.sb([128, TN], F32, f"rs{i}") for i in range(NSET)]
    ones = S.sb([128, 128], BF16, "ones")
    s_x = [P.sem(f"s_x{i}") for i in range(NSET)]
    s_o = [P.sem(f"s_o{i}") for i in range(NSET)]
    ob = None
    if outT is not None:
        ob = [S.sb([128, KC, TN], F32, f"ob{i}") for i in range(NSET)]
    epsc = S.sb([128, 1], F32, "epsc")
    P.op("vector", lambda e: e.memset(ones[:, :], 1.0), writes=["ones"])
    P.op("vector", lambda e: e.memset(epsc[:, :], EPS), writes=["epsc"])
    xv = xT.rearrange("(kc p) t -> p kc t", p=128) if xT is not None else None
    ov = outT.rearrange("(kc p) t -> p kc t", p=128) if outT is not None else None
    if pre_wait is not None:
        P.wait("sync", pre_wait)
    for t in range(nt):
        s = t % NSET
        a = t * TN
        n = min(TN, ntok - a)
        src = xsrc(a, n) if xsrc is not None else xv[:, :, tok0 + a:tok0 + a + n]
        if isinstance(src, list):
            tokp = None
            for pi, (off, m, sp) in enumerate(src):
                tokp = P.dma("sync", xs[s][:, :, off:off + m], sp, s_x[s], writes=([("xs", s)] if pi == 0 else []))
            P.last_w[("xs", s)] = tokp
        else:
            P.dma("sync", xs[s][:, :, 0:n], src, s_x[s], writes=[("xs", s)])
        P.op("scalar", lambda e, s=s, n=n: e.activation(out=sq[s][:, :, 0:n], in_=xs[s][:, :, 0:n], func=AF.Square),
             reads=[("xs", s)], writes=[("sq", s)])
        b = S.bank()
        for kc in range(KC):
            last = kc == KC - 1
            fn = (lambda e, b=b, s=s, kc=kc, n=n: e.matmul(S.ps[b][:, 0:n], ones[:, :], sq[s][:, kc, 0:n],
                                                          start=(kc == 0), stop=(kc == KC - 1)))
            if kc == 0:
                P._deps("tensor", ["ones", ("sq", s)], [("ps", b)], [])
            if last:
                P.op("tensor", fn, reads=["ones", ("sq", s)], writes=[("ps", b)])
            else:
                P.op("tensor", fn, signal=False)
        P.op("scalar", lambda e, s=s, b=b, n=n: e.activation(
            out=rs[s][:, 0:n], in_=S.ps[b][:, 0:n], func=AF.Sqrt, bias=epsc[:, 0:1], scale=1.0 / D),
            reads=[("ps", b), "epsc"], writes=[("rs", s)])
        P.op("vector", lambda e, s=s, n=n: e.reciprocal(out=rs[s][:, 0:n], in_=rs[s][:, 0:n]),
             reads=[("rs", s)], writes=[("rs", s)])
        for kc in range(KC):
            eng = "vector"
            if outT is None:
                dst = h[:, kc, hoff + a:hoff + a + n]
                wr = []
            else:
                dst = ob[s][:, kc, 0:n]
                wr = [("ob", s, kc)]
            P.op(eng, lambda e, s=s, kc=kc, n=n, dst=dst: e.scalar_tensor_tensor(
                out=dst, in0=xs[s][:, kc, 0:n], scalar=gcol[:, kc:kc + 1], in1=rs[s][:, 0:n],
                op0=ALU.mult, op1=ALU.mult),
                reads=[("xs", s), ("rs", s)], writes=wr)
        if outT is not None:
            P.dma("sync", ov[:, :, out0 + a:out0 + a + n], ob[s][:, :, 0:n], s_o[s],
                  reads=[("ob", s, kc) for kc in range(KC)])
    S.close()


def run_linear(S, Wv, KCn, groups, rhs, tiles, epilogue, pre=None, nslots=3, extra=(), wsize=6144):
    P = S.P
    G = max(len(g) for g in groups)
    pool = S.wpool(nslots, wsize)
    assert KCn * G * 128 <= pool["size"]
    nsl = len(pool["buf"])
    first = True
    for gi, cols in enumerate(groups):
        slot = pool["n"] % nsl
        pool["n"] += 1
        wb = pool["buf"][slot][:, 0:KCn * G * 128].rearrange("p (k g c) -> p k g c", g=G, c=128)
        wtok = None
        for ci, c0 in enumerate(cols):
            wtok = P.dma("gpsimd", wb[:, :, ci, :], Wv[:, :, c0:c0 + 128], pool["sem"][slot],
                         writes=([("wb", slot)] if ci == 0 else []))
        P.last_w[("wb", slot)] = wtok
        for ti, n in enumerate(tiles):
            if pre is not None:
                pre(gi, ti)
            banks = []
            for ci in range(len(cols)):
                b = S.bank()
                banks.append(b)
                P._deps("tensor", [("wb", slot)], [("ps", b)], extra if first else [])
                first = False
                for kc in range(KCn):
                    fn = (lambda e, b=b, wb=wb, ci=ci, kc=kc, ti=ti, n=n: e.matmul(
                        S.ps[b][:, 0:n], wb[:, kc, ci, :], rhs(kc, ti),
                        start=(kc == 0), stop=(kc == KCn - 1)))
                    if kc == KCn - 1:
                        P.op("tensor", fn, reads=[("wb", slot)], writes=[("ps", b)])
                    else:
                        P.op("tensor", fn, signal=False)
            epilogue(gi, ti, banks, n)


def conv_tiles(nout, tn=410):
    tiles = []
    a = 0
    while a < nout:
        n = min(tn, nout - a)
        tiles.append((a, n))
        a += n
    return tiles


def stage_sconv_in(nc, w_in, convp, h, y, S=None):
    own = S is None
    S = S or Stage(nc)
    P = S.P
    ct = conv_tiles(T + 2)
    Wv = w_in.rearrange("(kc p) m -> p kc m", p=128)
    groups = [[j * 128, D + j * 128, 2 * D + j * 128] for j in range(KC)]
    NS = 3
    cs = [S.sb([128, 412], F32, f"cs{i}") for i in range(NS)]
    cu = [S.sb([128, 412], F32, f"cu{i}") for i in range(NS)]
    a1 = [S.sb([128, 412], F32, f"a1{i}") for i in range(NS)]
    cnt = [0]

    def rhs(kc, ti):
        a, n = ct[ti]
        return h[:, kc, a:a + n + 2]

    def epi(gi, ti, banks, n2):
        a, n = ct[ti]
        bu, bb, bc = banks
        s = cnt[0] % NS
        cnt[0] += 1
        j = gi
        w0 = convp[:, j, 0:1]
        w1 = convp[:, j, 1:2]
        w2 = convp[:, j, 2:3]
        bia = convp[:, j, 3:4]
        P.op("scalar", lambda e: e.activation(out=cs[s][:, 0:n + 2], in_=S.ps[bc][:, 0:n + 2], func=AF.Copy),
             reads=[("ps", bc)], writes=[("cs", s)])
        P.op("vector", lambda e: e.tensor_tensor(out=cu[s][:, 0:n + 2], in0=S.ps[bu][:, 0:n + 2],
                                                 in1=cs[s][:, 0:n + 2], op=ALU.mult),
             reads=[("ps", bu), ("cs", s)], writes=[("cu", s)])
        P.op("scalar", lambda e: e.activation(out=a1[s][:, 0:n], in_=cu[s][:, 1:n + 1], func=AF.Identity,
                                              bias=bia, scale=w1),
             reads=[("cu", s)], writes=[("a1", s)])
        P.op("vector", lambda e: e.scalar_tensor_tensor(out=a1[s][:, 0:n], in0=cu[s][:, 0:n], scalar=w0,
                                                        in1=a1[s][:, 0:n], op0=ALU.mult, op1=ALU.add),
             reads=[("cu", s), ("a1", s)], writes=[("a1", s)])
        P.op("vector", lambda e: e.scalar_tensor_tensor(out=a1[s][:, 0:n], in0=cu[s][:, 2:n + 2], scalar=w2,
                                                        in1=a1[s][:, 0:n], op0=ALU.mult, op1=ALU.add),
             reads=[("cu", s), ("a1", s)], writes=[("a1", s)])
        S.note("y", P.op("vector", lambda e: e.tensor_tensor(out=y[:, j, a:a + n], in0=S.ps[bb][:, 1:n + 1],
                                                             in1=a1[s][:, 0:n], op=ALU.mult),
                         reads=[("ps", bb), ("a1", s)], writes=[]))

    run_linear(S, Wv, KC, groups, rhs, [n + 2 for (_, n) in ct], epi, nslots=3)
    if own:
        S.close()


def stage_proj_res(nc, Wv, KCn, rhs_t, ntok, xin, xin0, xout, xout0, tn=512, edge=None, S=None, extra=(), tag="x"):
    own = S is None
    S = S or Stage(nc)
    P = S.P
    tl = conv_tiles(ntok, tn)
    groups = [[m * 128] for m in range(KC)]
    NX = 8
    xb = S.get("xb", lambda: [S.sb([128, 512], F32, f"xb{i}") for i in range(NX)])
    s_l = [P.sem(f"s_l{i}") for i in range(NX)]
    s_s = [P.sem(f"s_s{i}") for i in range(NX)]
    order = [(gi, ti) for gi in range(KC) for ti in range(len(tl))]
    base = S.get("xbn", lambda: [0])
    i_base = base[0]
    base[0] += len(order)
    idx = {k: i for i, k in enumerate(order)}
    loaded = [0]

    def load(i):
        gi, ti = order[i]
        a, n = tl[ti]
        s = (i_base + i) % NX
        P.dma("sync", xb[s][:, 0:n], xin[gi * 128:(gi + 1) * 128, xin0 + a:xin0 + a + n], s_l[s],
              reads=[("xd", tag, gi, ti)], writes=[("xb", s)])

    def pre(gi, ti):
        i = idx[(gi, ti)]
        while loaded[0] <= min(i + 5, len(order) - 1):
            load(loaded[0])
            loaded[0] += 1

    def rhs(kc, ti):
        a, n = tl[ti]
        return rhs_t[:, kc, a:a + n]

    def epi(gi, ti, banks, n):
        i = idx[(gi, ti)]
        a, n = tl[ti]
        s = (i_base + i) % NX
        b = banks[0]
        P.op("vector", lambda e: e.tensor_tensor(out=xb[s][:, 0:n], in0=S.ps[b][:, 0:n], in1=xb[s][:, 0:n],
                                                 op=ALU.add),
             reads=[("ps", b), ("xb", s)], writes=[("xb", s)])
        P.dma("scalar", xout[gi * 128:(gi + 1) * 128, xout0 + a:xout0 + a + n], xb[s][:, 0:n], s_s[s],
              reads=[("xb", s)], writes=[("xd", tag, gi, ti)])
        if edge is not None and ti == 0:
            P.dma("scalar", edge[gi * 128:(gi + 1) * 128, 0:EW], xb[s][:, 0:EW], s_s[s], reads=[("xb", s)])
        if edge is not None and ti == len(tl) - 1:
            P.dma("scalar", edge[gi * 128:(gi + 1) * 128, EW:2 * EW], xb[s][:, n - EW:n], s_s[s], reads=[("xb", s)])

    run_linear(S, Wv, KCn, groups, rhs, [n for (_, n) in tl], epi, pre=pre, nslots=3, extra=extra)
    if own:
        S.close()


def stage_ffn_up(nc, w_up, convp, h2, g, fg, S=None):
    own = S is None
    S = S or Stage(nc)
    P = S.P
    ct = conv_tiles(T)
    Wv = w_up.rearrange("(kc p) m -> p kc m", p=128)
    groups = [[(FG_OFF[fg] + j) * 128, F + (FG_OFF[fg] + j) * 128] for j in range(FG_SIZES[fg])]
    NS = 3
    tmp = S.get("ffn_tmp", lambda: {nm: [S.sb([128, 412], F32, f"{nm}{i}") for i in range(NS)]
                                    for nm in ["A1", "B1"]})
    A1, B1 = tmp["A1"], tmp["B1"]
    cnt = S.get("ffn_cnt", lambda: [0])

    def rhs(kc, ti):
        a, n = ct[ti]
        return h2[:, kc, a:a + n + 2]

    def epi(gi, ti, banks, n2):
        a, n = ct[ti]
        ba, bb = banks
        s = cnt[0] % NS
        cnt[0] += 1
        ja = FG_OFF[fg] + gi
        jb = FC + FG_OFF[fg] + gi

        def taps(bank, j, X1, nm):
            w0 = convp[:, j, 0:1]
            w1 = convp[:, j, 1:2]
            w2 = convp[:, j, 2:3]
            bia = convp[:, j, 3:4]
            P.op("scalar", lambda e: e.activation(out=X1[s][:, 0:n], in_=S.ps[bank][:, 1:n + 1], func=AF.Identity,
                                                  bias=bia, scale=w1),
                 reads=[("ps", bank)], writes=[(nm, s)])
            P.op("vector", lambda e: e.scalar_tensor_tensor(out=X1[s][:, 0:n], in0=S.ps[bank][:, 0:n], scalar=w0,
                                                            in1=X1[s][:, 0:n], op0=ALU.mult, op1=ALU.add),
                 reads=[("ps", bank), (nm, s)], writes=[(nm, s)])
            P.op("vector", lambda e: e.scalar_tensor_tensor(out=X1[s][:, 0:n], in0=S.ps[bank][:, 2:n + 2], scalar=w2,
                                                            in1=X1[s][:, 0:n], op0=ALU.mult, op1=ALU.add),
                 reads=[("ps", bank), (nm, s)], writes=[(nm, s)])

        taps(ba, ja, A1, "A")
        taps(bb, jb, B1, "B")
        P.op("scalar", lambda e: e.activation(out=A1[s][:, 0:n], in_=A1[s][:, 0:n], func=AF.Silu),
             reads=[("A", s)], writes=[("A", s)])
        S.note("g", P.op("vector", lambda e: e.tensor_tensor(out=g[:, gi, a:a + n], in0=A1[s][:, 0:n],
                                                             in1=B1[s][:, 0:n], op=ALU.mult),
                         reads=[("A", s), ("B", s)], writes=[]))

    run_linear(S, Wv, KC, groups, rhs, [n + 2 for (_, n) in ct], epi, nslots=3)
    if own:
        S.close()


def ffn_layer(nc, per, w_up, w_down, convp, gcol, xin, xin_tok0, xout):
    h2 = per.enter_context(nc.sbuf_tensor(uid("h2"), [128, KC, T + 2], BF16))
    stage_norm(nc, xin, xin_tok0, T + 2, gcol, h=h2, hoff=0)
    g = per.enter_context(nc.sbuf_tensor(uid("g"), [128, FGC, T], BF16))
    Wd = w_down.rearrange("(fc p) m -> p fc m", p=128)
    S = Stage(nc)
    for fg in range(FGROUPS):
        stage_ffn_up(nc, w_up, convp, h2, g, fg, S=S)
        f0, fn = FG_OFF[fg], FG_SIZES[fg]
        if fg == 0:
            stage_proj_res(nc, Wd[:, f0:f0 + fn, :], fn, g, T, xin, xin_tok0 + 1, xout, 0,
                           S=S, extra=S.prod("g"))
        else:
            stage_proj_res(nc, Wd[:, f0:f0 + fn, :], fn, g, T, xout, 0, xout, 0,
                           S=S, extra=S.prod("g"))
    S.close()


def build_A():
    nc = bass.Bass("TRN2", target_bir_lowering=False)
    xin = nc.dram_tensor("xin", [D, T + 4], F32, kind="ExternalInput").ap()
    gmix = nc.dram_tensor("gmix", [128, KC], F32, kind="ExternalInput").ap()
    gffn = nc.dram_tensor("gffn", [128, KC], F32, kind="ExternalInput").ap()
    w_in = nc.dram_tensor("w_in", [D, 3 * D], F32, kind="ExternalInput").ap()
    w_out = nc.dram_tensor("w_out", [D, D], F32, kind="ExternalInput").ap()
    scp = nc.dram_tensor("scp", [128, KC, 4], F32, kind="ExternalInput").ap()
    w_up = nc.dram_tensor("w_up", [D, 2 * F], F32, kind="ExternalInput").ap()
    w_down = nc.dram_tensor("w_down", [F, D], F32, kind="ExternalInput").ap()
    ffp = nc.dram_tensor("ffp", [128, 2 * FC, 4], F32, kind="ExternalInput").ap()
    xa = nc.dram_tensor("xa", [D, T + 2], F32).ap()
    x1 = nc.dram_tensor("x1", [D, T], F32, kind="ExternalOutput").ap()
    with ExitStack() as st:
        consts = st.enter_context(nc.sbuf_tensor("consts", [128, 2 * KC + KC * 4 + 2 * FC * 4], F32))
        gm = consts[:, 0:KC]
        gf = consts[:, KC:2 * KC]
        sc = consts[:, 2 * KC:2 * KC + KC * 4].rearrange("p (j c) -> p j c", c=4)
        fp = consts[:, 2 * KC + KC * 4:].rearrange("p (j c) -> p j c", c=4)
        S = Stage(nc)
        sm = S.P.sem("s_c")
        S.P.dma("sync", gm, gmix, sm)
        S.P.dma("sync", gf, gffn, sm)
        S.P.dma("sync", sc, scp, sm)
        S.P.dma("sync", fp, ffp, sm)
        S.close()
        with ExitStack() as st2:
            h = st2.enter_context(nc.sbuf_tensor("h", [128, KC, T + 4], BF16))
            stage_norm(nc, xin, 0, T + 4, gm, h=h, hoff=0)
            y = st2.enter_context(nc.sbuf_tensor("y", [128, KC, T + 2], BF16))
            SA = Stage(nc)
            stage_sconv_in(nc, w_in, sc, h, y, S=SA)
            stage_proj_res(nc, w_out.rearrange("(kc p) m -> p kc m", p=128), KC, y, T + 2, xin, 1, xa, 0, tn=410,
                           S=SA, extra=SA.prod("y"))
            SA.close()
        with ExitStack() as st2:
            ffn_layer(nc, st2, w_up, w_down, fp, gf, xa, 0, x1)
    release_sems(nc)
    return nc


GH = [64 * d for d in DILS]
TK = [T + 2 * h for h in GH]
NQB = [T // (128 * d) for d in DILS]
NCH = [(NQB[g] + 1) * DILS[g] for g in range(NG)]
CH_OFF = [0, NCH[0], NCH[0] + NCH[1]]
NCH_TOT = sum(NCH)
QKV_G = 3 * D
ATT_SCALE = 128 ** -0.5
NEG = -30000.0
POOL_EVERY = 2


def ss(start, count, step):
    return slice(start, start + step * (count - 1) + 1, step)


def stage_qk(nc, w_qkv, hh, g, which, lo, ntok, dst, dst0, S=None, gap=None):
    own = S is None
    S = S or Stage(nc)
    P = S.P
    Wv = w_qkv.rearrange("(kc p) m -> p kc m", p=128)
    base = g * QKV_G + which * D
    groups = [[base + (2 * j) * 128, base + (2 * j + 1) * 128] for j in range(KC // 2)]
    tl = conv_tiles(ntok, 512)
    NST = 4
    stg = S.get("stg", lambda: [S.sb([128, 2048], BF16, f"stg{i}") for i in range(NST)])
    s_st = [P.sem(f"s_st{i}") for i in range(NST)]
    cnt = S.get("qk_cnt", lambda: [0])
    cbase = S.get("qk_chunk", lambda: [0])
    chunk0 = cbase[0]
    cbase[0] += KC

    def rhs(kc, ti):
        a, n = tl[ti]
        return hh[:, kc, lo + a:lo + a + n]

    def epi(gi, ti, banks, n):
        a, n = tl[ti]
        for ci, b in enumerate(banks):
            chunk = 2 * gi + ci
            sl = (chunk0 + chunk) % NST
            cnt[0] += 1
            if cnt[0] % 2 == 0:
                P.op("scalar", lambda e, sl=sl, b=b: e.activation(out=stg[sl][:, a:a + n], in_=S.ps[b][:, 0:n], func=AF.Copy),
                     reads=[("ps", b)], writes=[("stg", sl, ti)])
            else:
                P.op("vector", lambda e, sl=sl, b=b: e.tensor_copy(out=stg[sl][:, a:a + n], in_=S.ps[b][:, 0:n]),
                     reads=[("ps", b)], writes=[("stg", sl, ti)])
            if ti == len(tl) - 1:
                rd = [("stg", sl, t2) for t2 in range(len(tl))]
                if gap is None:
                    P.dma("sync", dst[chunk * 128:(chunk + 1) * 128, dst0:dst0 + ntok], stg[sl][:, 0:ntok], s_st[sl],
                          reads=rd)
                else:
                    gp, gw = gap
                    P.dma("sync", dst[chunk * 128:(chunk + 1) * 128, dst0:dst0 + gp], stg[sl][:, 0:gp], s_st[sl],
                          reads=rd)
                    P.dma("sync", dst[chunk * 128:(chunk + 1) * 128, dst0 + gp + gw:dst0 + gw + ntok],
                          stg[sl][:, gp:ntok], s_st[sl], reads=rd)

    run_linear(S, Wv, KC, groups, rhs, [n for (_, n) in tl], epi, nslots=3, wsize=8192)
    if own:
        S.close()


def stage_v(nc, w_qkv, hh, g, lo, ntok, dstV, row0, S=None, gap=None):
    own = S is None
    S = S or Stage(nc)
    P = S.P
    Wv = w_qkv.rearrange("(kc p) m -> p kc m", p=128)
    base = g * QKV_G + 2 * D
    pool = S.wpool(3, 8192)
    nsl = len(pool["buf"])
    NST = 4
    vst = S.get("vst", lambda: [S.sb([128, 512], BF16, f"vst{i}") for i in range(NST)])
    s_vs = [P.sem(f"s_vs{i}") for i in range(NST)]
    blocks = conv_tiles(ntok, 128 if gap is None else min(128, gap[0]))
    cntl = S.get("v_cnt", lambda: [0])
    for sl in range(4):
        slot = pool["n"] % nsl
        pool["n"] += 1
        wvs = pool["buf"][slot][:, 0:KC * 512].rearrange("p (k c) -> p k c", c=512)
        c0 = base + sl * 512
        P.dma("gpsimd", wvs, Wv[:, :, c0:c0 + 512], pool["sem"][slot], writes=[("wb", slot)])
        for (a, m) in blocks:
            b = S.bank()
            P._deps("tensor", [("wb", slot)], [("ps", b)], [])
            for kc in range(KC):
                fn = (lambda e, b=b, wvs=wvs, kc=kc, a=a, m=m: e.matmul(
                    S.ps[b][0:m, :], hh[:, kc, lo + a:lo + a + m], wvs[:, kc, :],
                    start=(kc == 0), stop=(kc == KC - 1)))
                if kc == KC - 1:
                    P.op("tensor", fn, reads=[("wb", slot)], writes=[("ps", b)])
                else:
                    P.op("tensor", fn, signal=False)
            cntl[0] += 1
            cnt = cntl[0]
            st = cnt % NST
            if cnt % 2 == 0:
                P.op("scalar", lambda e, st=st, b=b, m=m: e.activation(out=vst[st][0:m, :], in_=S.ps[b][0:m, :], func=AF.Copy),
                     reads=[("ps", b)], writes=[("vst", st)])
            else:
                P.op("vector", lambda e, st=st, b=b, m=m: e.tensor_copy(out=vst[st][0:m, :], in_=S.ps[b][0:m, :]),
                     reads=[("ps", b)], writes=[("vst", st)])
            ra = row0 + a + (gap[1] if (gap is not None and a >= gap[0]) else 0)
            P.dma("sync", dstV[ra:ra + m, sl * 512:(sl + 1) * 512], vst[st][0:m, :], s_vs[st],
                  reads=[("vst", st)])
    if own:
        S.close()


def stage_attn(nc, Qs, Ks, Vs, etab, kbias_d, y):
    S = Stage(nc)
    P = S.P
    NSL = 2
    Qh = [S.sb([128, NG, T], BF16, f"Qh{i}") for i in range(NSL)]
    Kh = [[S.sb([128, TK[g]], BF16, f"Kh{i}_{g}") for g in range(NG)] for i in range(NSL)]
    Vh = [[S.sb([128, NQB[g] + 1, DILS[g], 128], BF16, f"Vh{i}_{g}") for g in range(NG)] for i in range(NSL)]
    Eh = [S.sb([128, NG, 256], F32, f"Eh{i}") for i in range(NSL)]
    kb = S.sb([128, NCH_TOT], F32, "kb")
    ones = S.sb([128, 128], BF16, "ones")
    num = S.sb([128, T], F32, "num")
    den = S.sb([128, T], F32, "den")
    NPT = 7
    pt = [S.sb([128, 256], F32, f"pt{i}") for i in range(NPT)]
    pb = [S.sb([128, 256], BF16, f"pb{i}") for i in range(NPT)]
    s_ld = [P.sem(f"s_ld{i}") for i in range(NSL)]
    s_kb = P.sem("s_kb")
    P.dma("sync", kb[:, :], kbias_d, s_kb, writes=["kb"])
    P.op("vector", lambda e: e.memset(ones[:, :], 1.0), writes=["ones"])
    Vviews = [Vs[g].rearrange("(i p r) c -> p i r c", p=128, r=DILS[g]) for g in range(NG)]

    pending = []

    def load_head(hd):
        s = hd % NSL
        rows = slice(hd * 128, (hd + 1) * 128)
        jobs = []
        for g in range(NG):
            jobs.append((Qh[s][:, g, :], Qs[g][rows, :]))
            jobs.append((Kh[s][g][:, :], Ks[g][rows, :]))
            for i in range(NQB[g] + 1):
                jobs.append((Vh[s][g][:, i, :, :], Vviews[g][:, i, :, rows]))
        jobs.append((Eh[s][:, :, :], etab[:, hd, :, :]))
        for j, (o, i_) in enumerate(jobs):
            pending.append((s, o, i_, j == 0, j == len(jobs) - 1))

    recent = []

    def issue_loads(n):
        for _ in range(n):
            if not pending:
                return
            s, o, i_, first, last = pending.pop(0)
            if len(recent) >= 6:
                P.wait("sync", recent.pop(0))
            tok = P.dma("sync", o, i_, s_ld[s], writes=([("L", s)] if first else []))
            recent.append(tok)
            if last:
                P.last_w[("L", s)] = tok

    units = []
    for hd in range(NH):
        for g in range(NG):
            d = DILS[g]
            for r in range(d):
                for i in range(NQB[g] + 1):
                    units.append((hd, g, r, i))
    nU = len(units)
    LA = 3
    sb_n = [0]
    acc_n = [0]
    acc_of = {}
    sinfo = {}
    state = {"evac_prev": [], "evac_cur": [], "norm": []}

    def s_phase(u):
        hd, g, r, i = units[u]
        s = hd % NSL
        d = DILS[g]
        lo = max(i - 1, 0)
        hi = min(i, NQB[g] - 1)
        N = 128 * (hi - lo + 1)
        bs = sb_n[0] % 4
        sb_n[0] += 1
        k = u % NPT
        kslice = ss(128 * d * i + r, 128, d)
        qslice = ss(128 * d * lo + r, N, d)
        c = CH_OFF[g] + i * d + r
        e0 = 128 if i == 0 else 0
        P.op("tensor", lambda e: e.matmul(S.ps[bs][:, 0:N], Kh[s][g][:, kslice], Qh[s][:, g, qslice],
                                          start=True, stop=True),
             reads=[("L", s)], writes=[("ps", bs)])
        P.op("scalar", lambda e: e.activation(out=pt[k][:, 0:N], in_=S.ps[bs][:, 0:N], func=AF.Exp,
                                              bias=kb[:, c:c + 1], scale=ATT_SCALE),
             reads=[("ps", bs), "kb"], writes=[("pt", k)])
        P.op("vector" if (u % POOL_EVERY) != 0 else "gpsimd",
             lambda e: e.tensor_tensor(out=pb[k][:, 0:N], in0=pt[k][:, 0:N],
                                       in1=Eh[s][:, g, e0:e0 + N], op=ALU.mult),
             reads=[("pt", k), ("L", s)], writes=[("pb", k)])
        sinfo[u] = (lo, hi, k)

    def pv_phase(u):
        hd, g, r, i = units[u]
        s = hd % NSL
        d = DILS[g]
        lo, hi, k = sinfo.pop(u)
        done_blocks = []
        nblk = hi - lo + 1
        for bi, blk in enumerate(range(lo, hi + 1)):
            first = (i == blk)
            last = (i == blk + 1)
            if first:
                acc_of[(hd, g, r, blk)] = acc_n[0] % 2
                acc_n[0] += 1
            par = acc_of[(hd, g, r, blk)]
            bo = 4 + par
            bd = 6 + par
            cols = slice(128 * bi, 128 * (bi + 1))
            if first:
                P._deps("tensor", [], [("ps", bo), ("ps", bd)], [])
            fo = (lambda e, bo=bo, cols=cols, first=first, last=last: e.matmul(
                S.ps[bo][:, 0:128], Vh[s][g][:, i, r, :], pb[k][:, cols], start=first, stop=last))
            fd = (lambda e, bd=bd, cols=cols, first=first, last=last: e.matmul(
                S.ps[bd][:, 0:128], ones[:, :], pb[k][:, cols], start=first, stop=last))
            final = (bi == nblk - 1)
            P.op("tensor", fo, reads=[("pb", k), ("L", s)], signal=False)
            P.op("tensor", fd, reads=[("pb", k), "ones", ("L", s)],
                 writes=([("ps", bo), ("ps", bd)] if last else []), signal=(last or final))
            if last:
                done_blocks.append((blk, bo, bd))
        for (blk, bo, bd) in done_blocks:
            del acc_of[(hd, g, r, blk)]
            tsl = ss(128 * d * blk + r, 128, d)
            if g == 0:
                extra = state["norm"]
                t1 = P.op("scalar", lambda e, bo=bo, tsl=tsl: e.activation(out=num[:, tsl], in_=S.ps[bo][:, 0:128], func=AF.Copy),
                          reads=[("ps", bo)], extra=extra)
                t2 = P.op("vector", lambda e, bd=bd, tsl=tsl: e.tensor_copy(out=den[:, tsl], in_=S.ps[bd][:, 0:128]),
                          reads=[("ps", bd)], extra=extra)
            else:
                extra = state["evac_prev"]
                t1 = P.op("vector", lambda e, bo=bo, tsl=tsl: e.tensor_tensor(out=num[:, tsl], in0=S.ps[bo][:, 0:128],
                                                                               in1=num[:, tsl], op=ALU.add),
                          reads=[("ps", bo)], extra=extra)
                t2 = P.op("vector", lambda e, bd=bd, tsl=tsl: e.tensor_tensor(out=den[:, tsl], in0=S.ps[bd][:, 0:128],
                                                                               in1=den[:, tsl], op=ALU.add),
                          reads=[("ps", bd)], extra=extra)
            state["evac_cur"] = [t1, t2]
        if r == d - 1 and i == NQB[g]:
            state["evac_prev"] = list(state["evac_cur"])
            if g == NG - 1:
                t3 = P.op("vector", lambda e: e.reciprocal(out=den[:, :], in_=den[:, :]), extra=state["evac_prev"])
                t4 = P.op("vector", lambda e, hd=hd: e.tensor_tensor(out=y[:, hd, :], in0=num[:, :], in1=den[:, :],
                                                                       op=ALU.mult), extra=[t3] + state["evac_prev"])
                state["norm"] = [t4]

    load_head(0)
    issue_loads(1000)
    load_head(1)
    for u in range(nU + LA):
        issue_loads(2)
        if u < nU:
            s_phase(u)
        v = u - LA
        if v >= 0:
            pv_phase(v)
            hdv = units[v][0]
            if (v == nU - 1 or units[v + 1][0] != hdv) and hdv + 2 < NH:
                load_head(hdv + 2)
    S.close()


def build_B():
    nc = bass.Bass("TRN2", target_bir_lowering=False)
    x1e = nc.dram_tensor("x1e", [D, T + 2 * HALO], F32, kind="ExternalInput").ap()
    gmix = nc.dram_tensor("gmix", [128, KC], F32, kind="ExternalInput").ap()
    w_qkv = nc.dram_tensor("w_qkv", [D, NG * QKV_G], F32, kind="ExternalInput").ap()
    w_o = nc.dram_tensor("w_o", [D, D], F32, kind="ExternalInput").ap()
    etab = nc.dram_tensor("etab", [128, NH, NG, 256], F32, kind="ExternalInput").ap()
    kbias = nc.dram_tensor("kbias", [128, NCH_TOT], F32, kind="ExternalInput").ap()
    x1p = nc.dram_tensor("x1p", [D, T], F32, kind="ExternalOutput").ap()
    Qs = [nc.dram_tensor(f"Qs{g}", [D, T], BF16).ap() for g in range(NG)]
    Ks = [nc.dram_tensor(f"Ks{g}", [D, TK[g]], BF16).ap() for g in range(NG)]
    Vs = [nc.dram_tensor(f"Vs{g}", [TK[g], D], BF16).ap() for g in range(NG)]
    with ExitStack() as st:
        consts = st.enter_context(nc.sbuf_tensor("consts", [128, KC], F32))
        gm = consts[:, 0:KC]
        S = Stage(nc)
        S.P.dma("sync", gm, gmix, S.P.sem("s_c"))
        S.close()
        with ExitStack() as st2:
            hh = st2.enter_context(nc.sbuf_tensor("hh", [128, KC, 2048], BF16))
            for hf in range(2):
                stage_norm(nc, x1e, 2048 * hf, 2048, gm, h=hh, hoff=0)
                qlo = HALO if hf == 0 else 0
                SQ = Stage(nc)
                SQ.wpool(3, 8192)
                for g in range(NG):
                    klo_ext = HALO - GH[g]
                    khi_ext = HALO + T + GH[g]
                    lo_ext = max(klo_ext, 2048 * hf)
                    hi_ext = min(khi_ext, 2048 * (hf + 1))
                    n = hi_ext - lo_ext
                    stage_qk(nc, w_qkv, hh, g, 0, qlo, T // 2, Qs[g], (T // 2) * hf, S=SQ)
                    stage_qk(nc, w_qkv, hh, g, 1, lo_ext - 2048 * hf, n, Ks[g], lo_ext - klo_ext, S=SQ)
                    stage_v(nc, w_qkv, hh, g, lo_ext - 2048 * hf, n, Vs[g], lo_ext - klo_ext, S=SQ)
                SQ.close()
        with ExitStack() as st2:
            y = st2.enter_context(nc.sbuf_tensor("yatt", [128, NH, T], BF16))
            stage_attn(nc, Qs, Ks, Vs, etab, kbias, y)
            stage_proj_res(nc, w_o.rearrange("(kc p) m -> p kc m", p=128), KC, y, T, x1e, HALO, x1p, 0)
    release_sems(nc)
    return nc


def build_C():
    nc = bass.Bass("TRN2", target_bir_lowering=False)
    xin = nc.dram_tensor("xin", [D, T + 2], F32, kind="ExternalInput").ap()
    gffn = nc.dram_tensor("gffn", [128, KC], F32, kind="ExternalInput").ap()
    gfin = nc.dram_tensor("gfin", [128, KC], F32, kind="ExternalInput").ap()
    w_up = nc.dram_tensor("w_up", [D, 2 * F], F32, kind="ExternalInput").ap()
    w_down = nc.dram_tensor("w_down", [F, D], F32, kind="ExternalInput").ap()
    ffp = nc.dram_tensor("ffp", [128, 2 * FC, 4], F32, kind="ExternalInput").ap()
    x2 = nc.dram_tensor("x2", [D, T], F32).ap()
    outT = nc.dram_tensor("outT", [D, T], F32, kind="ExternalOutput").ap()
    with ExitStack() as st:
        consts = st.enter_context(nc.sbuf_tensor("consts", [128, 2 * KC + 2 * FC * 4], F32))
        gf = consts[:, 0:KC]
        gl = consts[:, KC:2 * KC]
        fp = consts[:, 2 * KC:].rearrange("p (j c) -> p j c", c=4)
        S = Stage(nc)
        sm = S.P.sem("s_c")
        S.P.dma("sync", gf, gffn, sm)
        S.P.dma("sync", gl, gfin, sm)
        S.P.dma("sync", fp, ffp, sm)
        S.close()
        with ExitStack() as st2:
            ffn_layer(nc, st2, w_up, w_down, fp, gf, xin, 0, x2)
        stage_norm(nc, x2, 0, T, gl, outT=outT, out0=0)
    release_sems(nc)
    return nc


NCONST = 3 * KC + 2 * KC + KC * 4 + 2 * (2 * FC * 4) + 2
CC_GROUPS = [[0, 1, 2, 3], [4, 5, 6, 7]]
CCN = 4
PW = 128
NPIECE = T // PW
SERIAL_CC = False
EW = 16


def stage_allgather(nc, src, dst):
    S = Stage(nc)
    cs = get_sem(nc, "s_cc")
    cs.n += 1
    v = cs.n
    S.P.ops["gpsimd"].append(lambda g: g.collective_compute(
        "AllGather", ALU.bypass, replica_groups=CC_GROUPS, ins=[src], outs=[dst]).then_inc(cs.h))
    S.P.ops["gpsimd"].append(lambda g: g.wait_ge(cs.h, v))
    S.close()


def stage_exchange_pieces(nc, x1own, xp, G):
    S = Stage(nc)
    P = S.P
    s_rp = [P.sem(f"s_rp{i}") for i in range(4)]
    toks = []
    for q in range(NPIECE):
        toks.append(P.dma("sync", xp[q], x1own[:, q * PW:(q + 1) * PW], s_rp[q % 4]))
        if q >= 3:
            P.wait("sync", toks[q - 3])
    S.close()
    S = Stage(nc)
    cs = get_sem(nc, "s_cc")
    for q in range(NPIECE):
        cs.n += 1
        S.P.ops["gpsimd"].append(lambda g, q=q: g.collective_compute(
            "AllGather", ALU.bypass, replica_groups=CC_GROUPS, ins=[xp[q]], outs=[G[q]]).then_inc(cs.h))
        if SERIAL_CC:
            v = cs.n
            S.P.ops["gpsimd"].append(lambda g, v=v: g.wait_ge(cs.h, v))
    S.close()
    return (cs, cs.n)


def stage_edges(nc, edge_all, xC, vmask):
    S = Stage(nc)
    P = S.P
    ec = S.sb([128, 2, KC, EW], F32, "ec")
    s_e = P.sem("s_e")
    s_e2 = P.sem("s_e2")

    def srcf(side):
        def f(e):
            pid = e.partition_id()
            r = (pid + 3) % 4 if side == 0 else (pid + 1) % 4
            c0 = EW if side == 0 else 0
            return edge_all[bass.ds(r * D, D), c0:c0 + EW].rearrange("(kc p) c -> p kc c", p=128)
        return f

    P.dma("sync", ec[:, 0, :, :], srcf(0), s_e, writes=[("ec", 0)])
    P.dma("sync", ec[:, 1, :, :], srcf(1), s_e2, writes=[("ec", 1)])
    P.op("vector", lambda e: e.tensor_scalar(out=ec[:, 0, :, :], in0=ec[:, 0, :, :], scalar1=vmask[:, 0:1], scalar2=None,
                                             op0=ALU.mult), reads=[("ec", 0)], writes=[("ec", 0)])
    P.op("vector", lambda e: e.tensor_scalar(out=ec[:, 1, :, :], in0=ec[:, 1, :, :], scalar1=vmask[:, 1:2], scalar2=None,
                                             op0=ALU.mult), reads=[("ec", 1)], writes=[("ec", 1)])
    xv = xC.rearrange("(kc p) t -> p kc t", p=128)
    P.dma("sync", xv[:, :, 0:EW], ec[:, 0, :, :], s_e, reads=[("ec", 0)])
    P.dma("sync", xv[:, :, EW + T:EW + T + EW], ec[:, 1, :, :], s_e2, reads=[("ec", 1)])
    S.close()


def build_fused():
    nc = bass.Bass("TRN2", target_bir_lowering=False)
    EI = "ExternalInput"
    xin = nc.dram_tensor("xin", [D, T + 4], F32, kind=EI).ap()
    cst = nc.dram_tensor("cst", [128, NCONST], F32, kind=EI).ap()
    w_in = nc.dram_tensor("w_in", [D, 3 * D], F32, kind=EI).ap()
    w_out = nc.dram_tensor("w_out", [D, D], F32, kind=EI).ap()
    w_up0 = nc.dram_tensor("w_up0", [D, 2 * F], F32, kind=EI).ap()
    w_down0 = nc.dram_tensor("w_down0", [F, D], F32, kind=EI).ap()
    w_qkv = nc.dram_tensor("w_qkv", [D, NG * QKV_G], F32, kind=EI).ap()
    w_o = nc.dram_tensor("w_o", [D, D], F32, kind=EI).ap()
    w_up1 = nc.dram_tensor("w_up1", [D, 2 * F], F32, kind=EI).ap()
    w_down1 = nc.dram_tensor("w_down1", [F, D], F32, kind=EI).ap()
    etab = nc.dram_tensor("etab", [128, NH, NG, 256], F32, kind=EI).ap()
    kbias = nc.dram_tensor("kbias", [128, NCH_TOT], F32, kind=EI).ap()
    outT = nc.dram_tensor("outT", [D, T], F32, kind="ExternalOutput").ap()
    xa = nc.dram_tensor("xa", [D, T + 2], F32).ap()
    x1own = nc.dram_tensor("x1own", [D, T], F32).ap()
    xp = [nc.dram_tensor(f"xp{q}", [D, PW], F32).ap() for q in range(NPIECE)]
    Gp = [nc.dram_tensor(f"Gp{q}", [CCN * D, PW], F32).ap() for q in range(NPIECE)]
    Qs = [nc.dram_tensor(f"Qs{g}", [D, T], BF16).ap() for g in range(NG)]
    Ks = [nc.dram_tensor(f"Ks{g}", [D, TK[g]], BF16).ap() for g in range(NG)]
    Vs = [nc.dram_tensor(f"Vs{g}", [TK[g], D], BF16).ap() for g in range(NG)]
    xC = nc.dram_tensor("xC", [D, T + 2 * EW], F32).ap()
    edge_in = nc.dram_tensor("edge_in", [D, 2 * EW], F32).ap()
    edge_all = nc.dram_tensor("edge_all", [CCN * D, 2 * EW], F32).ap()
    x2 = nc.dram_tensor("x2", [D, T], F32).ap()
    nc.cache_partition_id()
    with ExitStack() as st:
        consts = st.enter_context(nc.sbuf_tensor("consts", [128, NCONST], F32))
        o = 0
        gm0 = consts[:, o:o + KC]; o += KC
        gf0 = consts[:, o:o + KC]; o += KC
        gm1 = consts[:, o:o + KC]; o += KC
        gf1 = consts[:, o:o + KC]; o += KC
        gfin = consts[:, o:o + KC]; o += KC
        sc = consts[:, o:o + KC * 4].rearrange("p (j c) -> p j c", c=4); o += KC * 4
        fp0 = consts[:, o:o + 2 * FC * 4].rearrange("p (j c) -> p j c", c=4); o += 2 * FC * 4
        fp1 = consts[:, o:o + 2 * FC * 4].rearrange("p (j c) -> p j c", c=4); o += 2 * FC * 4
        vmask = consts[:, o:o + 2]; o += 2
        assert o == NCONST
        S = Stage(nc)
        S.P.dma("sync", consts[:, :], cst, S.P.sem("s_c"))
        S.close()
        with ExitStack() as st2:
            h = st2.enter_context(nc.sbuf_tensor("h", [128, KC, T + 4], BF16))
            stage_norm(nc, xin, 0, T + 4, gm0, h=h, hoff=0)
            y = st2.enter_context(nc.sbuf_tensor("y", [128, KC, T + 2], BF16))
            SA = Stage(nc)
            stage_sconv_in(nc, w_in, sc, h, y, S=SA)
            stage_proj_res(nc, w_out.rearrange("(kc p) m -> p kc m", p=128), KC, y, T + 2, xin, 1, xa, 0, tn=410,
                           S=SA, extra=SA.prod("y"))
            SA.close()
        with ExitStack() as st2:
            ffn_layer(nc, st2, w_up0, w_down0, fp0, gf0, xa, 0, x1own)
        cc_tok = stage_exchange_pieces(nc, x1own, xp, Gp)

        def xsrc_halo(a, n):
            pieces = []
            for j in range(n // PW):
                c = a + j * PW
                if c < HALO:
                    q = (T - HALO + c) // PW
                    sh = 3
                else:
                    q = (c - HALO) // PW
                    sh = 1

                def f(e, q=q, sh=sh):
                    r = (e.partition_id() + sh) % 4
                    return Gp[q][bass.ds(r * D, D), :].rearrange("(kc p) t -> p kc t", p=128)
                pieces.append((j * PW, PW, f))
            return pieces

        with ExitStack() as st2:
            hh = st2.enter_context(nc.sbuf_tensor("hh", [128, KC, 2048], BF16))
            stage_norm(nc, x1own, 0, T, gm1, h=hh, hoff=0)
            SQ = Stage(nc)
            SQ.wpool(3, 8192)
            for g in range(NG):
                stage_qk(nc, w_qkv, hh, g, 0, 0, T, Qs[g], 0, S=SQ)
                stage_qk(nc, w_qkv, hh, g, 1, 0, T, Ks[g], GH[g], S=SQ)
                stage_v(nc, w_qkv, hh, g, 0, T, Vs[g], GH[g], S=SQ)
            SQ.close()
            stage_norm(nc, None, 0, 2048, gm1, h=hh, hoff=0, xsrc=xsrc_halo, pre_wait=cc_tok)
            SQ = Stage(nc)
            SQ.wpool(3, 8192)
            for g in range(NG):
                gh = GH[g]
                stage_qk(nc, w_qkv, hh, g, 1, HALO - gh, 2 * gh, Ks[g], 0, S=SQ, gap=(gh, T))
                stage_v(nc, w_qkv, hh, g, HALO - gh, 2 * gh, Vs[g], 0, S=SQ, gap=(gh, T))
            SQ.close()
        with ExitStack() as st2:
            yat = st2.enter_context(nc.sbuf_tensor("yatt", [128, NH, T], BF16))
            stage_attn(nc, Qs, Ks, Vs, etab, kbias, yat)
            stage_proj_res(nc, w_o.rearrange("(kc p) m -> p kc m", p=128), KC, yat, T, x1own, 0, xC, EW, edge=edge_in)
        stage_allgather(nc, edge_in[:, :], edge_all[:, :])
        stage_edges(nc, edge_all, xC, vmask)
        with ExitStack() as st2:
            ffn_layer(nc, st2, w_up1, w_down1, fp1, gf1, xC, EW - 1, x2)
        stage_norm(nc, x2, 0, T, gfin, outT=outT, out0=0)
    release_sems(nc)
    return nc


def col_layout(v):
    return np.ascontiguousarray(v.reshape(-1, 128).T.astype(np.float32))


def conv_layout(w, b):
    C = w.shape[1] // 128
    out = np.empty((128, C, 4), np.float32)
    for k in range(3):
        out[:, :, k] = w[k].reshape(C, 128).T
    out[:, :, 3] = b.reshape(C, 128).T
    return out


def core_tokens(c):
    b = c // 4
    a = (c % 4) * T
    return b, a


def padded_slice_T(x, b, lo, hi):
    out = np.zeros((x.shape[2], hi - lo), np.float32)
    l2, h2 = max(lo, 0), min(hi, x.shape[1])
    out[:, l2 - lo:h2 - lo] = x[b, l2:h2, :].T
    return out


def alibi_tables():
    slopes = (2.0 ** (-8.0 * np.arange(1, NH + 1) / NH)).astype(np.float64)
    kp = np.arange(128)[:, None]
    qf = np.arange(128)[None, :]
    dB = qf - kp - 64
    dA = qf - kp + 64
    out = np.zeros((128, NH, NG, 256), np.float32)
    for h in range(NH):
        for g in range(NG):
            d = DILS[g]
            eb = np.where(np.abs(dB) <= 64, np.exp(-slopes[h] * d * np.abs(dB)), 0.0)
            ea = np.where(np.abs(dA) <= 64, np.exp(-slopes[h] * d * np.abs(dA)), 0.0)
            out[:, h, g, 0:128] = eb
            out[:, h, g, 128:256] = ea
    return out


def key_bias(a):
    out = np.zeros((128, NCH_TOT), np.float32)
    p = np.arange(128)
    for g in range(NG):
        d = DILS[g]
        for i in range(NQB[g] + 1):
            for r in range(d):
                k = 128 * d * i + d * p + r
                pos = a + k - GH[g]
                out[:, CH_OFF[g] + i * d + r] = np.where((pos >= 0) & (pos < SEQ), 0.0, NEG)
    return out


_NC_CACHE = {}


def get_nc(name, fn):
    if name not in _NC_CACHE:
        _NC_CACHE[name] = fn()
    return _NC_CACHE[name]


def run_A(inp):
    x = inp["x"]
    nc = get_nc("A", build_A)
    in_maps = []
    shared = {
        "gmix": col_layout(inp["mix_norm_g"][0]),
        "gffn": col_layout(inp["ffn_norm_g"][0]),
        "w_in": np.ascontiguousarray(inp["sc_w_in"][0]),
        "w_out": np.ascontiguousarray(inp["sc_w_out"][0]),
        "scp": conv_layout(inp["sc_conv_w"][0], inp["sc_conv_b"][0]),
        "w_up": np.ascontiguousarray(inp["ffn_w_up"][0]),
        "w_down": np.ascontiguousarray(inp["ffn_w_down"][0]),
        "ffp": conv_layout(inp["ffn_conv_w"][0], inp["ffn_conv_b"][0]),
    }
    for c in range(NCORES):
        b, a = core_tokens(c)
        m = dict(shared)
        m["xin"] = padded_slice_T(x, b, a - 2, a + T + 2)
        in_maps.append(m)
    res = run_bass_kernel_spmd(nc, in_maps, core_ids=list(range(NCORES)))
    x1 = np.empty_like(x)
    for c in range(NCORES):
        b, a = core_tokens(c)
        x1[b, a:a + T, :] = res.results[c]["x1"].T
    return x1


def run_B(inp, x1):
    nc = get_nc("B", build_B)
    shared = {
        "gmix": col_layout(inp["mix_norm_g"][1]),
        "w_qkv": np.ascontiguousarray(inp["attn_w_qkv"][0]),
        "w_o": np.ascontiguousarray(inp["attn_w_out"][0]),
        "etab": alibi_tables(),
    }
    in_maps = []
    for c in range(NCORES):
        b, a = core_tokens(c)
        m = dict(shared)
        m["x1e"] = padded_slice_T(x1, b, a - HALO, a + T + HALO)
        m["kbias"] = key_bias(a)
        in_maps.append(m)
    res = run_bass_kernel_spmd(nc, in_maps, core_ids=list(range(NCORES)))
    x1p = np.empty_like(x1)
    for c in range(NCORES):
        b, a = core_tokens(c)
        x1p[b, a:a + T, :] = res.results[c]["x1p"].T
    return x1p


def run_C(inp, x1p):
    nc = get_nc("C", build_C)
    shared = {
        "gffn": col_layout(inp["ffn_norm_g"][1]),
        "gfin": col_layout(inp["final_norm_g"]),
        "w_up": np.ascontiguousarray(inp["ffn_w_up"][1]),
        "w_down": np.ascontiguousarray(inp["ffn_w_down"][1]),
        "ffp": conv_layout(inp["ffn_conv_w"][1], inp["ffn_conv_b"][1]),
    }
    in_maps = []
    for c in range(NCORES):
        b, a = core_tokens(c)
        m = dict(shared)
        m["xin"] = padded_slice_T(x1p, b, a - 1, a + T + 1)
        in_maps.append(m)
    res = run_bass_kernel_spmd(nc, in_maps, core_ids=list(range(NCORES)))
    out = np.empty_like(x1p)
    for c in range(NCORES):
        b, a = core_tokens(c)
        out[b, a:a + T, :] = res.results[c]["outT"].T
    return out


def run_fused(inp):
    x = inp["x"]
    nc = get_nc("F", build_fused)
    cparts = [
        col_layout(inp["mix_norm_g"][0]), col_layout(inp["ffn_norm_g"][0]),
        col_layout(inp["mix_norm_g"][1]), col_layout(inp["ffn_norm_g"][1]),
        col_layout(inp["final_norm_g"]),
        conv_layout(inp["sc_conv_w"][0], inp["sc_conv_b"][0]).reshape(128, -1),
        conv_layout(inp["ffn_conv_w"][0], inp["ffn_conv_b"][0]).reshape(128, -1),
        conv_layout(inp["ffn_conv_w"][1], inp["ffn_conv_b"][1]).reshape(128, -1),
    ]
    shared = {
        "w_in": np.ascontiguousarray(inp["sc_w_in"][0]),
        "w_out": np.ascontiguousarray(inp["sc_w_out"][0]),
        "w_up0": np.ascontiguousarray(inp["ffn_w_up"][0]),
        "w_down0": np.ascontiguousarray(inp["ffn_w_down"][0]),
        "w_qkv": np.ascontiguousarray(inp["attn_w_qkv"][0]),
        "w_o": np.ascontiguousarray(inp["attn_w_out"][0]),
        "w_up1": np.ascontiguousarray(inp["ffn_w_up"][1]),
        "w_down1": np.ascontiguousarray(inp["ffn_w_down"][1]),
        "etab": alibi_tables(),
    }
    in_maps = []
    for c in range(NCORES):
        b, a = core_tokens(c)
        m = dict(shared)
        m["xin"] = padded_slice_T(x, b, a - 2, a + T + 2)
        m["kbias"] = key_bias(a)
        vm = np.zeros((128, 2), np.float32)
        vm[:, 0] = 1.0 if a > 0 else 0.0
        vm[:, 1] = 1.0 if a + T < SEQ else 0.0
        m["cst"] = np.ascontiguousarray(np.concatenate(cparts + [vm], axis=1))
        in_maps.append(m)
    res = run_bass_kernel_spmd(nc, in_maps, core_ids=list(range(NCORES)))
    out = np.empty_like(x)
    for c in range(NCORES):
        b, a = core_tokens(c)
        out[b, a:a + T, :] = res.results[c]["outT"].T
    return out


FUSED = True


def kernel(**inputs):
    inp = {k: np.asarray(v) for k, v in inputs.items()}
    if FUSED:
        return run_fused(inp).astype(np.float32)
    x1 = run_A(inp)
    x1p = run_B(inp, x1)
    return run_C(inp, x1p).astype(np.float32)
```

```python
import numpy as np
from contextlib import ExitStack
import concourse.bass as bass
import concourse.mybir as mybir
from concourse.bass_utils import run_bass_kernel_spmd

F32 = mybir.dt.float32
BF16 = mybir.dt.bfloat16
AF = mybir.ActivationFunctionType
ALU = mybir.AluOpType

D = 2048
KC = 16
T = 2048
NCORES = 8
SEQ = 8192
F = 5632
FC = 44
NH = 16
NG = 3
DILS = (1, 4, 16)
HALO = 1024
EPS = 1e-5
FG_SIZES = [15, 15, 14]
FG_OFF = [0, 15, 30]
FGROUPS = len(FG_SIZES)
FGC = max(FG_SIZES)

ENGS = ["sync", "scalar", "vector", "gpsimd", "tensor"]


class Sem:
    def __init__(self, nc, stack, name):
        self.h = stack.enter_context(nc.semaphore(uid(name)))
        self.n = 0

    def inc(self, k):
        self.n += k
        return (self, self.n)


_SEMS = {}


def get_sem(nc, name):
    key = (id(nc), name)
    if key not in _SEMS:
        stack = _SEMS.setdefault((id(nc), "__stack__"), ExitStack())
        _SEMS[key] = Sem(nc, stack, name)
    return _SEMS[key]


def release_sems(nc):
    st = _SEMS.pop((id(nc), "__stack__"), None)
    for k in [k for k in _SEMS if k[0] == id(nc)]:
        del _SEMS[k]
    if st is not None:
        st.close()


class Prog:
    def __init__(self, nc, stack):
        self.nc = nc
        self.stack = stack
        self.ops = {e: [] for e in ENGS}
        self.done = {e: get_sem(nc, "done_" + e) for e in ["scalar", "vector", "gpsimd", "tensor"]}
        self.waited = {}
        self.last_w = {}
        self.readers = {}
        self.dma_sems = []

    def sem(self, name):
        s = get_sem(self.nc, name)
        if s not in self.dma_sems:
            self.dma_sems.append(s)
        return s

    def wait(self, eng, tok):
        if tok is None:
            return
        s, v = tok
        key = (eng, id(s))
        if self.waited.get(key, 0) >= v:
            return
        self.waited[key] = v
        self.ops[eng].append(lambda e, s=s, v=v: e.wait_ge(s.h, v))

    def _deps(self, eng, reads, writes, extra):
        for w in extra:
            self.wait(eng, w)
        for k in reads:
            if k in self.last_w:
                self.wait(eng, self.last_w[k])
        for k in writes:
            if k in self.last_w:
                self.wait(eng, self.last_w[k])
            for tok in self.readers.get(k, {}).values():
                self.wait(eng, tok)

    def _track(self, tok, reads, writes):
        s, v = tok
        for k in reads:
            self.readers.setdefault(k, {})[id(s)] = tok
        for k in writes:
            self.last_w[k] = tok
            self.readers[k] = {}

    def op(self, eng, fn, reads=(), writes=(), extra=(), signal=True):
        self._deps(eng, reads, writes, extra)
        if not signal:
            self.ops[eng].append(lambda e, fn=fn: fn(e))
            return None
        s = self.done[eng]
        tok = s.inc(1)
        self.ops[eng].append(lambda e, fn=fn, s=s: fn(e).then_inc(s.h, 1))
        self._track(tok, reads, writes)
        return tok

    def dma(self, eng, out, in_, sem, reads=(), writes=(), extra=(), slow=False):
        self._deps(eng, reads, writes, extra)
        tok = sem.inc(16)
        kw = {"allow_slow_non_contiguous": True} if slow else {}
        self.ops[eng].append(
            lambda e, out=out, in_=in_, sem=sem, kw=kw: e.dma_start(
                out=(out(e) if callable(out) else out),
                in_=(in_(e) if callable(in_) else in_), **kw).then_inc(sem.h, 16))
        self._track(tok, reads, writes)
        return tok

    def finish(self):
        for s in self.dma_sems:
            if s.n > 0:
                self.wait("sync", (s, s.n))

    def emit(self):
        self.finish()
        nc = self.nc
        with nc.Block() as block:
            for name in ENGS:
                ops = self.ops[name]

                def body(e, ops=ops):
                    for f in ops:
                        f(e)
                getattr(block, name)(body)


_UID = [0]


def uid(name):
    _UID[0] += 1
    return f"{name}_{_UID[0]}"


class Stage:
    def __init__(self, nc):
        self.nc = nc
        self.st = ExitStack()
        self.P = Prog(nc, self.st)
        self.ps = [self.st.enter_context(nc.psum_tensor(uid(f"ps{i}"), [128, 512], F32)) for i in range(8)]
        self.psn = 0

    def sb(self, shape, dt, name=None):
        return self.st.enter_context(self.nc.sbuf_tensor(uid(name or "t"), shape, dt))

    def get(self, key, fn):
        if not hasattr(self, "cache"):
            self.cache = {}
        if key not in self.cache:
            self.cache[key] = fn()
        return self.cache[key]

    def wpool(self, nslots=3, wsize=6144):
        def mk():
            return {"buf": [self.sb([128, wsize], BF16, f"wraw{i}") for i in range(nslots)],
                    "sem": [self.P.sem(f"s_w{i}") for i in range(nslots)], "n": 0, "size": wsize}
        return self.get("wpool", mk)

    def note(self, name, tok):
        if tok is None:
            return
        d = self.get(("prod", name), dict)
        d[id(tok[0])] = tok

    def prod(self, name):
        return list(self.get(("prod", name), dict).values())

    def bank(self):
        b = self.psn % 8
        self.psn += 1
        return b

    def close(self):
        self.P.emit()
        self.st.close()


def stage_norm(nc, xT, tok0, ntok, gcol, h=None, hoff=0, outT=None, out0=0, xsrc=None, pre_wait=None):
    S = Stage(nc)
    P = S.P
    TN = 512
    nt = (ntok + TN - 1) // TN
    NSET = 2
    xs = [S.sb([128, KC, TN], F32, f"xs{i}") for i in range(NSET)]
    sq = [S.sb([128, KC, TN], BF16, f"sq{i}") for i in range(NSET)]
    rs = [S.sb([128, TN], F32, f"rs{i}") for i in range(NSET)]
    ones = S.sb([128, 128], BF16, "ones")
    s_x = [P.sem(f"s_x{i}") for i in range(NSET)]
    s_o = [P.sem(f"s_o{i}") for i in range(NSET)]
    ob = None
    if outT is not None:
        ob = [S.sb([128, KC, TN], F32, f"ob{i}") for i in range(NSET)]
    epsc = S.sb([128, 1], F32, "epsc")
    P.op("vector", lambda e: e.memset(ones[:, :], 1.0), writes=["ones"])
    P.op("vector", lambda e: e.memset(epsc[:, :], EPS), writes=["epsc"])
    xv = xT.rearrange("(kc p) t -> p kc t", p=128) if xT is not None else None
    ov = outT.rearrange("(kc p) t -> p kc t", p=128) if outT is not None else None
    if pre_wait is not None:
        P.wait("sync", pre_wait)
    for t in range(nt):
        s = t % NSET
        a = t * TN
        n = min(TN, ntok - a)
        src = xsrc(a, n) if xsrc is not None else xv[:, :, tok0 + a:tok0 + a + n]
        if isinstance(src, list):
            tokp = None
            for pi, (off, m, sp) in enumerate(src):
                tokp = P.dma("sync", xs[s][:, :, off:off + m], sp, s_x[s], writes=([("xs", s)] if pi == 0 else []))
            P.last_w[("xs", s)] = tokp
        else:
            P.dma("sync", xs[s][:, :, 0:n], src, s_x[s], writes=[("xs", s)])
        P.op("scalar", lambda e, s=s, n=n: e.activation(out=sq[s][:, :, 0:n], in_=xs[s][:, :, 0:n], func=AF.Square),
             reads=[("xs", s)], writes=[("sq", s)])
        b = S.bank()
        for kc in range(KC):
            last = kc == KC - 1
            fn = (lambda e, b=b, s=s, kc=kc, n=n: e.matmul(S.ps[b][:, 0:n], ones[:, :], sq[s][:, kc, 0:n],
                                                          start=(kc == 0), stop=(kc == KC - 1)))
            if kc == 0:
                P._deps("tensor", ["ones", ("sq", s)], [("ps", b)], [])
            if last:
                P.op("tensor", fn, reads=["ones", ("sq", s)], writes=[("ps", b)])
            else:
                P.op("tensor", fn, signal=False)
        P.op("scalar", lambda e, s=s, b=b, n=n: e.activation(
            out=rs[s][:, 0:n], in_=S.ps[b][:, 0:n], func=AF.Sqrt, bias=epsc[:, 0:1], scale=1.0 / D),
            reads=[("ps", b), "epsc"], writes=[("rs", s)])
        P.op("vector", lambda e, s=s, n=n: e.reciprocal(out=rs[s][:, 0:n], in_=rs[s][:, 0:n]),
             reads=[("rs", s)], writes=[("rs", s)])
        for kc in range(KC):
            eng = "vector"
            if outT is None:
                dst = h[:, kc, hoff + a:hoff + a + n]
                wr = []
            else:
                dst = ob[s][:, kc, 0:n]
                wr = [("ob", s, kc)]
            P.op(eng, lambda e, s=s, kc=kc, n=n, dst=dst: e.scalar_tensor_tensor(
                out=dst, in0=xs[s][:, kc, 0:n], scalar=gcol[:, kc:kc + 1], in1=rs[s][:, 0:n],
                op0=ALU.mult, op1=ALU.mult),
                reads=[("xs", s), ("rs", s)], writes=wr)
        if outT is not None:
            P.dma("sync", ov[:, :, out0 + a:out0 + a + n], ob[s][:, :, 0:n], s_o[s],
                  reads=[("ob", s, kc) for kc in range(KC)])
    S.close()


def run_linear(S, Wv, KCn, groups, rhs, tiles, epilogue, pre=None, nslots=3, extra=(), wsize=6144):
    P = S.P
    G = max(len(g) for g in groups)
    pool = S.wpool(nslots, wsize)
    assert KCn * G * 128 <= pool["size"]
    nsl = len(pool["buf"])
    first = True
    for gi, cols in enumerate(groups):
        slot = pool["n"] % nsl
        pool["n"] += 1
        wb = pool["buf"][slot][:, 0:KCn * G * 128].rearrange("p (k g c) -> p k g c", g=G, c=128)
        wtok = None
        for ci, c0 in enumerate(cols):
            wtok = P.dma("gpsimd", wb[:, :, ci, :], Wv[:, :, c0:c0 + 128], pool["sem"][slot],
                         writes=([("wb", slot)] if ci == 0 else []))
        P.last_w[("wb", slot)] = wtok
        for ti, n in enumerate(tiles):
            if pre is not None:
                pre(gi, ti)
            banks = []
            for ci in range(len(cols)):
                b = S.bank()
                banks.append(b)
                P._deps("tensor", [("wb", slot)], [("ps", b)], extra if first else [])
                first = False
                for kc in range(KCn):
                    fn = (lambda e, b=b, wb=wb, ci=ci, kc=kc, ti=ti, n=n: e.matmul(
                        S.ps[b][:, 0:n], wb[:, kc, ci, :], rhs(kc, ti),
                        start=(kc == 0), stop=(kc == KCn - 1)))
                    if kc == KCn - 1:
                        P.op("tensor", fn, reads=[("wb", slot)], writes=[("ps", b)])
                    else:
                        P.op("tensor", fn, signal=False)
            epilogue(gi, ti, banks, n)


def conv_tiles(nout, tn=410):
    tiles = []
    a = 0
    while a < nout:
        n = min(tn, nout - a)
        tiles.append((a, n))
        a += n
    return tiles


def stage_sconv_in(nc, w_in, convp, h, y, S=None):
    own = S is None
    S = S or Stage(nc)
    P = S.P
    ct = conv_tiles(T + 2)
    Wv = w_in.rearrange("(kc p) m -> p kc m", p=128)
    groups = [[j * 128, D + j * 128, 2 * D + j * 128] for j in range(KC)]
    NS = 3
    cs = [S.sb([128, 412], F32, f"cs{i}") for i in range(NS)]
    cu = [S.sb([128, 412], F32, f"cu{i}") for i in range(NS)]
    a1 = [S.sb([128, 412], F32, f"a1{i}") for i in range(NS)]
    cnt = [0]

    def rhs(kc, ti):
        a, n = ct[ti]
        return h[:, kc, a:a + n + 2]

    def epi(gi, ti, banks, n2):
        a, n = ct[ti]
        bu, bb, bc = banks
        s = cnt[0] % NS
        cnt[0] += 1
        j = gi
        w0 = convp[:, j, 0:1]
        w1 = convp[:, j, 1:2]
        w2 = convp[:, j, 2:3]
        bia = convp[:, j, 3:4]
        P.op("scalar", lambda e: e.activation(out=cs[s][:, 0:n + 2], in_=S.ps[bc][:, 0:n + 2], func=AF.Copy),
             reads=[("ps", bc)], writes=[("cs", s)])
        P.op("vector", lambda e: e.tensor_tensor(out=cu[s][:, 0:n + 2], in0=S.ps[bu][:, 0:n + 2],
                                                 in1=cs[s][:, 0:n + 2], op=ALU.mult),
             reads=[("ps", bu), ("cs", s)], writes=[("cu", s)])
        P.op("scalar", lambda e: e.activation(out=a1[s][:, 0:n], in_=cu[s][:, 1:n + 1], func=AF.Identity,
                                              bias=bia, scale=w1),
             reads=[("cu", s)], writes=[("a1", s)])
        P.op("vector", lambda e: e.scalar_tensor_tensor(out=a1[s][:, 0:n], in0=cu[s][:, 0:n], scalar=w0,
                                                        in1=a1[s][:, 0:n], op0=ALU.mult, op1=ALU.add),
             reads=[("cu", s), ("a1", s)], writes=[("a1", s)])
        P.op("vector", lambda e: e.scalar_tensor_tensor(out=a1[s][:, 0:n], in0=cu[s][:, 2:n + 2], scalar=w2,
                                                        in1=a1[s][:, 0:n], op0=ALU.mult, op1=ALU.add),
             reads=[("cu", s), ("a1", s)], writes=[("a1", s)])
        S.note("y", P.op("vector", lambda e: e.tensor_tensor(out=y[:, j, a:a + n], in0=S.ps[bb][:, 1:n + 1],
                                                             in1=a1[s][:, 0:n], op=ALU.mult),
                         reads=[("ps", bb), ("a1", s)], writes=[]))

    run_linear(S, Wv, KC, groups, rhs, [n + 2 for (_, n) in ct], epi, nslots=3)
    if own:
        S.close()


def stage_proj_res(nc, Wv, KCn, rhs_t, ntok, xin, xin0, xout, xout0, tn=512, edge=None, S=None, extra=(), tag="x"):
    own = S is None
    S = S or Stage(nc)
    P = S.P
    tl = conv_tiles(ntok, tn)
    groups = [[m * 128] for m in range(KC)]
    NX = 8
    xb = S.get("xb", lambda: [S.sb([128, 512], F32, f"xb{i}") for i in range(NX)])
    s_l = [P.sem(f"s_l{i}") for i in range(NX)]
    s_s = [P.sem(f"s_s{i}") for i in range(NX)]
    order = [(gi, ti) for gi in range(KC) for ti in range(len(tl))]
    base = S.get("xbn", lambda: [0])
    i_base = base[0]
    base[0] += len(order)
    idx = {k: i for i, k in enumerate(order)}
    loaded = [0]

    def load(i):
        gi, ti = order[i]
        a, n = tl[ti]
        s = (i_base + i) % NX
        P.dma("sync", xb[s][:, 0:n], xin[gi * 128:(gi + 1) * 128, xin0 + a:xin0 + a + n], s_l[s],
              reads=[("xd", tag, gi, ti)], writes=[("xb", s)])

    def pre(gi, ti):
        i = idx[(gi, ti)]
        while loaded[0] <= min(i + 5, len(order) - 1):
            load(loaded[0])
            loaded[0] += 1

    def rhs(kc, ti):
        a, n = tl[ti]
        return rhs_t[:, kc, a:a + n]

    def epi(gi, ti, banks, n):
        i = idx[(gi, ti)]
        a, n = tl[ti]
        s = (i_base + i) % NX
        b = banks[0]
        P.op("vector", lambda e: e.tensor_tensor(out=xb[s][:, 0:n], in0=S.ps[b][:, 0:n], in1=xb[s][:, 0:n],
                                                 op=ALU.add),
             reads=[("ps", b), ("xb", s)], writes=[("xb", s)])
        P.dma("scalar", xout[gi * 128:(gi + 1) * 128, xout0 + a:xout0 + a + n], xb[s][:, 0:n], s_s[s],
              reads=[("xb", s)], writes=[("xd", tag, gi, ti)])
        if edge is not None and ti == 0:
            P.dma("scalar", edge[gi * 128:(gi + 1) * 128, 0:EW], xb[s][:, 0:EW], s_s[s], reads=[("xb", s)])
        if edge is not None and ti == len(tl) - 1:
            P.dma("scalar", edge[gi * 128:(gi + 1) * 128, EW:2 * EW], xb[s][:, n - EW:n], s_s[s], reads=[("xb", s)])

    run_linear(S, Wv, KCn, groups, rhs, [n for (_, n) in tl], epi, pre=pre, nslots=3, extra=extra)
    if own:
        S.close()


def stage_ffn_up(nc, w_up, convp, h2, g, fg, S=None):
    own = S is None
    S = S or Stage(nc)
    P = S.P
    ct = conv_tiles(T)
    Wv = w_up.rearrange("(kc p) m -> p kc m", p=128)
    groups = [[(FG_OFF[fg] + j) * 128, F + (FG_OFF[fg] + j) * 128] for j in range(FG_SIZES[fg])]
    NS = 3
    tmp = S.get("ffn_tmp", lambda: {nm: [S.sb([128, 412], F32, f"{nm}{i}") for i in range(NS)]
                                    for nm in ["A1", "B1"]})
    A1, B1 = tmp["A1"], tmp["B1"]
    cnt = S.get("ffn_cnt", lambda: [0])

    def rhs(kc, ti):
        a, n = ct[ti]
        return h2[:, kc, a:a + n + 2]

    def epi(gi, ti, banks, n2):
        a, n = ct[ti]
        ba, bb = banks
        s = cnt[0] % NS
        cnt[0] += 1
        ja = FG_OFF[fg] + gi
        jb = FC + FG_OFF[fg] + gi

        def taps(bank, j, X1, nm):
            w0 = convp[:, j, 0:1]
            w1 = convp[:, j, 1:2]
            w2 = convp[:, j, 2:3]
            bia = convp[:, j, 3:4]
            P.op("scalar", lambda e: e.activation(out=X1[s][:, 0:n], in_=S.ps[bank][:, 1:n + 1], func=AF.Identity,
                                                  bias=bia, scale=w1),
                 reads=[("ps", bank)], writes=[(nm, s)])
            P.op("vector", lambda e: e.scalar_tensor_tensor(out=X1[s][:, 0:n], in0=S.ps[bank][:, 0:n], scalar=w0,
                                                            in1=X1[s][:, 0:n], op0=ALU.mult, op1=ALU.add),
                 reads=[("ps", bank), (nm, s)], writes=[(nm, s)])
            P.op("vector", lambda e: e.scalar_tensor_tensor(out=X1[s][:, 0:n], in0=S.ps[bank][:, 2:n + 2], scalar=w2,
                                                            in1=X1[s][:, 0:n], op0=ALU.mult, op1=ALU.add),
                 reads=[("ps", bank), (nm, s)], writes=[(nm, s)])

        taps(ba, ja, A1, "A")
        taps(bb, jb, B1, "B")
        P.op("scalar", lambda e: e.activation(out=A1[s][:, 0:n], in_=A1[s][:, 0:n], func=AF.Silu),
             reads=[("A", s)], writes=[("A", s)])
        S.note("g", P.op("vector", lambda e: e.tensor_tensor(out=g[:, gi, a:a + n], in0=A1[s][:, 0:n],
                                                             in1=B1[s][:, 0:n], op=ALU.mult),
                         reads=[("A", s), ("B", s)], writes=[]))

    run_linear(S, Wv, KC, groups, rhs, [n + 2 for (_, n) in ct], epi, nslots=3)
    if own:
        S.close()


def ffn_layer(nc, per, w_up, w_down, convp, gcol, xin, xin_tok0, xout):
    h2 = per.enter_context(nc.sbuf_tensor(uid("h2"), [128, KC, T + 2], BF16))
    stage_norm(nc, xin, xin_tok0, T + 2, gcol, h=h2, hoff=0)
    g = per.enter_context(nc.sbuf_tensor(uid("g"), [128, FGC, T], BF16))
    Wd = w_down.rearrange("(fc p) m -> p fc m", p=128)
    S = Stage(nc)
    for fg in range(FGROUPS):
        stage_ffn_up(nc, w_up, convp, h2, g, fg, S=S)
        f0, fn = FG_OFF[fg], FG_SIZES[fg]
        if fg == 0:
            stage_proj_res(nc, Wd[:, f0:f0 + fn, :], fn, g, T, xin, xin_tok0 + 1, xout, 0,
                           S=S, extra=S.prod("g"))
        else:
            stage_proj_res(nc, Wd[:, f0:f0 + fn, :], fn, g, T, xout, 0, xout, 0,
                           S=S, extra=S.prod("g"))
    S.close()


def build_A():
    nc = bass.Bass("TRN2", target_bir_lowering=False)
    xin = nc.dram_tensor("xin", [D, T + 4], F32, kind="ExternalInput").ap()
    gmix = nc.dram_tensor("gmix", [128, KC], F32, kind="ExternalInput").ap()
    gffn = nc.dram_tensor("gffn", [128, KC], F32, kind="ExternalInput").ap()
    w_in = nc.dram_tensor("w_in", [D, 3 * D], F32, kind="ExternalInput").ap()
    w_out = nc.dram_tensor("w_out", [D, D], F32, kind="ExternalInput").ap()
    scp = nc.dram_tensor("scp", [128, KC, 4], F32, kind="ExternalInput").ap()
    w_up = nc.dram_tensor("w_up", [D, 2 * F], F32, kind="ExternalInput").ap()
    w_down = nc.dram_tensor("w_down", [F, D], F32, kind="ExternalInput").ap()
    ffp = nc.dram_tensor("ffp", [128, 2 * FC, 4], F32, kind="ExternalInput").ap()
    xa = nc.dram_tensor("xa", [D, T + 2], F32).ap()
    x1 = nc.dram_tensor("x1", [D, T], F32, kind="ExternalOutput").ap()
    with ExitStack() as st:
        consts = st.enter_context(nc.sbuf_tensor("consts", [128, 2 * KC + KC * 4 + 2 * FC * 4], F32))
        gm = consts[:, 0:KC]
        gf = consts[:, KC:2 * KC]
        sc = consts[:, 2 * KC:2 * KC + KC * 4].rearrange("p (j c) -> p j c", c=4)
        fp = consts[:, 2 * KC + KC * 4:].rearrange("p (j c) -> p j c", c=4)
        S = Stage(nc)
        sm = S.P.sem("s_c")
        S.P.dma("sync", gm, gmix, sm)
        S.P.dma("sync", gf, gffn, sm)
        S.P.dma("sync", sc, scp, sm)
        S.P.dma("sync", fp, ffp, sm)
        S.close()
        with ExitStack() as st2:
            h = st2.enter_context(nc.sbuf_tensor("h", [128, KC, T + 4], BF16))
            stage_norm(nc, xin, 0, T + 4, gm, h=h, hoff=0)
            y = st2.enter_context(nc.sbuf_tensor("y", [128, KC, T + 2], BF16))
            SA = Stage(nc)
            stage_sconv_in(nc, w_in, sc, h, y, S=SA)
            stage_proj_res(nc, w_out.rearrange("(kc p) m -> p kc m", p=128), KC, y, T + 2, xin, 1, xa, 0, tn=410,
                           S=SA, extra=SA.prod("y"))
            SA.close()
        with ExitStack() as st2:
            ffn_layer(nc, st2, w_up, w_down, fp, gf, xa, 0, x1)
    release_sems(nc)
    return nc


GH = [64 * d for d in DILS]
TK = [T + 2 * h for h in GH]
NQB = [T // (128 * d) for d in DILS]
NCH = [(NQB[g] + 1) * DILS[g] for g in range(NG)]
CH_OFF = [0, NCH[0], NCH[0] + NCH[1]]
NCH_TOT = sum(NCH)
QKV_G = 3 * D
ATT_SCALE = 128 ** -0.5
NEG = -30000.0
POOL_EVERY = 2


def ss(start, count, step):
    return slice(start, start + step * (count - 1) + 1, step)


def stage_qk(nc, w_qkv, hh, g, which, lo, ntok, dst, dst0, S=None, gap=None):
    own = S is None
    S = S or Stage(nc)
    P = S.P
    Wv = w_qkv.rearrange("(kc p) m -> p kc m", p=128)
    base = g * QKV_G + which * D
    groups = [[base + (2 * j) * 128, base + (2 * j + 1) * 128] for j in range(KC // 2)]
    tl = conv_tiles(ntok, 512)
    NST = 4
    stg = S.get("stg", lambda: [S.sb([128, 2048], BF16, f"stg{i}") for i in range(NST)])
    s_st = [P.sem(f"s_st{i}") for i in range(NST)]
    cnt = S.get("qk_cnt", lambda: [0])
    cbase = S.get("qk_chunk", lambda: [0])
    chunk0 = cbase[0]
    cbase[0] += KC

    def rhs(kc, ti):
        a, n = tl[ti]
        return hh[:, kc, lo + a:lo + a + n]

    def epi(gi, ti, banks, n):
        a, n = tl[ti]
        for ci, b in enumerate(banks):
            chunk = 2 * gi + ci
            sl = (chunk0 + chunk) % NST
            cnt[0] += 1
            if cnt[0] % 2 == 0:
                P.op("scalar", lambda e, sl=sl, b=b: e.activation(out=stg[sl][:, a:a + n], in_=S.ps[b][:, 0:n], func=AF.Copy),
                     reads=[("ps", b)], writes=[("stg", sl, ti)])
            else:
                P.op("vector", lambda e, sl=sl, b=b: e.tensor_copy(out=stg[sl][:, a:a + n], in_=S.ps[b][:, 0:n]),
                     reads=[("ps", b)], writes=[("stg", sl, ti)])
            if ti == len(tl) - 1:
                rd = [("stg", sl, t2) for t2 in range(len(tl))]
                if gap is None:
                    P.dma("sync", dst[chunk * 128:(chunk + 1) * 128, dst0:dst0 + ntok], stg[sl][:, 0:ntok], s_st[sl],
                          reads=rd)
                else:
                    gp, gw = gap
                    P.dma("sync", dst[chunk * 128:(chunk + 1) * 128, dst0:dst0 + gp], stg[sl][:, 0:gp], s_st[sl],
                          reads=rd)
                    P.dma("sync", dst[chunk * 128:(chunk + 1) * 128, dst0 + gp + gw:dst0 + gw + ntok],
                          stg[sl][:, gp:ntok], s_st[sl], reads=rd)

    run_linear(S, Wv, KC, groups, rhs, [n for (_, n) in tl], epi, nslots=3, wsize=8192)
    if own:
        S.close()


def stage_v(nc, w_qkv, hh, g, lo, ntok, dstV, row0, S=None, gap=None):
    own = S is None
    S = S or Stage(nc)
    P = S.P
    Wv = w_qkv.rearrange("(kc p) m -> p kc m", p=128)
    base = g * QKV_G + 2 * D
    pool = S.wpool(3, 8192)
    nsl = len(pool["buf"])
    NST = 4
    vst = S.get("vst", lambda: [S.sb([128, 512], BF16, f"vst{i}") for i in range(NST)])
    s_vs = [P.sem(f"s_vs{i}") for i in range(NST)]
    blocks = conv_tiles(ntok, 128 if gap is None else min(128, gap[0]))
    cntl = S.get("v_cnt", lambda: [0])
    for sl in range(4):
        slot = pool["n"] % nsl
        pool["n"] += 1
        wvs = pool["buf"][slot][:, 0:KC * 512].rearrange("p (k c) -> p k c", c=512)
        c0 = base + sl * 512
        P.dma("gpsimd", wvs, Wv[:, :, c0:c0 + 512], pool["sem"][slot], writes=[("wb", slot)])
        for (a, m) in blocks:
            b = S.bank()
            P._deps("tensor", [("wb", slot)], [("ps", b)], [])
            for kc in range(KC):
                fn = (lambda e, b=b, wvs=wvs, kc=kc, a=a, m=m: e.matmul(
                    S.ps[b][0:m, :], hh[:, kc, lo + a:lo + a + m], wvs[:, kc, :],
                    start=(kc == 0), stop=(kc == KC - 1)))
                if kc == KC - 1:
                    P.op("tensor", fn, reads=[("wb", slot)], writes=[("ps", b)])
                else:
                    P.op("tensor", fn, signal=False)
            cntl[0] += 1
            cnt = cntl[0]
            st = cnt % NST
            if cnt % 2 == 0:
                P.op("scalar", lambda e, st=st, b=b, m=m: e.activation(out=vst[st][0:m, :], in_=S.ps[b][0:m, :], func=AF.Copy),
                     reads=[("ps", b)], writes=[("vst", st)])
            else:
                P.op("vector", lambda e, st=st, b=b, m=m: e.tensor_copy(out=vst[st][0:m, :], in_=S.ps[b][0:m, :]),
                     reads=[("ps", b)], writes=[("vst", st)])
            ra = row0 + a + (gap[1] if (gap is not None and a >= gap[0]) else 0)
            P.dma("sync", dstV[ra:ra + m, sl * 512:(sl + 1) * 512], vst[st][0:m, :], s_vs[st],
                  reads=[("vst", st)])
    if own:
        S.close()


def stage_attn(nc, Qs, Ks, Vs, etab, kbias_d, y):
    S = Stage(nc)
    P = S.P
    NSL = 2
    Qh = [S.sb([128, NG, T], BF16, f"Qh{i}") for i in range(NSL)]
    Kh = [[S.sb([128, TK[g]], BF16, f"Kh{i}_{g}") for g in range(NG)] for i in range(NSL)]
    Vh = [[S.sb([128, NQB[g] + 1, DILS[g], 128], BF16, f"Vh{i}_{g}") for g in range(NG)] for i in range(NSL)]
    Eh = [S.sb([128, NG, 256], F32, f"Eh{i}") for i in range(NSL)]
    kb = S.sb([128, NCH_TOT], F32, "kb")
    ones = S.sb([128, 128], BF16, "ones")
    num = S.sb([128, T], F32, "num")
    den = S.sb([128, T], F32, "den")
    NPT = 7
    pt = [S.sb([128, 256], F32, f"pt{i}") for i in range(NPT)]
    pb = [S.sb([128, 256], BF16, f"pb{i}") for i in range(NPT)]
    s_ld = [P.sem(f"s_ld{i}") for i in range(NSL)]
    s_kb = P.sem("s_kb")
    P.dma("sync", kb[:, :], kbias_d, s_kb, writes=["kb"])
    P.op("vector", lambda e: e.memset(ones[:, :], 1.0), writes=["ones"])
    Vviews = [Vs[g].rearrange("(i p r) c -> p i r c", p=128, r=DILS[g]) for g in range(NG)]

    pending = []

    def load_head(hd):
        s = hd % NSL
        rows = slice(hd * 128, (hd + 1) * 128)
        jobs = []
        for g in range(NG):
            jobs.append((Qh[s][:, g, :], Qs[g][rows, :]))
            jobs.append((Kh[s][g][:, :], Ks[g][rows, :]))
            for i in range(NQB[g] + 1):
                jobs.append((Vh[s][g][:, i, :, :], Vviews[g][:, i, :, rows]))
        jobs.append((Eh[s][:, :, :], etab[:, hd, :, :]))
        for j, (o, i_) in enumerate(jobs):
            pending.append((s, o, i_, j == 0, j == len(jobs) - 1))

    recent = []

    def issue_loads(n):
        for _ in range(n):
            if not pending:
                return
            s, o, i_, first, last = pending.pop(0)
            if len(recent) >= 6:
                P.wait("sync", recent.pop(0))
            tok = P.dma("sync", o, i_, s_ld[s], writes=([("L", s)] if first else []))
            recent.append(tok)
            if last:
                P.last_w[("L", s)] = tok

    units = []
    for hd in range(NH):
        for g in range(NG):
            d = DILS[g]
            for r in range(d):
                for i in range(NQB[g] + 1):
                    units.append((hd, g, r, i))
    nU = len(units)
    LA = 3
    sb_n = [0]
    acc_n = [0]
    acc_of = {}
    sinfo = {}
    state = {"evac_prev": [], "evac_cur": [], "norm": []}

    def s_phase(u):
        hd, g, r, i = units[u]
        s = hd % NSL
        d = DILS[g]
        lo = max(i - 1, 0)
        hi = min(i, NQB[g] - 1)
        N = 128 * (hi - lo + 1)
        bs = sb_n[0] % 4
        sb_n[0] += 1
        k = u % NPT
        kslice = ss(128 * d * i + r, 128, d)
        qslice = ss(128 * d * lo + r, N, d)
        c = CH_OFF[g] + i * d + r
        e0 = 128 if i == 0 else 0
        P.op("tensor", lambda e: e.matmul(S.ps[bs][:, 0:N], Kh[s][g][:, kslice], Qh[s][:, g, qslice],
                                          start=True, stop=True),
             reads=[("L", s)], writes=[("ps", bs)])
        P.op("scalar", lambda e: e.activation(out=pt[k][:, 0:N], in_=S.ps[bs][:, 0:N], func=AF.Exp,
                                              bias=kb[:, c:c + 1], scale=ATT_SCALE),
             reads=[("ps", bs), "kb"], writes=[("pt", k)])
        P.op("vector" if (u % POOL_EVERY) != 0 else "gpsimd",
             lambda e: e.tensor_tensor(out=pb[k][:, 0:N], in0=pt[k][:, 0:N],
                                       in1=Eh[s][:, g, e0:e0 + N], op=ALU.mult),
             reads=[("pt", k), ("L", s)], writes=[("pb", k)])
        sinfo[u] = (lo, hi, k)

    def pv_phase(u):
        hd, g, r, i = units[u]
        s = hd % NSL
        d = DILS[g]
        lo, hi, k = sinfo.pop(u)
        done_blocks = []
        nblk = hi - lo + 1
        for bi, blk in enumerate(range(lo, hi + 1)):
            first = (i == blk)
            last = (i == blk + 1)
            if first:
                acc_of[(hd, g, r, blk)] = acc_n[0] % 2
                acc_n[0] += 1
            par = acc_of[(hd, g, r, blk)]
            bo = 4 + par
            bd = 6 + par
            cols = slice(128 * bi, 128 * (bi + 1))
            if first:
                P._deps("tensor", [], [("ps", bo), ("ps", bd)], [])
            fo = (lambda e, bo=bo, cols=cols, first=first, last=last: e.matmul(
                S.ps[bo][:, 0:128], Vh[s][g][:, i, r, :], pb[k][:, cols], start=first, stop=last))
            fd = (lambda e, bd=bd, cols=cols, first=first, last=last: e.matmul(
                S.ps[bd][:, 0:128], ones[:, :], pb[k][:, cols], start=first, stop=last))
            final = (bi == nblk - 1)
            P.op("tensor", fo, reads=[("pb", k), ("L", s)], signal=False)
            P.op("tensor", fd, reads=[("pb", k), "ones", ("L", s)],
                 writes=([("ps", bo), ("ps", bd)] if last else []), signal=(last or final))
            if last:
                done_blocks.append((blk, bo, bd))
        for (blk, bo, bd) in done_blocks:
            del acc_of[(hd, g, r, blk)]
            tsl = ss(128 * d * blk + r, 128, d)
            if g == 0:
                extra = state["norm"]
                t1 = P.op("scalar", lambda e, bo=bo, tsl=tsl: e.activation(out=num[:, tsl], in_=S.ps[bo][:, 0:128], func=AF.Copy),
                          reads=[("ps", bo)], extra=extra)
                t2 = P.op("vector", lambda e, bd=bd, tsl=tsl: e.tensor_copy(out=den[:, tsl], in_=S.ps[bd][:, 0:128]),
                          reads=[("ps", bd)], extra=extra)
            else:
                extra = state["evac_prev"]
                t1 = P.op("vector", lambda e, bo=bo, tsl=tsl: e.tensor_tensor(out=num[:, tsl], in0=S.ps[bo][:, 0:128],
                                                                               in1=num[:, tsl], op=ALU.add),
                          reads=[("ps", bo)], extra=extra)
                t2 = P.op("vector", lambda e, bd=bd, tsl=tsl: e.tensor_tensor(out=den[:, tsl], in0=S.ps[bd][:, 0:128],
                                                                               in1=den[:, tsl], op=ALU.add),
                          reads=[("ps", bd)], extra=extra)
            state["evac_cur"] = [t1, t2]
        if r == d - 1 and i == NQB[g]:
            state["evac_prev"] = list(state["evac_cur"])
            if g == NG - 1:
                t3 = P.op("vector", lambda e: e.reciprocal(out=den[:, :], in_=den[:, :]), extra=state["evac_prev"])
                t4 = P.op("vector", lambda e, hd=hd: e.tensor_tensor(out=y[:, hd, :], in0=num[:, :], in1=den[:, :],
                                                                       op=ALU.mult), extra=[t3] + state["evac_prev"])
                state["norm"] = [t4]

    load_head(0)
    issue_loads(1000)
    load_head(1)
    for u in range(nU + LA):
        issue_loads(2)
        if u < nU:
            s_phase(u)
        v = u - LA
        if v >= 0:
            pv_phase(v)
            hdv = units[v][0]
            if (v == nU - 1 or units[v + 1][0] != hdv) and hdv + 2 < NH:
                load_head(hdv + 2)
    S.close()


def build_B():
    nc = bass.Bass("TRN2", target_bir_lowering=False)
    x1e = nc.dram_tensor("x1e", [D, T + 2 * HALO], F32, kind="ExternalInput").ap()
    gmix = nc.dram_tensor("gmix", [128, KC], F32, kind="ExternalInput").ap()
    w_qkv = nc.dram_tensor("w_qkv", [D, NG * QKV_G], F32, kind="ExternalInput").ap()
    w_o = nc.dram_tensor("w_o", [D, D], F32, kind="ExternalInput").ap()
    etab = nc.dram_tensor("etab", [128, NH, NG, 256], F32, kind="ExternalInput").ap()
    kbias = nc.dram_tensor("kbias", [128, NCH_TOT], F32, kind="ExternalInput").ap()
    x1p = nc.dram_tensor("x1p", [D, T], F32, kind="ExternalOutput").ap()
    Qs = [nc.dram_tensor(f"Qs{g}", [D, T], BF16).ap() for g in range(NG)]
    Ks = [nc.dram_tensor(f"Ks{g}", [D, TK[g]], BF16).ap() for g in range(NG)]
    Vs = [nc.dram_tensor(f"Vs{g}", [TK[g], D], BF16).ap() for g in range(NG)]
    with ExitStack() as st:
        consts = st.enter_context(nc.sbuf_tensor("consts", [128, KC], F32))
        gm = consts[:, 0:KC]
        S = Stage(nc)
        S.P.dma("sync", gm, gmix, S.P.sem("s_c"))
        S.close()
        with ExitStack() as st2:
            hh = st2.enter_context(nc.sbuf_tensor("hh", [128, KC, 2048], BF16))
            for hf in range(2):
                stage_norm(nc, x1e, 2048 * hf, 2048, gm, h=hh, hoff=0)
                qlo = HALO if hf == 0 else 0
                SQ = Stage(nc)
                SQ.wpool(3, 8192)
                for g in range(NG):
                    klo_ext = HALO - GH[g]
                    khi_ext = HALO + T + GH[g]
                    lo_ext = max(klo_ext, 2048 * hf)
                    hi_ext = min(khi_ext, 2048 * (hf + 1))
                    n = hi_ext - lo_ext
                    stage_qk(nc, w_qkv, hh, g, 0, qlo, T // 2, Qs[g], (T // 2) * hf, S=SQ)
                    stage_qk(nc, w_qkv, hh, g, 1, lo_ext - 2048 * hf, n, Ks[g], lo_ext - klo_ext, S=SQ)
                    stage_v(nc, w_qkv, hh, g, lo_ext - 2048 * hf, n, Vs[g], lo_ext - klo_ext, S=SQ)
                SQ.close()
        with ExitStack() as st2:
            y = st2.enter_context(nc.sbuf_tensor("yatt", [128, NH, T], BF16))
            stage_attn(nc, Qs, Ks, Vs, etab, kbias, y)
            stage_proj_res(nc, w_o.rearrange("(kc p) m -> p kc m", p=128), KC, y, T, x1e, HALO, x1p, 0)
    release_sems(nc)
    return nc


def build_C():
    nc = bass.Bass("TRN2", target_bir_lowering=False)
    xin = nc.dram_tensor("xin", [D, T + 2], F32, kind="ExternalInput").ap()
    gffn = nc.dram_tensor("gffn", [128, KC], F32, kind="ExternalInput").ap()
    gfin = nc.dram_tensor("gfin", [128, KC], F32, kind="ExternalInput").ap()
    w_up = nc.dram_tensor("w_up", [D, 2 * F], F32, kind="ExternalInput").ap()
    w_down = nc.dram_tensor("w_down", [F, D], F32, kind="ExternalInput").ap()
    ffp = nc.dram_tensor("ffp", [128, 2 * FC, 4], F32, kind="ExternalInput").ap()
    x2 = nc.dram_tensor("x2", [D, T], F32).ap()
    outT = nc.dram_tensor("outT", [D, T], F32, kind="ExternalOutput").ap()
    with ExitStack() as st:
        consts = st.enter_context(nc.sbuf_tensor("consts", [128, 2 * KC + 2 * FC * 4], F32))
        gf = consts[:, 0:KC]
        gl = consts[:, KC:2 * KC]
        fp = consts[:, 2 * KC:].rearrange("p (j c) -> p j c", c=4)
        S = Stage(nc)
        sm = S.P.sem("s_c")
        S.P.dma("sync", gf, gffn, sm)
        S.P.dma("sync", gl, gfin, sm)
        S.P.dma("sync", fp, ffp, sm)
        S.close()
        with ExitStack() as st2:
            ffn_layer(nc, st2, w_up, w_down, fp, gf, xin, 0, x2)
        stage_norm(nc, x2, 0, T, gl, outT=outT, out0=0)
    release_sems(nc)
    return nc


NCONST = 3 * KC + 2 * KC + KC * 4 + 2 * (2 * FC * 4) + 2
CC_GROUPS = [[0, 1, 2, 3], [4, 5, 6, 7]]
CCN = 4
PW = 128
NPIECE = T // PW
HPW = 256
HNP = T // HPW
SERIAL_CC = False
EW = 16


def stage_allgather(nc, src, dst):
    S = Stage(nc)
    cs = get_sem(nc, "s_cc")
    cs.n += 1
    v = cs.n
    S.P.ops["gpsimd"].append(lambda g: g.collective_compute(
        "AllGather", ALU.bypass, replica_groups=CC_GROUPS, ins=[src], outs=[dst]).then_inc(cs.h))
    S.P.ops["gpsimd"].append(lambda g: g.wait_ge(cs.h, v))
    S.close()


def stage_exchange_pieces(nc, x1own, xp, G):
    S = Stage(nc)
    P = S.P
    s_rp = [P.sem(f"s_rp{i}") for i in range(4)]
    toks = []
    for q in range(NPIECE):
        toks.append(P.dma("sync", xp[q], x1own[:, q * PW:(q + 1) * PW], s_rp[q % 4]))
        if q >= 3:
            P.wait("sync", toks[q - 3])
    S.close()
    S = Stage(nc)
    cs = get_sem(nc, "s_cc")
    for q in range(NPIECE):
        cs.n += 1
        S.P.ops["gpsimd"].append(lambda g, q=q: g.collective_compute(
            "AllGather", ALU.bypass, replica_groups=CC_GROUPS, ins=[xp[q]], outs=[G[q]]).then_inc(cs.h))
        if SERIAL_CC:
            v = cs.n
            S.P.ops["gpsimd"].append(lambda g, v=v: g.wait_ge(cs.h, v))
    S.close()
    return (cs, cs.n)


def stage_exchange_h(nc, hh, hp, Gh):
    S = Stage(nc)
    P = S.P
    s_hp = [P.sem(f"s_rp{i}") for i in range(4)]
    toks = []
    for q in range(HNP):
        toks.append(P.dma("sync", hp[q].rearrange("(kc p) t -> p kc t", p=128), hh[:, :, q * HPW:(q + 1) * HPW],
                          s_hp[q % 4]))
        if q >= 3:
            P.wait("sync", toks[q - 3])
    S.close()
    S = Stage(nc)
    cs = get_sem(nc, "s_cc")
    for q in range(HNP):
        cs.n += 1
        S.P.ops["gpsimd"].append(lambda g, q=q: g.collective_compute(
            "AllGather", ALU.bypass, replica_groups=CC_GROUPS, ins=[hp[q]], outs=[Gh[q]]).then_inc(cs.h))
    S.close()
    return (cs, cs.n)


def stage_load_halo_h(nc, hh, Gh, cc_tok):
    S = Stage(nc)
    P = S.P
    s_hl = [P.sem(f"s_rp{i}") for i in range(4)]
    P.wait("sync", cc_tok)
    toks = []
    n = 0
    for side in range(2):
        for j in range(HALO // HPW):
            q = (T - HALO) // HPW + j if side == 0 else j
            sh = 3 if side == 0 else 1
            c0 = side * HALO + j * HPW

            def f(e, q=q, sh=sh):
                r = (e.partition_id() + sh) % 4
                return Gh[q][bass.ds(r * D, D), :].rearrange("(kc p) t -> p kc t", p=128)
            toks.append(P.dma("sync", hh[:, :, c0:c0 + HPW], f, s_hl[n % 4]))
            if n >= 3:
                P.wait("sync", toks[n - 3])
            n += 1
    S.close()


def stage_edges(nc, edge_all, xC, vmask):
    S = Stage(nc)
    P = S.P
    ec = S.sb([128, 2, KC, EW], F32, "ec")
    s_e = P.sem("s_e")
    s_e2 = P.sem("s_e2")

    def srcf(side):
        def f(e):
            pid = e.partition_id()
            r = (pid + 3) % 4 if side == 0 else (pid + 1) % 4
            c0 = EW if side == 0 else 0
            return edge_all[bass.ds(r * D, D), c0:c0 + EW].rearrange("(kc p) c -> p kc c", p=128)
        return f

    P.dma("sync", ec[:, 0, :, :], srcf(0), s_e, writes=[("ec", 0)])
    P.dma("sync", ec[:, 1, :, :], srcf(1), s_e2, writes=[("ec", 1)])
    P.op("vector", lambda e: e.tensor_scalar(out=ec[:, 0, :, :], in0=ec[:, 0, :, :], scalar1=vmask[:, 0:1], scalar2=None,
                                             op0=ALU.mult), reads=[("ec", 0)], writes=[("ec", 0)])
    P.op("vector", lambda e: e.tensor_scalar(out=ec[:, 1, :, :], in0=ec[:, 1, :, :], scalar1=vmask[:, 1:2], scalar2=None,
                                             op0=ALU.mult), reads=[("ec", 1)], writes=[("ec", 1)])
    xv = xC.rearrange("(kc p) t -> p kc t", p=128)
    P.dma("sync", xv[:, :, 0:EW], ec[:, 0, :, :], s_e, reads=[("ec", 0)])
    P.dma("sync", xv[:, :, EW + T:EW + T + EW], ec[:, 1, :, :], s_e2, reads=[("ec", 1)])
    S.close()


def build_fused():
    nc = bass.Bass("TRN2", target_bir_lowering=False)
    EI = "ExternalInput"
    xin = nc.dram_tensor("xin", [D, T + 4], F32, kind=EI).ap()
    cst = nc.dram_tensor("cst", [128, NCONST], F32, kind=EI).ap()
    w_in = nc.dram_tensor("w_in", [D, 3 * D], F32, kind=EI).ap()
    w_out = nc.dram_tensor("w_out", [D, D], F32, kind=EI).ap()
    w_up0 = nc.dram_tensor("w_up0", [D, 2 * F], F32, kind=EI).ap()
    w_down0 = nc.dram_tensor("w_down0", [F, D], F32, kind=EI).ap()
    w_qkv = nc.dram_tensor("w_qkv", [D, NG * QKV_G], F32, kind=EI).ap()
    w_o = nc.dram_tensor("w_o", [D, D], F32, kind=EI).ap()
    w_up1 = nc.dram_tensor("w_up1", [D, 2 * F], F32, kind=EI).ap()
    w_down1 = nc.dram_tensor("w_down1", [F, D], F32, kind=EI).ap()
    etab = nc.dram_tensor("etab", [128, NH, NG, 256], F32, kind=EI).ap()
    kbias = nc.dram_tensor("kbias", [128, NCH_TOT], F32, kind=EI).ap()
    outT = nc.dram_tensor("outT", [D, T], F32, kind="ExternalOutput").ap()
    xa = nc.dram_tensor("xa", [D, T + 2], F32).ap()
    x1own = nc.dram_tensor("x1own", [D, T], F32).ap()
    hp = [nc.dram_tensor(f"hp{q}", [D, HPW], BF16).ap() for q in range(HNP)]
    Gh = [nc.dram_tensor(f"Gh{q}", [CCN * D, HPW], BF16).ap() for q in range(HNP)]
    Qs = [nc.dram_tensor(f"Qs{g}", [D, T], BF16).ap() for g in range(NG)]
    Ks = [nc.dram_tensor(f"Ks{g}", [D, TK[g]], BF16).ap() for g in range(NG)]
    Vs = [nc.dram_tensor(f"Vs{g}", [TK[g], D], BF16).ap() for g in range(NG)]
    xC = nc.dram_tensor("xC", [D, T + 2 * EW], F32).ap()
    edge_in = nc.dram_tensor("edge_in", [D, 2 * EW], F32).ap()
    edge_all = nc.dram_tensor("edge_all", [CCN * D, 2 * EW], F32).ap()
    x2 = nc.dram_tensor("x2", [D, T], F32).ap()
    nc.cache_partition_id()
    with ExitStack() as st:
        consts = st.enter_context(nc.sbuf_tensor("consts", [128, NCONST], F32))
        o = 0
        gm0 = consts[:, o:o + KC]; o += KC
        gf0 = consts[:, o:o + KC]; o += KC
        gm1 = consts[:, o:o + KC]; o += KC
        gf1 = consts[:, o:o + KC]; o += KC
        gfin = consts[:, o:o + KC]; o += KC
        sc = consts[:, o:o + KC * 4].rearrange("p (j c) -> p j c", c=4); o += KC * 4
        fp0 = consts[:, o:o + 2 * FC * 4].rearrange("p (j c) -> p j c", c=4); o += 2 * FC * 4
        fp1 = consts[:, o:o + 2 * FC * 4].rearrange("p (j c) -> p j c", c=4); o += 2 * FC * 4
        vmask = consts[:, o:o + 2]; o += 2
        assert o == NCONST
        S = Stage(nc)
        S.P.dma("sync", consts[:, :], cst, S.P.sem("s_c"))
        S.close()
        with ExitStack() as st2:
            h = st2.enter_context(nc.sbuf_tensor("h", [128, KC, T + 4], BF16))
            stage_norm(nc, xin, 0, T + 4, gm0, h=h, hoff=0)
            y = st2.enter_context(nc.sbuf_tensor("y", [128, KC, T + 2], BF16))
            SA = Stage(nc)
            stage_sconv_in(nc, w_in, sc, h, y, S=SA)
            stage_proj_res(nc, w_out.rearrange("(kc p) m -> p kc m", p=128), KC, y, T + 2, xin, 1, xa, 0, tn=410,
                           S=SA, extra=SA.prod("y"))
            SA.close()
        with ExitStack() as st2:
            ffn_layer(nc, st2, w_up0, w_down0, fp0, gf0, xa, 0, x1own)
        with ExitStack() as st2:
            hh = st2.enter_context(nc.sbuf_tensor("hh", [128, KC, 2048], BF16))
            stage_norm(nc, x1own, 0, T, gm1, h=hh, hoff=0)
            cc_tok = stage_exchange_h(nc, hh, hp, Gh)
            SQ = Stage(nc)
            SQ.wpool(3, 8192)
            for g in range(NG):
                stage_qk(nc, w_qkv, hh, g, 0, 0, T, Qs[g], 0, S=SQ)
                stage_qk(nc, w_qkv, hh, g, 1, 0, T, Ks[g], GH[g], S=SQ)
                stage_v(nc, w_qkv, hh, g, 0, T, Vs[g], GH[g], S=SQ)
            SQ.close()
            stage_load_halo_h(nc, hh, Gh, cc_tok)
            SQ = Stage(nc)
            SQ.wpool(3, 8192)
            for g in range(NG):
                gh = GH[g]
                stage_qk(nc, w_qkv, hh, g, 1, HALO - gh, 2 * gh, Ks[g], 0, S=SQ, gap=(gh, T))
                stage_v(nc, w_qkv, hh, g, HALO - gh, 2 * gh, Vs[g], 0, S=SQ, gap=(gh, T))
            SQ.close()
        with ExitStack() as st2:
            yat = st2.enter_context(nc.sbuf_tensor("yatt", [128, NH, T], BF16))
            stage_attn(nc, Qs, Ks, Vs, etab, kbias, yat)
            stage_proj_res(nc, w_o.rearrange("(kc p) m -> p kc m", p=128), KC, yat, T, x1own, 0, xC, EW, edge=edge_in)
        stage_allgather(nc, edge_in[:, :], edge_all[:, :])
        stage_edges(nc, edge_all, xC, vmask)
        with ExitStack() as st2:
            ffn_layer(nc, st2, w_up1, w_down1, fp1, gf1, xC, EW - 1, x2)
        stage_norm(nc, x2, 0, T, gfin, outT=outT, out0=0)
    release_sems(nc)
    return nc


def col_layout(v):
    return np.ascontiguousarray(v.reshape(-1, 128).T.astype(np.float32))


def conv_layout(w, b):
    C = w.shape[1] // 128
    out = np.empty((128, C, 4), np.float32)
    for k in range(3):
        out[:, :, k] = w[k].reshape(C, 128).T
    out[:, :, 3] = b.reshape(C, 128).T
    return out


def core_tokens(c):
    b = c // 4
    a = (c % 4) * T
    return b, a


def padded_slice_T(x, b, lo, hi):
    out = np.zeros((x.shape[2], hi - lo), np.float32)
    l2, h2 = max(lo, 0), min(hi, x.shape[1])
    out[:, l2 - lo:h2 - lo] = x[b, l2:h2, :].T
    return out


def alibi_tables():
    slopes = (2.0 ** (-8.0 * np.arange(1, NH + 1) / NH)).astype(np.float64)
    kp = np.arange(128)[:, None]
    qf = np.arange(128)[None, :]
    dB = qf - kp - 64
    dA = qf - kp + 64
    out = np.zeros((128, NH, NG, 256), np.float32)
    for h in range(NH):
        for g in range(NG):
            d = DILS[g]
            eb = np.where(np.abs(dB) <= 64, np.exp(-slopes[h] * d * np.abs(dB)), 0.0)
            ea = np.where(np.abs(dA) <= 64, np.exp(-slopes[h] * d * np.abs(dA)), 0.0)
            out[:, h, g, 0:128] = eb
            out[:, h, g, 128:256] = ea
    return out


def key_bias(a):
    out = np.zeros((128, NCH_TOT), np.float32)
    p = np.arange(128)
    for g in range(NG):
        d = DILS[g]
        for i in range(NQB[g] + 1):
            for r in range(d):
                k = 128 * d * i + d * p + r
                pos = a + k - GH[g]
                out[:, CH_OFF[g] + i * d + r] = np.where((pos >= 0) & (pos < SEQ), 0.0, NEG)
    return out


_NC_CACHE = {}


def get_nc(name, fn):
    if name not in _NC_CACHE:
        _NC_CACHE[name] = fn()
    return _NC_CACHE[name]


def run_A(inp):
    x = inp["x"]
    nc = get_nc("A", build_A)
    in_maps = []
    shared = {
        "gmix": col_layout(inp["mix_norm_g"][0]),
        "gffn": col_layout(inp["ffn_norm_g"][0]),
        "w_in": np.ascontiguousarray(inp["sc_w_in"][0]),
        "w_out": np.ascontiguousarray(inp["sc_w_out"][0]),
        "scp": conv_layout(inp["sc_conv_w"][0], inp["sc_conv_b"][0]),
        "w_up": np.ascontiguousarray(inp["ffn_w_up"][0]),
        "w_down": np.ascontiguousarray(inp["ffn_w_down"][0]),
        "ffp": conv_layout(inp["ffn_conv_w"][0], inp["ffn_conv_b"][0]),
    }
    for c in range(NCORES):
        b, a = core_tokens(c)
        m = dict(shared)
        m["xin"] = padded_slice_T(x, b, a - 2, a + T + 2)
        in_maps.append(m)
    res = run_bass_kernel_spmd(nc, in_maps, core_ids=list(range(NCORES)))
    x1 = np.empty_like(x)
    for c in range(NCORES):
        b, a = core_tokens(c)
        x1[b, a:a + T, :] = res.results[c]["x1"].T
    return x1


def run_B(inp, x1):
    nc = get_nc("B", build_B)
    shared = {
        "gmix": col_layout(inp["mix_norm_g"][1]),
        "w_qkv": np.ascontiguousarray(inp["attn_w_qkv"][0]),
        "w_o": np.ascontiguousarray(inp["attn_w_out"][0]),
        "etab": alibi_tables(),
    }
    in_maps = []
    for c in range(NCORES):
        b, a = core_tokens(c)
        m = dict(shared)
        m["x1e"] = padded_slice_T(x1, b, a - HALO, a + T + HALO)
        m["kbias"] = key_bias(a)
        in_maps.append(m)
    res = run_bass_kernel_spmd(nc, in_maps, core_ids=list(range(NCORES)))
    x1p = np.empty_like(x1)
    for c in range(NCORES):
        b, a = core_tokens(c)
        x1p[b, a:a + T, :] = res.results[c]["x1p"].T
    return x1p


def run_C(inp, x1p):
    nc = get_nc("C", build_C)
    shared = {
        "gffn": col_layout(inp["ffn_norm_g"][1]),
        "gfin": col_layout(inp["final_norm_g"]),
        "w_up": np.ascontiguousarray(inp["ffn_w_up"][1]),
        "w_down": np.ascontiguousarray(inp["ffn_w_down"][1]),
        "ffp": conv_layout(inp["ffn_conv_w"][1], inp["ffn_conv_b"][1]),
    }
    in_maps = []
    for c in range(NCORES):
        b, a = core_tokens(c)
        m = dict(shared)
        m["xin"] = padded_slice_T(x1p, b, a - 1, a + T + 1)
        in_maps.append(m)
    res = run_bass_kernel_spmd(nc, in_maps, core_ids=list(range(NCORES)))
    out = np.empty_like(x1p)
    for c in range(NCORES):
        b, a = core_tokens(c)
        out[b, a:a + T, :] = res.results[c]["outT"].T
    return out


def run_fused(inp):
    x = inp["x"]
    nc = get_nc("F", build_fused)
    cparts = [
        col_layout(inp["mix_norm_g"][0]), col_layout(inp["ffn_norm_g"][0]),
        col_layout(inp["mix_norm_g"][1]), col_layout(inp["ffn_norm_g"][1]),
        col_layout(inp["final_norm_g"]),
        conv_layout(inp["sc_conv_w"][0], inp["sc_conv_b"][0]).reshape(128, -1),
        conv_layout(inp["ffn_conv_w"][0], inp["ffn_conv_b"][0]).reshape(128, -1),
        conv_layout(inp["ffn_conv_w"][1], inp["ffn_conv_b"][1]).reshape(128, -1),
    ]
    shared = {
        "w_in": np.ascontiguousarray(inp["sc_w_in"][0]),
        "w_out": np.ascontiguousarray(inp["sc_w_out"][0]),
        "w_up0": np.ascontiguousarray(inp["ffn_w_up"][0]),
        "w_down0": np.ascontiguousarray(inp["ffn_w_down"][0]),
        "w_qkv": np.ascontiguousarray(inp["attn_w_qkv"][0]),
        "w_o": np.ascontiguousarray(inp["attn_w_out"][0]),
        "w_up1": np.ascontiguousarray(inp["ffn_w_up"][1]),
        "w_down1": np.ascontiguousarray(inp["ffn_w_down"][1]),
        "etab": alibi_tables(),
    }
    in_maps = []
    for c in range(NCORES):
        b, a = core_tokens(c)
        m = dict(shared)
        m["xin"] = padded_slice_T(x, b, a - 2, a + T + 2)
        m["kbias"] = key_bias(a)
        vm = np.zeros((128, 2), np.float32)
        vm[:, 0] = 1.0 if a > 0 else 0.0
        vm[:, 1] = 1.0 if a + T < SEQ else 0.0
        m["cst"] = np.ascontiguousarray(np.concatenate(cparts + [vm], axis=1))
        in_maps.append(m)
    res = run_bass_kernel_spmd(nc, in_maps, core_ids=list(range(NCORES)))
    out = np.empty_like(x)
    for c in range(NCORES):
        b, a = core_tokens(c)
        out[b, a:a + T, :] = res.results[c]["outT"].T
    return out


FUSED = True


def kernel(**inputs):
    inp = {k: np.asarray(v) for k, v in inputs.items()}
    if FUSED:
        return run_fused(inp).astype(np.float32)
    x1 = run_A(inp)
    x1p = run_B(inp, x1)
    return run_C(inp, x1p).astype(np.float32)
```

```python
import numpy as np
from contextlib import ExitStack
import concourse.bass as bass
import concourse.mybir as mybir
from concourse.bass_utils import run_bass_kernel_spmd

F32 = mybir.dt.float32
BF16 = mybir.dt.bfloat16
AF = mybir.ActivationFunctionType
ALU = mybir.AluOpType

D = 2048
KC = 16
T = 2048
NCORES = 8
SEQ = 8192
F = 5632
FC = 44
NH = 16
NG = 3
DILS = (1, 4, 16)
HALO = 1024
EPS = 1e-5
FG_SIZES = [15, 15, 14]
FG_OFF = [0, 15, 30]
FGROUPS = len(FG_SIZES)
FGC = max(FG_SIZES)

ENGS = ["sync", "scalar", "vector", "gpsimd", "tensor"]


class Sem:
    def __init__(self, nc, stack, name):
        self.h = stack.enter_context(nc.semaphore(uid(name)))
        self.n = 0

    def inc(self, k):
        self.n += k
        return (self, self.n)


_SEMS = {}


def get_sem(nc, name):
    key = (id(nc), name)
    if key not in _SEMS:
        stack = _SEMS.setdefault((id(nc), "__stack__"), ExitStack())
        _SEMS[key] = Sem(nc, stack, name)
    return _SEMS[key]


def release_sems(nc):
    st = _SEMS.pop((id(nc), "__stack__"), None)
    for k in [k for k in _SEMS if k[0] == id(nc)]:
        del _SEMS[k]
    if st is not None:
        st.close()


class Prog:
    def __init__(self, nc, stack):
        self.nc = nc
        self.stack = stack
        self.ops = {e: [] for e in ENGS}
        self.done = {e: get_sem(nc, "done_" + e) for e in ["scalar", "vector", "gpsimd", "tensor"]}
        self.waited = {}
        self.last_w = {}
        self.readers = {}
        self.dma_sems = []

    def sem(self, name):
        s = get_sem(self.nc, name)
        if s not in self.dma_sems:
            self.dma_sems.append(s)
        return s

    def wait(self, eng, tok):
        if tok is None:
            return
        s, v = tok
        key = (eng, id(s))
        if self.waited.get(key, 0) >= v:
            return
        self.waited[key] = v
        self.ops[eng].append(lambda e, s=s, v=v: e.wait_ge(s.h, v))

    def _deps(self, eng, reads, writes, extra):
        for w in extra:
            self.wait(eng, w)
        for k in reads:
            if k in self.last_w:
                self.wait(eng, self.last_w[k])
        for k in writes:
            if k in self.last_w:
                self.wait(eng, self.last_w[k])
            for tok in self.readers.get(k, {}).values():
                self.wait(eng, tok)

    def _track(self, tok, reads, writes):
        s, v = tok
        for k in reads:
            self.readers.setdefault(k, {})[id(s)] = tok
        for k in writes:
            self.last_w[k] = tok
            self.readers[k] = {}

    def op(self, eng, fn, reads=(), writes=(), extra=(), signal=True):
        self._deps(eng, reads, writes, extra)
        if not signal:
            self.ops[eng].append(lambda e, fn=fn: fn(e))
            return None
        s = self.done[eng]
        tok = s.inc(1)
        self.ops[eng].append(lambda e, fn=fn, s=s: fn(e).then_inc(s.h, 1))
        self._track(tok, reads, writes)
        return tok

    def dma(self, eng, out, in_, sem, reads=(), writes=(), extra=(), slow=False):
        self._deps(eng, reads, writes, extra)
        tok = sem.inc(16)
        kw = {"allow_slow_non_contiguous": True} if slow else {}
        self.ops[eng].append(
            lambda e, out=out, in_=in_, sem=sem, kw=kw: e.dma_start(
                out=(out(e) if callable(out) else out),
                in_=(in_(e) if callable(in_) else in_), **kw).then_inc(sem.h, 16))
        self._track(tok, reads, writes)
        return tok

    def finish(self):
        for s in self.dma_sems:
            if s.n > 0:
                self.wait("sync", (s, s.n))

    def emit(self):
        self.finish()
        nc = self.nc
        with nc.Block() as block:
            for name in ENGS:
                ops = self.ops[name]

                def body(e, ops=ops):
                    for f in ops:
                        f(e)
                getattr(block, name)(body)


_UID = [0]


def uid(name):
    _UID[0] += 1
    return f"{name}_{_UID[0]}"


class Stage:
    def __init__(self, nc):
        self.nc = nc
        self.st = ExitStack()
        self.P = Prog(nc, self.st)
        self.ps = [self.st.enter_context(nc.psum_tensor(uid(f"ps{i}"), [128, 512], F32)) for i in range(8)]
        self.psn = 0

    def sb(self, shape, dt, name=None):
        return self.st.enter_context(self.nc.sbuf_tensor(uid(name or "t"), shape, dt))

    def get(self, key, fn):
        if not hasattr(self, "cache"):
            self.cache = {}
        if key not in self.cache:
            self.cache[key] = fn()
        return self.cache[key]

    def wpool(self, nslots=3, wsize=6144):
        def mk():
            return {"buf": [self.sb([128, wsize], BF16, f"wraw{i}") for i in range(nslots)],
                    "sem": [self.P.sem(f"s_w{i}") for i in range(nslots)], "n": 0, "size": wsize}
        return self.get("wpool", mk)

    def note(self, name, tok):
        if tok is None:
            return
        d = self.get(("prod", name), dict)
        d[id(tok[0])] = tok

    def prod(self, name):
        return list(self.get(("prod", name), dict).values())

    def bank(self):
        b = self.psn % 8
        self.psn += 1
        return b

    def close(self):
        self.P.emit()
        self.st.close()


def stage_norm(nc, xT, tok0, ntok, gcol, h=None, hoff=0, outT=None, out0=0, xsrc=None, pre_wait=None):
    S = Stage(nc)
    P = S.P
    TN = 512
    nt = (ntok + TN - 1) // TN
    NSET = 2
    xs = [S.sb([128, KC, TN], F32, f"xs{i}") for i in range(NSET)]
    sq = [S.sb([128, KC, TN], BF16, f"sq{i}") for i in range(NSET)]
    rs = [S.sb([128, TN], F32, f"rs{i}") for i in range(NSET)]
    ones = S.sb([128, 128], BF16, "ones")
    s_x = [P.sem(f"s_x{i}") for i in range(NSET)]
    s_o = [P.sem(f"s_o{i}") for i in range(NSET)]
    ob = None
    if outT is not None:
        ob = [S.sb([128, KC, TN], F32, f"ob{i}") for i in range(NSET)]
    epsc = S.sb([128, 1], F32, "epsc")
    P.op("vector", lambda e: e.memset(ones[:, :], 1.0), writes=["ones"])
    P.op("vector", lambda e: e.memset(epsc[:, :], EPS), writes=["epsc"])
    xv = xT.rearrange("(kc p) t -> p kc t", p=128) if xT is not None else None
    ov = outT.rearrange("(kc p) t -> p kc t", p=128) if outT is not None else None
    if pre_wait is not None:
        P.wait("sync", pre_wait)
    for t in range(nt):
        s = t % NSET
        a = t * TN
        n = min(TN, ntok - a)
        src = xsrc(a, n) if xsrc is not None else xv[:, :, tok0 + a:tok0 + a + n]
        if isinstance(src, list):
            tokp = None
            for pi, (off, m, sp) in enumerate(src):
                tokp = P.dma("sync", xs[s][:, :, off:off + m], sp, s_x[s], writes=([("xs", s)] if pi == 0 else []))
            P.last_w[("xs", s)] = tokp
        else:
            P.dma("sync", xs[s][:, :, 0:n], src, s_x[s], writes=[("xs", s)])
        P.op("scalar", lambda e, s=s, n=n: e.activation(out=sq[s][:, :, 0:n], in_=xs[s][:, :, 0:n], func=AF.Square),
             reads=[("xs", s)], writes=[("sq", s)])
        b = S.bank()
        for kc in range(KC):
            last = kc == KC - 1
            fn = (lambda e, b=b, s=s, kc=kc, n=n: e.matmul(S.ps[b][:, 0:n], ones[:, :], sq[s][:, kc, 0:n],
                                                          start=(kc == 0), stop=(kc == KC - 1)))
            if kc == 0:
                P._deps("tensor", ["ones", ("sq", s)], [("ps", b)], [])
            if last:
                P.op("tensor", fn, reads=["ones", ("sq", s)], writes=[("ps", b)])
            else:
                P.op("tensor", fn, signal=False)
        P.op("scalar", lambda e, s=s, b=b, n=n: e.activation(
            out=rs[s][:, 0:n], in_=S.ps[b][:, 0:n], func=AF.Sqrt, bias=epsc[:, 0:1], scale=1.0 / D),
            reads=[("ps", b), "epsc"], writes=[("rs", s)])
        P.op("vector", lambda e, s=s, n=n: e.reciprocal(out=rs[s][:, 0:n], in_=rs[s][:, 0:n]),
             reads=[("rs", s)], writes=[("rs", s)])
        for kc in range(KC):
            eng = "vector"
            if outT is None:
                dst = h[:, kc, hoff + a:hoff + a + n]
                wr = []
            else:
                dst = ob[s][:, kc, 0:n]
                wr = [("ob", s, kc)]
            P.op(eng, lambda e, s=s, kc=kc, n=n, dst=dst: e.scalar_tensor_tensor(
                out=dst, in0=xs[s][:, kc, 0:n], scalar=gcol[:, kc:kc + 1], in1=rs[s][:, 0:n],
                op0=ALU.mult, op1=ALU.mult),
                reads=[("xs", s), ("rs", s)], writes=wr)
        if outT is not None:
            P.dma("sync", ov[:, :, out0 + a:out0 + a + n], ob[s][:, :, 0:n], s_o[s],
                  reads=[("ob", s, kc) for kc in range(KC)])
    S.close()


def run_linear(S, Wv, KCn, groups, rhs, tiles, epilogue, pre=None, nslots=3, extra=(), wsize=6144):
    P = S.P
    G = max(len(g) for g in groups)
    pool = S.wpool(nslots, wsize)
    assert KCn * G * 128 <= pool["size"]
    nsl = len(pool["buf"])
    first = True
    for gi, cols in enumerate(groups):
        slot = pool["n"] % nsl
        pool["n"] += 1
        wb = pool["buf"][slot][:, 0:KCn * G * 128].rearrange("p (k g c) -> p k g c", g=G, c=128)
        wtok = None
        for ci, c0 in enumerate(cols):
            wtok = P.dma("gpsimd", wb[:, :, ci, :], Wv[:, :, c0:c0 + 128], pool["sem"][slot],
                         writes=([("wb", slot)] if ci == 0 else []))
        P.last_w[("wb", slot)] = wtok
        for ti, n in enumerate(tiles):
            if pre is not None:
                pre(gi, ti)
            banks = []
            for ci in range(len(cols)):
                b = S.bank()
                banks.append(b)
                P._deps("tensor", [("wb", slot)], [("ps", b)], extra if first else [])
                first = False
                for kc in range(KCn):
                    fn = (lambda e, b=b, wb=wb, ci=ci, kc=kc, ti=ti, n=n: e.matmul(
                        S.ps[b][:, 0:n], wb[:, kc, ci, :], rhs(kc, ti),
                        start=(kc == 0), stop=(kc == KCn - 1)))
                    if kc == KCn - 1:
                        P.op("tensor", fn, reads=[("wb", slot)], writes=[("ps", b)])
                    else:
                        P.op("tensor", fn, signal=False)
            epilogue(gi, ti, banks, n)


def conv_tiles(nout, tn=410):
    tiles = []
    a = 0
    while a < nout:
        n = min(tn, nout - a)
        tiles.append((a, n))
        a += n
    return tiles


def stage_sconv_in(nc, w_in, convp, h, y, S=None):
    own = S is None
    S = S or Stage(nc)
    P = S.P
    ct = conv_tiles(T + 2)
    Wv = w_in.rearrange("(kc p) m -> p kc m", p=128)
    groups = [[j * 128, D + j * 128, 2 * D + j * 128] for j in range(KC)]
    NS = 3
    cs = [S.sb([128, 412], F32, f"cs{i}") for i in range(NS)]
    cu = [S.sb([128, 412], F32, f"cu{i}") for i in range(NS)]
    a1 = [S.sb([128, 412], F32, f"a1{i}") for i in range(NS)]
    cnt = [0]

    def rhs(kc, ti):
        a, n = ct[ti]
        return h[:, kc, a:a + n + 2]

    def epi(gi, ti, banks, n2):
        a, n = ct[ti]
        bu, bb, bc = banks
        s = cnt[0] % NS
        cnt[0] += 1
        j = gi
        w0 = convp[:, j, 0:1]
        w1 = convp[:, j, 1:2]
        w2 = convp[:, j, 2:3]
        bia = convp[:, j, 3:4]
        P.op("scalar", lambda e: e.activation(out=cs[s][:, 0:n + 2], in_=S.ps[bc][:, 0:n + 2], func=AF.Copy),
             reads=[("ps", bc)], writes=[("cs", s)])
        P.op("vector", lambda e: e.tensor_tensor(out=cu[s][:, 0:n + 2], in0=S.ps[bu][:, 0:n + 2],
                                                 in1=cs[s][:, 0:n + 2], op=ALU.mult),
             reads=[("ps", bu), ("cs", s)], writes=[("cu", s)])
        P.op("scalar", lambda e: e.activation(out=a1[s][:, 0:n], in_=cu[s][:, 1:n + 1], func=AF.Identity,
                                              bias=bia, scale=w1),
             reads=[("cu", s)], writes=[("a1", s)])
        P.op("vector", lambda e: e.scalar_tensor_tensor(out=a1[s][:, 0:n], in0=cu[s][:, 0:n], scalar=w0,
                                                        in1=a1[s][:, 0:n], op0=ALU.mult, op1=ALU.add),
             reads=[("cu", s), ("a1", s)], writes=[("a1", s)])
        P.op("vector", lambda e: e.scalar_tensor_tensor(out=a1[s][:, 0:n], in0=cu[s][:, 2:n + 2], scalar=w2,
                                                        in1=a1[s][:, 0:n], op0=ALU.mult, op1=ALU.add),
             reads=[("cu", s), ("a1", s)], writes=[("a1", s)])
        S.note("y", P.op("vector", lambda e: e.tensor_tensor(out=y[:, j, a:a + n], in0=S.ps[bb][:, 1:n + 1],
                                                             in1=a1[s][:, 0:n], op=ALU.mult),
                         reads=[("ps", bb), ("a1", s)], writes=[]))

    run_linear(S, Wv, KC, groups, rhs, [n + 2 for (_, n) in ct], epi, nslots=3)
    if own:
        S.close()


def stage_proj_res(nc, Wv, KCn, rhs_t, ntok, xin, xin0, xout, xout0, tn=512, edge=None, S=None, extra=(), tag="x"):
    own = S is None
    S = S or Stage(nc)
    P = S.P
    tl = conv_tiles(ntok, tn)
    groups = [[m * 128] for m in range(KC)]
    NX = 8
    xb = S.get("xb", lambda: [S.sb([128, 512], F32, f"xb{i}") for i in range(NX)])
    s_l = [P.sem(f"s_l{i}") for i in range(NX)]
    s_s = [P.sem(f"s_s{i}") for i in range(NX)]
    order = [(gi, ti) for gi in range(KC) for ti in range(len(tl))]
    base = S.get("xbn", lambda: [0])
    i_base = base[0]
    base[0] += len(order)
    idx = {k: i for i, k in enumerate(order)}
    loaded = [0]

    def load(i):
        gi, ti = order[i]
        a, n = tl[ti]
        s = (i_base + i) % NX
        P.dma("sync", xb[s][:, 0:n], xin[gi * 128:(gi + 1) * 128, xin0 + a:xin0 + a + n], s_l[s],
              reads=[("xd", tag, gi, ti)], writes=[("xb", s)])

    def pre(gi, ti):
        i = idx[(gi, ti)]
        while loaded[0] <= min(i + 5, len(order) - 1):
            load(loaded[0])
            loaded[0] += 1

    def rhs(kc, ti):
        a, n = tl[ti]
        return rhs_t[:, kc, a:a + n]

    def epi(gi, ti, banks, n):
        i = idx[(gi, ti)]
        a, n = tl[ti]
        s = (i_base + i) % NX
        b = banks[0]
        P.op("vector", lambda e: e.tensor_tensor(out=xb[s][:, 0:n], in0=S.ps[b][:, 0:n], in1=xb[s][:, 0:n],
                                                 op=ALU.add),
             reads=[("ps", b), ("xb", s)], writes=[("xb", s)])
        P.dma("scalar", xout[gi * 128:(gi + 1) * 128, xout0 + a:xout0 + a + n], xb[s][:, 0:n], s_s[s],
              reads=[("xb", s)], writes=[("xd", tag, gi, ti)])
        if edge is not None and ti == 0:
            P.dma("scalar", edge[gi * 128:(gi + 1) * 128, 0:EW], xb[s][:, 0:EW], s_s[s], reads=[("xb", s)])
        if edge is not None and ti == len(tl) - 1:
            P.dma("scalar", edge[gi * 128:(gi + 1) * 128, EW:2 * EW], xb[s][:, n - EW:n], s_s[s], reads=[("xb", s)])

    run_linear(S, Wv, KCn, groups, rhs, [n for (_, n) in tl], epi, pre=pre, nslots=3, extra=extra)
    if own:
        S.close()


def stage_ffn_up(nc, w_up, convp, h2, g, fg, S=None):
    own = S is None
    S = S or Stage(nc)
    P = S.P
    ct = conv_tiles(T)
    Wv = w_up.rearrange("(kc p) m -> p kc m", p=128)
    groups = [[(FG_OFF[fg] + j) * 128, F + (FG_OFF[fg] + j) * 128] for j in range(FG_SIZES[fg])]
    NS = 3
    tmp = S.get("ffn_tmp", lambda: {nm: [S.sb([128, 412], F32, f"{nm}{i}") for i in range(NS)]
                                    for nm in ["A1", "B1"]})
    A1, B1 = tmp["A1"], tmp["B1"]
    cnt = S.get("ffn_cnt", lambda: [0])

    def rhs(kc, ti):
        a, n = ct[ti]
        return h2[:, kc, a:a + n + 2]

    def epi(gi, ti, banks, n2):
        a, n = ct[ti]
        ba, bb = banks
        s = cnt[0] % NS
        cnt[0] += 1
        ja = FG_OFF[fg] + gi
        jb = FC + FG_OFF[fg] + gi

        def taps(bank, j, X1, nm):
            w0 = convp[:, j, 0:1]
            w1 = convp[:, j, 1:2]
            w2 = convp[:, j, 2:3]
            bia = convp[:, j, 3:4]
            P.op("scalar", lambda e: e.activation(out=X1[s][:, 0:n], in_=S.ps[bank][:, 1:n + 1], func=AF.Identity,
                                                  bias=bia, scale=w1),
                 reads=[("ps", bank)], writes=[(nm, s)])
            P.op("vector", lambda e: e.scalar_tensor_tensor(out=X1[s][:, 0:n], in0=S.ps[bank][:, 0:n], scalar=w0,
                                                            in1=X1[s][:, 0:n], op0=ALU.mult, op1=ALU.add),
                 reads=[("ps", bank), (nm, s)], writes=[(nm, s)])
            P.op("vector", lambda e: e.scalar_tensor_tensor(out=X1[s][:, 0:n], in0=S.ps[bank][:, 2:n + 2], scalar=w2,
                                                            in1=X1[s][:, 0:n], op0=ALU.mult, op1=ALU.add),
                 reads=[("ps", bank), (nm, s)], writes=[(nm, s)])

        taps(ba, ja, A1, "A")
        taps(bb, jb, B1, "B")
        P.op("scalar", lambda e: e.activation(out=A1[s][:, 0:n], in_=A1[s][:, 0:n], func=AF.Silu),
             reads=[("A", s)], writes=[("A", s)])
        S.note("g", P.op("vector", lambda e: e.tensor_tensor(out=g[:, gi, a:a + n], in0=A1[s][:, 0:n],
                                                             in1=B1[s][:, 0:n], op=ALU.mult),
                         reads=[("A", s), ("B", s)], writes=[]))

    run_linear(S, Wv, KC, groups, rhs, [n + 2 for (_, n) in ct], epi, nslots=3)
    if own:
        S.close()


def ffn_layer(nc, per, w_up, w_down, convp, gcol, xin, xin_tok0, xout):
    h2 = per.enter_context(nc.sbuf_tensor(uid("h2"), [128, KC, T + 2], BF16))
    stage_norm(nc, xin, xin_tok0, T + 2, gcol, h=h2, hoff=0)
    g = per.enter_context(nc.sbuf_tensor(uid("g"), [128, FGC, T], BF16))
    Wd = w_down.rearrange("(fc p) m -> p fc m", p=128)
    S = Stage(nc)
    for fg in range(FGROUPS):
        stage_ffn_up(nc, w_up, convp, h2, g, fg, S=S)
        f0, fn = FG_OFF[fg], FG_SIZES[fg]
        if fg == 0:
            stage_proj_res(nc, Wd[:, f0:f0 + fn, :], fn, g, T, xin, xin_tok0 + 1, xout, 0,
                           S=S, extra=S.prod("g"))
        else:
            stage_proj_res(nc, Wd[:, f0:f0 + fn, :], fn, g, T, xout, 0, xout, 0,
                           S=S, extra=S.prod("g"))
    S.close()


def build_A():
    nc = bass.Bass("TRN2", target_bir_lowering=False)
    xin = nc.dram_tensor("xin", [D, T + 4], F32, kind="ExternalInput").ap()
    gmix = nc.dram_tensor("gmix", [128, KC], F32, kind="ExternalInput").ap()
    gffn = nc.dram_tensor("gffn", [128, KC], F32, kind="ExternalInput").ap()
    w_in = nc.dram_tensor("w_in", [D, 3 * D], F32, kind="ExternalInput").ap()
    w_out = nc.dram_tensor("w_out", [D, D], F32, kind="ExternalInput").ap()
    scp = nc.dram_tensor("scp", [128, KC, 4], F32, kind="ExternalInput").ap()
    w_up = nc.dram_tensor("w_up", [D, 2 * F], F32, kind="ExternalInput").ap()
    w_down = nc.dram_tensor("w_down", [F, D], F32, kind="ExternalInput").ap()
    ffp = nc.dram_tensor("ffp", [128, 2 * FC, 4], F32, kind="ExternalInput").ap()
    xa = nc.dram_tensor("xa", [D, T + 2], F32).ap()
    x1 = nc.dram_tensor("x1", [D, T], F32, kind="ExternalOutput").ap()
    with ExitStack() as st:
        consts = st.enter_context(nc.sbuf_tensor("consts", [128, 2 * KC + KC * 4 + 2 * FC * 4], F32))
        gm = consts[:, 0:KC]
        gf = consts[:, KC:2 * KC]
        sc = consts[:, 2 * KC:2 * KC + KC * 4].rearrange("p (j c) -> p j c", c=4)
        fp = consts[:, 2 * KC + KC * 4:].rearrange("p (j c) -> p j c", c=4)
        S = Stage(nc)
        sm = S.P.sem("s_c")
        S.P.dma("sync", gm, gmix, sm)
        S.P.dma("sync", gf, gffn, sm)
        S.P.dma("sync", sc, scp, sm)
        S.P.dma("sync", fp, ffp, sm)
        S.close()
        with ExitStack() as st2:
            h = st2.enter_context(nc.sbuf_tensor("h", [128, KC, T + 4], BF16))
            stage_norm(nc, xin, 0, T + 4, gm, h=h, hoff=0)
            y = st2.enter_context(nc.sbuf_tensor("y", [128, KC, T + 2], BF16))
            SA = Stage(nc)
            stage_sconv_in(nc, w_in, sc, h, y, S=SA)
            stage_proj_res(nc, w_out.rearrange("(kc p) m -> p kc m", p=128), KC, y, T + 2, xin, 1, xa, 0, tn=410,
                           S=SA, extra=SA.prod("y"))
            SA.close()
        with ExitStack() as st2:
            ffn_layer(nc, st2, w_up, w_down, fp, gf, xa, 0, x1)
    release_sems(nc)
    return nc


GH = [64 * d for d in DILS]
TK = [T + 2 * h for h in GH]
NQB = [T // (128 * d) for d in DILS]
NCH = [(NQB[g] + 1) * DILS[g] for g in range(NG)]
CH_OFF = [0, NCH[0], NCH[0] + NCH[1]]
NCH_TOT = sum(NCH)
QKV_G = 3 * D
ATT_SCALE = 128 ** -0.5
NEG = -30000.0
POOL_EVERY = 2


def ss(start, count, step):
    return slice(start, start + step * (count - 1) + 1, step)


def stage_qk(nc, w_qkv, hh, g, which, lo, ntok, dst, dst0, S=None, gap=None):
    own = S is None
    S = S or Stage(nc)
    P = S.P
    Wv = w_qkv.rearrange("(kc p) m -> p kc m", p=128)
    base = g * QKV_G + which * D
    groups = [[base + (2 * j) * 128, base + (2 * j + 1) * 128] for j in range(KC // 2)]
    tl = conv_tiles(ntok, 512)
    NST = 4
    stg = S.get("stg", lambda: [S.sb([128, 2560], BF16, f"stg{i}") for i in range(NST)])
    s_st = [P.sem(f"s_st{i}") for i in range(NST)]
    cnt = S.get("qk_cnt", lambda: [0])
    cbase = S.get("qk_chunk", lambda: [0])
    chunk0 = cbase[0]
    cbase[0] += KC

    def rhs(kc, ti):
        a, n = tl[ti]
        return hh[:, kc, lo + a:lo + a + n]

    def epi(gi, ti, banks, n):
        a, n = tl[ti]
        for ci, b in enumerate(banks):
            chunk = 2 * gi + ci
            sl = (chunk0 + chunk) % NST
            cnt[0] += 1
            if cnt[0] % 2 == 0:
                P.op("scalar", lambda e, sl=sl, b=b: e.activation(out=stg[sl][:, a:a + n], in_=S.ps[b][:, 0:n], func=AF.Copy),
                     reads=[("ps", b)], writes=[("stg", sl, ti)])
            else:
                P.op("vector", lambda e, sl=sl, b=b: e.tensor_copy(out=stg[sl][:, a:a + n], in_=S.ps[b][:, 0:n]),
                     reads=[("ps", b)], writes=[("stg", sl, ti)])
            if ti == len(tl) - 1:
                rd = [("stg", sl, t2) for t2 in range(len(tl))]
                if gap is None:
                    P.dma("sync", dst[chunk * 128:(chunk + 1) * 128, dst0:dst0 + ntok], stg[sl][:, 0:ntok], s_st[sl],
                          reads=rd)
                else:
                    gp, gw = gap
                    P.dma("sync", dst[chunk * 128:(chunk + 1) * 128, dst0:dst0 + gp], stg[sl][:, 0:gp], s_st[sl],
                          reads=rd)
                    P.dma("sync", dst[chunk * 128:(chunk + 1) * 128, dst0 + gp + gw:dst0 + gw + ntok],
                          stg[sl][:, gp:ntok], s_st[sl], reads=rd)

    run_linear(S, Wv, KC, groups, rhs, [n for (_, n) in tl], epi, nslots=3, wsize=8192)
    if own:
        S.close()


def stage_v(nc, w_qkv, hh, g, lo, ntok, dstV, row0, S=None, gap=None):
    own = S is None
    S = S or Stage(nc)
    P = S.P
    Wv = w_qkv.rearrange("(kc p) m -> p kc m", p=128)
    base = g * QKV_G + 2 * D
    pool = S.wpool(3, 8192)
    nsl = len(pool["buf"])
    NST = 4
    vst = S.get("vst", lambda: [S.sb([128, 512], BF16, f"vst{i}") for i in range(NST)])
    s_vs = [P.sem(f"s_vs{i}") for i in range(NST)]
    blocks = conv_tiles(ntok, 128 if gap is None else min(128, gap[0]))
    cntl = S.get("v_cnt", lambda: [0])
    for sl in range(4):
        slot = pool["n"] % nsl
        pool["n"] += 1
        wvs = pool["buf"][slot][:, 0:KC * 512].rearrange("p (k c) -> p k c", c=512)
        c0 = base + sl * 512
        P.dma("gpsimd", wvs, Wv[:, :, c0:c0 + 512], pool["sem"][slot], writes=[("wb", slot)])
        for (a, m) in blocks:
            b = S.bank()
            P._deps("tensor", [("wb", slot)], [("ps", b)], [])
            for kc in range(KC):
                fn = (lambda e, b=b, wvs=wvs, kc=kc, a=a, m=m: e.matmul(
                    S.ps[b][0:m, :], hh[:, kc, lo + a:lo + a + m], wvs[:, kc, :],
                    start=(kc == 0), stop=(kc == KC - 1)))
                if kc == KC - 1:
                    P.op("tensor", fn, reads=[("wb", slot)], writes=[("ps", b)])
                else:
                    P.op("tensor", fn, signal=False)
            cntl[0] += 1
            cnt = cntl[0]
            st = cnt % NST
            if cnt % 2 == 0:
                P.op("scalar", lambda e, st=st, b=b, m=m: e.activation(out=vst[st][0:m, :], in_=S.ps[b][0:m, :], func=AF.Copy),
                     reads=[("ps", b)], writes=[("vst", st)])
            else:
                P.op("vector", lambda e, st=st, b=b, m=m: e.tensor_copy(out=vst[st][0:m, :], in_=S.ps[b][0:m, :]),
                     reads=[("ps", b)], writes=[("vst", st)])
            ra = row0 + a + (gap[1] if (gap is not None and a >= gap[0]) else 0)
            P.dma("sync", dstV[ra:ra + m, sl * 512:(sl + 1) * 512], vst[st][0:m, :], s_vs[st],
                  reads=[("vst", st)])
    if own:
        S.close()


def stage_attn(nc, Qs, Ks, Vs, etab, kbias_d, y):
    S = Stage(nc)
    P = S.P
    NSL = 2
    Qh = [S.sb([128, NG, T], BF16, f"Qh{i}") for i in range(NSL)]
    Kh = [[S.sb([128, TK[g]], BF16, f"Kh{i}_{g}") for g in range(NG)] for i in range(NSL)]
    Vh = [[S.sb([128, NQB[g] + 1, DILS[g], 128], BF16, f"Vh{i}_{g}") for g in range(NG)] for i in range(NSL)]
    Eh = [S.sb([128, NG, 256], F32, f"Eh{i}") for i in range(NSL)]
    kb = S.sb([128, NCH_TOT], F32, "kb")
    ones = S.sb([128, 128], BF16, "ones")
    num = S.sb([128, T], F32, "num")
    den = S.sb([128, T], F32, "den")
    NPT = 7
    pt = [S.sb([128, 256], F32, f"pt{i}") for i in range(NPT)]
    pb = [S.sb([128, 256], BF16, f"pb{i}") for i in range(NPT)]
    s_ld = [P.sem(f"s_ld{i}") for i in range(NSL)]
    s_kb = P.sem("s_kb")
    P.dma("sync", kb[:, :], kbias_d, s_kb, writes=["kb"])
    P.op("vector", lambda e: e.memset(ones[:, :], 1.0), writes=["ones"])
    Vviews = [Vs[g].rearrange("(i p r) c -> p i r c", p=128, r=DILS[g]) for g in range(NG)]

    pending = []

    def load_head(hd):
        s = hd % NSL
        rows = slice(hd * 128, (hd + 1) * 128)
        jobs = []
        for g in range(NG):
            jobs.append((Qh[s][:, g, :], Qs[g][rows, :]))
            jobs.append((Kh[s][g][:, :], Ks[g][rows, :]))
            for i in range(NQB[g] + 1):
                jobs.append((Vh[s][g][:, i, :, :], Vviews[g][:, i, :, rows]))
        jobs.append((Eh[s][:, :, :], etab[:, hd, :, :]))
        for j, (o, i_) in enumerate(jobs):
            pending.append((s, o, i_, j == 0, j == len(jobs) - 1))

    recent = []

    def issue_loads(n):
        for _ in range(n):
            if not pending:
                return
            s, o, i_, first, last = pending.pop(0)
            if len(recent) >= 6:
                P.wait("sync", recent.pop(0))
            tok = P.dma("sync", o, i_, s_ld[s], writes=([("L", s)] if first else []))
            recent.append(tok)
            if last:
                P.last_w[("L", s)] = tok

    units = []
    for hd in range(NH):
        for g in range(NG):
            d = DILS[g]
            for r in range(d):
                for i in range(NQB[g] + 1):
                    units.append((hd, g, r, i))
    nU = len(units)
    LA = 3
    sb_n = [0]
    acc_n = [0]
    acc_of = {}
    sinfo = {}
    state = {"evac_prev": [], "evac_cur": [], "norm": []}

    def s_phase(u):
        hd, g, r, i = units[u]
        s = hd % NSL
        d = DILS[g]
        lo = max(i - 1, 0)
        hi = min(i, NQB[g] - 1)
        N = 128 * (hi - lo + 1)
        bs = sb_n[0] % 4
        sb_n[0] += 1
        k = u % NPT
        kslice = ss(128 * d * i + r, 128, d)
        qslice = ss(128 * d * lo + r, N, d)
        c = CH_OFF[g] + i * d + r
        e0 = 128 if i == 0 else 0
        P.op("tensor", lambda e: e.matmul(S.ps[bs][:, 0:N], Kh[s][g][:, kslice], Qh[s][:, g, qslice],
                                          start=True, stop=True),
             reads=[("L", s)], writes=[("ps", bs)])
        P.op("scalar", lambda e: e.activation(out=pt[k][:, 0:N], in_=S.ps[bs][:, 0:N], func=AF.Exp,
                                              bias=kb[:, c:c + 1], scale=ATT_SCALE),
             reads=[("ps", bs), "kb"], writes=[("pt", k)])
        P.op("vector" if (u % POOL_EVERY) != 0 else "gpsimd",
             lambda e: e.tensor_tensor(out=pb[k][:, 0:N], in0=pt[k][:, 0:N],
                                       in1=Eh[s][:, g, e0:e0 + N], op=ALU.mult),
             reads=[("pt", k), ("L", s)], writes=[("pb", k)])
        sinfo[u] = (lo, hi, k)

    def pv_phase(u):
        hd, g, r, i = units[u]
        s = hd % NSL
        d = DILS[g]
        lo, hi, k = sinfo.pop(u)
        done_blocks = []
        nblk = hi - lo + 1
        for bi, blk in enumerate(range(lo, hi + 1)):
            first = (i == blk)
            last = (i == blk + 1)
            if first:
                acc_of[(hd, g, r, blk)] = acc_n[0] % 2
                acc_n[0] += 1
            par = acc_of[(hd, g, r, blk)]
            bo = 4 + par
            bd = 6 + par
            cols = slice(128 * bi, 128 * (bi + 1))
            if first:
                P._deps("tensor", [], [("ps", bo), ("ps", bd)], [])
            fo = (lambda e, bo=bo, cols=cols, first=first, last=last: e.matmul(
                S.ps[bo][:, 0:128], Vh[s][g][:, i, r, :], pb[k][:, cols], start=first, stop=last))
            fd = (lambda e, bd=bd, cols=cols, first=first, last=last: e.matmul(
                S.ps[bd][:, 0:128], ones[:, :], pb[k][:, cols], start=first, stop=last))
            final = (bi == nblk - 1)
            P.op("tensor", fo, reads=[("pb", k), ("L", s)], signal=False)
            P.op("tensor", fd, reads=[("pb", k), "ones", ("L", s)],
                 writes=([("ps", bo), ("ps", bd)] if last else []), signal=(last or final))
            if last:
                done_blocks.append((blk, bo, bd))
        for (blk, bo, bd) in done_blocks:
            del acc_of[(hd, g, r, blk)]
            tsl = ss(128 * d * blk + r, 128, d)
            if g == 0:
                extra = state["norm"]
                t1 = P.op("scalar", lambda e, bo=bo, tsl=tsl: e.activation(out=num[:, tsl], in_=S.ps[bo][:, 0:128], func=AF.Copy),
                          reads=[("ps", bo)], extra=extra)
                t2 = P.op("vector", lambda e, bd=bd, tsl=tsl: e.tensor_copy(out=den[:, tsl], in_=S.ps[bd][:, 0:128]),
                          reads=[("ps", bd)], extra=extra)
            else:
                extra = state["evac_prev"]
                t1 = P.op("vector", lambda e, bo=bo, tsl=tsl: e.tensor_tensor(out=num[:, tsl], in0=S.ps[bo][:, 0:128],
                                                                               in1=num[:, tsl], op=ALU.add),
                          reads=[("ps", bo)], extra=extra)
                t2 = P.op("vector", lambda e, bd=bd, tsl=tsl: e.tensor_tensor(out=den[:, tsl], in0=S.ps[bd][:, 0:128],
                                                                               in1=den[:, tsl], op=ALU.add),
                          reads=[("ps", bd)], extra=extra)
            state["evac_cur"] = [t1, t2]
        if r == d - 1 and i == NQB[g]:
            state["evac_prev"] = list(state["evac_cur"])
            if g == NG - 1:
                t3 = P.op("vector", lambda e: e.reciprocal(out=den[:, :], in_=den[:, :]), extra=state["evac_prev"])
                t4 = P.op("vector", lambda e, hd=hd: e.tensor_tensor(out=y[:, hd, :], in0=num[:, :], in1=den[:, :],
                                                                       op=ALU.mult), extra=[t3] + state["evac_prev"])
                state["norm"] = [t4]

    load_head(0)
    issue_loads(1000)
    load_head(1)
    for u in range(nU + LA):
        issue_loads(2)
        if u < nU:
            s_phase(u)
        v = u - LA
        if v >= 0:
            pv_phase(v)
            hdv = units[v][0]
            if (v == nU - 1 or units[v + 1][0] != hdv) and hdv + 2 < NH:
                load_head(hdv + 2)
    S.close()


def build_B():
    nc = bass.Bass("TRN2", target_bir_lowering=False)
    x1e = nc.dram_tensor("x1e", [D, T + 2 * HALO], F32, kind="ExternalInput").ap()
    gmix = nc.dram_tensor("gmix", [128, KC], F32, kind="ExternalInput").ap()
    w_qkv = nc.dram_tensor("w_qkv", [D, NG * QKV_G], F32, kind="ExternalInput").ap()
    w_o = nc.dram_tensor("w_o", [D, D], F32, kind="ExternalInput").ap()
    etab = nc.dram_tensor("etab", [128, NH, NG, 256], F32, kind="ExternalInput").ap()
    kbias = nc.dram_tensor("kbias", [128, NCH_TOT], F32, kind="ExternalInput").ap()
    x1p = nc.dram_tensor("x1p", [D, T], F32, kind="ExternalOutput").ap()
    Qs = [nc.dram_tensor(f"Qs{g}", [D, T], BF16).ap() for g in range(NG)]
    Ks = [nc.dram_tensor(f"Ks{g}", [D, TK[g]], BF16).ap() for g in range(NG)]
    Vs = [nc.dram_tensor(f"Vs{g}", [TK[g], D], BF16).ap() for g in range(NG)]
    with ExitStack() as st:
        consts = st.enter_context(nc.sbuf_tensor("consts", [128, KC], F32))
        gm = consts[:, 0:KC]
        S = Stage(nc)
        S.P.dma("sync", gm, gmix, S.P.sem("s_c"))
        S.close()
        with ExitStack() as st2:
            hh = st2.enter_context(nc.sbuf_tensor("hh", [128, KC, 2048], BF16))
            for hf in range(2):
                stage_norm(nc, x1e, 2048 * hf, 2048, gm, h=hh, hoff=0)
                qlo = HALO if hf == 0 else 0
                SQ = Stage(nc)
                SQ.wpool(3, 8192)
                for g in range(NG):
                    klo_ext = HALO - GH[g]
                    khi_ext = HALO + T + GH[g]
                    lo_ext = max(klo_ext, 2048 * hf)
                    hi_ext = min(khi_ext, 2048 * (hf + 1))
                    n = hi_ext - lo_ext
                    stage_qk(nc, w_qkv, hh, g, 0, qlo, T // 2, Qs[g], (T // 2) * hf, S=SQ)
                    stage_qk(nc, w_qkv, hh, g, 1, lo_ext - 2048 * hf, n, Ks[g], lo_ext - klo_ext, S=SQ)
                    stage_v(nc, w_qkv, hh, g, lo_ext - 2048 * hf, n, Vs[g], lo_ext - klo_ext, S=SQ)
                SQ.close()
        with ExitStack() as st2:
            y = st2.enter_context(nc.sbuf_tensor("yatt", [128, NH, T], BF16))
            stage_attn(nc, Qs, Ks, Vs, etab, kbias, y)
            stage_proj_res(nc, w_o.rearrange("(kc p) m -> p kc m", p=128), KC, y, T, x1e, HALO, x1p, 0)
    release_sems(nc)
    return nc


def build_C():
    nc = bass.Bass("TRN2", target_bir_lowering=False)
    xin = nc.dram_tensor("xin", [D, T + 2], F32, kind="ExternalInput").ap()
    gffn = nc.dram_tensor("gffn", [128, KC], F32, kind="ExternalInput").ap()
    gfin = nc.dram_tensor("gfin", [128, KC], F32, kind="ExternalInput").ap()
    w_up = nc.dram_tensor("w_up", [D, 2 * F], F32, kind="ExternalInput").ap()
    w_down = nc.dram_tensor("w_down", [F, D], F32, kind="ExternalInput").ap()
    ffp = nc.dram_tensor("ffp", [128, 2 * FC, 4], F32, kind="ExternalInput").ap()
    x2 = nc.dram_tensor("x2", [D, T], F32).ap()
    outT = nc.dram_tensor("outT", [D, T], F32, kind="ExternalOutput").ap()
    with ExitStack() as st:
        consts = st.enter_context(nc.sbuf_tensor("consts", [128, 2 * KC + 2 * FC * 4], F32))
        gf = consts[:, 0:KC]
        gl = consts[:, KC:2 * KC]
        fp = consts[:, 2 * KC:].rearrange("p (j c) -> p j c", c=4)
        S = Stage(nc)
        sm = S.P.sem("s_c")
        S.P.dma("sync", gf, gffn, sm)
        S.P.dma("sync", gl, gfin, sm)
        S.P.dma("sync", fp, ffp, sm)
        S.close()
        with ExitStack() as st2:
            ffn_layer(nc, st2, w_up, w_down, fp, gf, xin, 0, x2)
        stage_norm(nc, x2, 0, T, gl, outT=outT, out0=0)
    release_sems(nc)
    return nc


NCONST = 3 * KC + 2 * KC + KC * 4 + 2 * (2 * FC * 4) + 2
CC_GROUPS = [[0, 1, 2, 3], [4, 5, 6, 7]]
CCN = 4
PW = 128
NPIECE = T // PW
HPW = 256
HNP = T // HPW
SERIAL_CC = False
EW = 16


_NO_CC = False


def stage_allgather(nc, src, dst):
    if _NO_CC:
        return
    S = Stage(nc)
    cs = get_sem(nc, "s_cc")
    cs.n += 1
    v = cs.n
    S.P.ops["gpsimd"].append(lambda g: g.collective_compute(
        "AllGather", ALU.bypass, replica_groups=CC_GROUPS, ins=[src], outs=[dst]).then_inc(cs.h))
    S.P.ops["gpsimd"].append(lambda g: g.wait_ge(cs.h, v))
    S.close()


def stage_exchange_pieces(nc, x1own, xp, G):
    S = Stage(nc)
    P = S.P
    s_rp = [P.sem(f"s_rp{i}") for i in range(4)]
    toks = []
    for q in range(NPIECE):
        toks.append(P.dma("sync", xp[q], x1own[:, q * PW:(q + 1) * PW], s_rp[q % 4]))
        if q >= 3:
            P.wait("sync", toks[q - 3])
    S.close()
    S = Stage(nc)
    cs = get_sem(nc, "s_cc")
    for q in range(NPIECE):
        cs.n += 1
        S.P.ops["gpsimd"].append(lambda g, q=q: g.collective_compute(
            "AllGather", ALU.bypass, replica_groups=CC_GROUPS, ins=[xp[q]], outs=[G[q]]).then_inc(cs.h))
        if SERIAL_CC:
            v = cs.n
            S.P.ops["gpsimd"].append(lambda g, v=v: g.wait_ge(cs.h, v))
    S.close()
    return (cs, cs.n)


def stage_exchange_h(nc, hh, hp, Gh, hoff=0):
    S = Stage(nc)
    P = S.P
    s_hp = [P.sem(f"s_rp{i}") for i in range(4)]
    toks = []
    for q in range(HNP):
        toks.append(P.dma("sync", hp[q].rearrange("(kc p) t -> p kc t", p=128),
                          hh[:, :, hoff + q * HPW:hoff + (q + 1) * HPW], s_hp[q % 4]))
        if q >= 3:
            P.wait("sync", toks[q - 3])
    S.close()
    S = Stage(nc)
    cs = get_sem(nc, "s_cc")
    for q in range(HNP):
        if _NO_CC:
            break
        cs.n += 1
        S.P.ops["gpsimd"].append(lambda g, q=q: g.collective_compute(
            "AllGather", ALU.bypass, replica_groups=CC_GROUPS, ins=[hp[q]], outs=[Gh[q]]).then_inc(cs.h))
    S.close()
    return (cs, cs.n)


def stage_load_h(nc, hh, Gh, cc_tok, jobs):
    S = Stage(nc)
    P = S.P
    s_hl = [P.sem(f"s_rp{i}") for i in range(4)]
    P.wait("sync", cc_tok)
    toks = []
    for n, (c0, q, sh) in enumerate(jobs):
        def f(e, q=q, sh=sh):
            r = (e.partition_id() + sh) % 4
            return Gh[q][bass.ds(r * D, D), :].rearrange("(kc p) t -> p kc t", p=128)
        toks.append(P.dma("sync", hh[:, :, c0:c0 + HPW], f, s_hl[n % 4]))
        if n >= 3:
            P.wait("sync", toks[n - 3])
    S.close()


def stage_load_halo_h(nc, hh, Gh, cc_tok):
    S = Stage(nc)
    P = S.P
    s_hl = [P.sem(f"s_rp{i}") for i in range(4)]
    P.wait("sync", cc_tok)
    toks = []
    n = 0
    for side in range(2):
        for j in range(HALO // HPW):
            q = (T - HALO) // HPW + j if side == 0 else j
            sh = 3 if side == 0 else 1
            c0 = side * HALO + j * HPW

            def f(e, q=q, sh=sh):
                r = (e.partition_id() + sh) % 4
                return Gh[q][bass.ds(r * D, D), :].rearrange("(kc p) t -> p kc t", p=128)
            toks.append(P.dma("sync", hh[:, :, c0:c0 + HPW], f, s_hl[n % 4]))
            if n >= 3:
                P.wait("sync", toks[n - 3])
            n += 1
    S.close()


def stage_edges(nc, edge_all, xC, vmask):
    S = Stage(nc)
    P = S.P
    ec = S.sb([128, 2, KC, EW], F32, "ec")
    s_e = P.sem("s_e")
    s_e2 = P.sem("s_e2")

    def srcf(side):
        def f(e):
            pid = e.partition_id()
            r = (pid + 3) % 4 if side == 0 else (pid + 1) % 4
            c0 = EW if side == 0 else 0
            return edge_all[bass.ds(r * D, D), c0:c0 + EW].rearrange("(kc p) c -> p kc c", p=128)
        return f

    P.dma("sync", ec[:, 0, :, :], srcf(0), s_e, writes=[("ec", 0)])
    P.dma("sync", ec[:, 1, :, :], srcf(1), s_e2, writes=[("ec", 1)])
    P.op("vector", lambda e: e.tensor_scalar(out=ec[:, 0, :, :], in0=ec[:, 0, :, :], scalar1=vmask[:, 0:1], scalar2=None,
                                             op0=ALU.mult), reads=[("ec", 0)], writes=[("ec", 0)])
    P.op("vector", lambda e: e.tensor_scalar(out=ec[:, 1, :, :], in0=ec[:, 1, :, :], scalar1=vmask[:, 1:2], scalar2=None,
                                             op0=ALU.mult), reads=[("ec", 1)], writes=[("ec", 1)])
    xv = xC.rearrange("(kc p) t -> p kc t", p=128)
    P.dma("sync", xv[:, :, 0:EW], ec[:, 0, :, :], s_e, reads=[("ec", 0)])
    P.dma("sync", xv[:, :, EW + T:EW + T + EW], ec[:, 1, :, :], s_e2, reads=[("ec", 1)])
    S.close()


def build_fused():
    nc = bass.Bass("TRN2", target_bir_lowering=False)
    EI = "ExternalInput"
    xin = nc.dram_tensor("xin", [D, T + 4], F32, kind=EI).ap()
    cst = nc.dram_tensor("cst", [128, NCONST], F32, kind=EI).ap()
    w_in = nc.dram_tensor("w_in", [D, 3 * D], F32, kind=EI).ap()
    w_out = nc.dram_tensor("w_out", [D, D], F32, kind=EI).ap()
    w_up0 = nc.dram_tensor("w_up0", [D, 2 * F], F32, kind=EI).ap()
    w_down0 = nc.dram_tensor("w_down0", [F, D], F32, kind=EI).ap()
    w_qkv = nc.dram_tensor("w_qkv", [D, NG * QKV_G], F32, kind=EI).ap()
    w_o = nc.dram_tensor("w_o", [D, D], F32, kind=EI).ap()
    w_up1 = nc.dram_tensor("w_up1", [D, 2 * F], F32, kind=EI).ap()
    w_down1 = nc.dram_tensor("w_down1", [F, D], F32, kind=EI).ap()
    etab = nc.dram_tensor("etab", [128, NH, NG, 256], F32, kind=EI).ap()
    kbias = nc.dram_tensor("kbias", [128, NCH_TOT], F32, kind=EI).ap()
    outT = nc.dram_tensor("outT", [D, T], F32, kind="ExternalOutput").ap()
    xa = nc.dram_tensor("xa", [D, T + 2], F32).ap()
    x1own = nc.dram_tensor("x1own", [D, T], F32).ap()
    hp = [nc.dram_tensor(f"hp{q}", [D, HPW], BF16).ap() for q in range(HNP)]
    Gh = [nc.dram_tensor(f"Gh{q}", [CCN * D, HPW], BF16).ap() for q in range(HNP)]
    Qs = [nc.dram_tensor(f"Qs{g}", [D, T], BF16).ap() for g in range(NG)]
    Ks = [nc.dram_tensor(f"Ks{g}", [D, TK[g]], BF16).ap() for g in range(NG)]
    Vs = [nc.dram_tensor(f"Vs{g}", [TK[g], D], BF16).ap() for g in range(NG)]
    xC = nc.dram_tensor("xC", [D, T + 2 * EW], F32).ap()
    edge_in = nc.dram_tensor("edge_in", [D, 2 * EW], F32).ap()
    edge_all = nc.dram_tensor("edge_all", [CCN * D, 2 * EW], F32).ap()
    x2 = nc.dram_tensor("x2", [D, T], F32).ap()
    nc.cache_partition_id()
    with ExitStack() as st:
        consts = st.enter_context(nc.sbuf_tensor("consts", [128, NCONST], F32))
        o = 0
        gm0 = consts[:, o:o + KC]; o += KC
        gf0 = consts[:, o:o + KC]; o += KC
        gm1 = consts[:, o:o + KC]; o += KC
        gf1 = consts[:, o:o + KC]; o += KC
        gfin = consts[:, o:o + KC]; o += KC
        sc = consts[:, o:o + KC * 4].rearrange("p (j c) -> p j c", c=4); o += KC * 4
        fp0 = consts[:, o:o + 2 * FC * 4].rearrange("p (j c) -> p j c", c=4); o += 2 * FC * 4
        fp1 = consts[:, o:o + 2 * FC * 4].rearrange("p (j c) -> p j c", c=4); o += 2 * FC * 4
        vmask = consts[:, o:o + 2]; o += 2
        assert o == NCONST
        S = Stage(nc)
        S.P.dma("sync", consts[:, :], cst, S.P.sem("s_c"))
        S.close()
        with ExitStack() as st2:
            h = st2.enter_context(nc.sbuf_tensor("h", [128, KC, T + 4], BF16))
            stage_norm(nc, xin, 0, T + 4, gm0, h=h, hoff=0)
            y = st2.enter_context(nc.sbuf_tensor("y", [128, KC, T + 2], BF16))
            SA = Stage(nc)
            stage_sconv_in(nc, w_in, sc, h, y, S=SA)
            stage_proj_res(nc, w_out.rearrange("(kc p) m -> p kc m", p=128), KC, y, T + 2, xin, 1, xa, 0, tn=410,
                           S=SA, extra=SA.prod("y"))
            SA.close()
        with ExitStack() as st2:
            ffn_layer(nc, st2, w_up0, w_down0, fp0, gf0, xa, 0, x1own)
        NEAR = HPW
        with ExitStack() as st2:
            hh = st2.enter_context(nc.sbuf_tensor("hh", [128, KC, T + 2 * NEAR], BF16))
            stage_norm(nc, x1own, 0, T, gm1, h=hh, hoff=NEAR)
            cc_tok = stage_exchange_h(nc, hh, hp, Gh, hoff=NEAR)
            SQ = Stage(nc)
            SQ.wpool(3, 8192)
            for g in range(NG):
                stage_qk(nc, w_qkv, hh, g, 0, NEAR, T, Qs[g], 0, S=SQ)
            stage_qk(nc, w_qkv, hh, 2, 1, NEAR, T, Ks[2], GH[2], S=SQ)
            stage_v(nc, w_qkv, hh, 2, NEAR, T, Vs[2], GH[2], S=SQ)
            SQ.close()
            stage_load_h(nc, hh, Gh, cc_tok, [(0, HNP - 1, 3), (NEAR + T, 0, 1)])
            SQ = Stage(nc)
            SQ.wpool(3, 8192)
            for g in range(2):
                gh = GH[g]
                stage_qk(nc, w_qkv, hh, g, 1, NEAR - gh, T + 2 * gh, Ks[g], 0, S=SQ)
                stage_v(nc, w_qkv, hh, g, NEAR - gh, T + 2 * gh, Vs[g], 0, S=SQ)
            SQ.close()
            jobs = [(j * HPW, HNP - HALO // HPW + j, 3) for j in range(HALO // HPW)] + \
                   [(HALO + j * HPW, j, 1) for j in range(HALO // HPW)]
            stage_load_h(nc, hh, Gh, cc_tok, jobs)
            SQ = Stage(nc)
            SQ.wpool(3, 8192)
            stage_qk(nc, w_qkv, hh, 2, 1, 0, 2 * HALO, Ks[2], 0, S=SQ, gap=(HALO, T))
            stage_v(nc, w_qkv, hh, 2, 0, 2 * HALO, Vs[2], 0, S=SQ, gap=(HALO, T))
            SQ.close()
        with ExitStack() as st2:
            yat = st2.enter_context(nc.sbuf_tensor("yatt", [128, NH, T], BF16))
            stage_attn(nc, Qs, Ks, Vs, etab, kbias, yat)
            stage_proj_res(nc, w_o.rearrange("(kc p) m -> p kc m", p=128), KC, yat, T, x1own, 0, xC, EW, edge=edge_in)
        stage_allgather(nc, edge_in[:, :], edge_all[:, :])
        stage_edges(nc, edge_all, xC, vmask)
        with ExitStack() as st2:
            ffn_layer(nc, st2, w_up1, w_down1, fp1, gf1, xC, EW - 1, x2)
        stage_norm(nc, x2, 0, T, gfin, outT=outT, out0=0)
    release_sems(nc)
    return nc


def col_layout(v):
    return np.ascontiguousarray(v.reshape(-1, 128).T.astype(np.float32))


def conv_layout(w, b):
    C = w.shape[1] // 128
    out = np.empty((128, C, 4), np.float32)
    for k in range(3):
        out[:, :, k] = w[k].reshape(C, 128).T
    out[:, :, 3] = b.reshape(C, 128).T
    return out


def core_tokens(c):
    b = c // 4
    a = (c % 4) * T
    return b, a


def padded_slice_T(x, b, lo, hi):
    out = np.zeros((x.shape[2], hi - lo), np.float32)
    l2, h2 = max(lo, 0), min(hi, x.shape[1])
    out[:, l2 - lo:h2 - lo] = x[b, l2:h2, :].T
    return out


def alibi_tables():
    slopes = (2.0 ** (-8.0 * np.arange(1, NH + 1) / NH)).astype(np.float64)
    kp = np.arange(128)[:, None]
    qf = np.arange(128)[None, :]
    dB = qf - kp - 64
    dA = qf - kp + 64
    out = np.zeros((128, NH, NG, 256), np.float32)
    for h in range(NH):
        for g in range(NG):
            d = DILS[g]
            eb = np.where(np.abs(dB) <= 64, np.exp(-slopes[h] * d * np.abs(dB)), 0.0)
            ea = np.where(np.abs(dA) <= 64, np.exp(-slopes[h] * d * np.abs(dA)), 0.0)
            out[:, h, g, 0:128] = eb
            out[:, h, g, 128:256] = ea
    return out


def key_bias(a):
    out = np.zeros((128, NCH_TOT), np.float32)
    p = np.arange(128)
    for g in range(NG):
        d = DILS[g]
        for i in range(NQB[g] + 1):
            for r in range(d):
                k = 128 * d * i + d * p + r
                pos = a + k - GH[g]
                out[:, CH_OFF[g] + i * d + r] = np.where((pos >= 0) & (pos < SEQ), 0.0, NEG)
    return out


_NC_CACHE = {}


def get_nc(name, fn):
    if name not in _NC_CACHE:
        _NC_CACHE[name] = fn()
    return _NC_CACHE[name]


def run_A(inp):
    x = inp["x"]
    nc = get_nc("A", build_A)
    in_maps = []
    shared = {
        "gmix": col_layout(inp["mix_norm_g"][0]),
        "gffn": col_layout(inp["ffn_norm_g"][0]),
        "w_in": np.ascontiguousarray(inp["sc_w_in"][0]),
        "w_out": np.ascontiguousarray(inp["sc_w_out"][0]),
        "scp": conv_layout(inp["sc_conv_w"][0], inp["sc_conv_b"][0]),
        "w_up": np.ascontiguousarray(inp["ffn_w_up"][0]),
        "w_down": np.ascontiguousarray(inp["ffn_w_down"][0]),
        "ffp": conv_layout(inp["ffn_conv_w"][0], inp["ffn_conv_b"][0]),
    }
    for c in range(NCORES):
        b, a = core_tokens(c)
        m = dict(shared)
        m["xin"] = padded_slice_T(x, b, a - 2, a + T + 2)
        in_maps.append(m)
    res = run_bass_kernel_spmd(nc, in_maps, core_ids=list(range(NCORES)))
    x1 = np.empty_like(x)
    for c in range(NCORES):
        b, a = core_tokens(c)
        x1[b, a:a + T, :] = res.results[c]["x1"].T
    return x1


def run_B(inp, x1):
    nc = get_nc("B", build_B)
    shared = {
        "gmix": col_layout(inp["mix_norm_g"][1]),
        "w_qkv": np.ascontiguousarray(inp["attn_w_qkv"][0]),
        "w_o": np.ascontiguousarray(inp["attn_w_out"][0]),
        "etab": alibi_tables(),
    }
    in_maps = []
    for c in range(NCORES):
        b, a = core_tokens(c)
        m = dict(shared)
        m["x1e"] = padded_slice_T(x1, b, a - HALO, a + T + HALO)
        m["kbias"] = key_bias(a)
        in_maps.append(m)
    res = run_bass_kernel_spmd(nc, in_maps, core_ids=list(range(NCORES)))
    x1p = np.empty_like(x1)
    for c in range(NCORES):
        b, a = core_tokens(c)
        x1p[b, a:a + T, :] = res.results[c]["x1p"].T
    return x1p


def run_C(inp, x1p):
    nc = get_nc("C", build_C)
    shared = {
        "gffn": col_layout(inp["ffn_norm_g"][1]),
        "gfin": col_layout(inp["final_norm_g"]),
        "w_up": np.ascontiguousarray(inp["ffn_w_up"][1]),
        "w_down": np.ascontiguousarray(inp["ffn_w_down"][1]),
        "ffp": conv_layout(inp["ffn_conv_w"][1], inp["ffn_conv_b"][1]),
    }
    in_maps = []
    for c in range(NCORES):
        b, a = core_tokens(c)
        m = dict(shared)
        m["xin"] = padded_slice_T(x1p, b, a - 1, a + T + 1)
        in_maps.append(m)
    res = run_bass_kernel_spmd(nc, in_maps, core_ids=list(range(NCORES)))
    out = np.empty_like(x1p)
    for c in range(NCORES):
        b, a = core_tokens(c)
        out[b, a:a + T, :] = res.results[c]["outT"].T
    return out


def run_fused(inp):
    x = inp["x"]
    nc = get_nc("F", build_fused)
    cparts = [
        col_layout(inp["mix_norm_g"][0]), col_layout(inp["ffn_norm_g"][0]),
        col_layout(inp["mix_norm_g"][1]), col_layout(inp["ffn_norm_g"][1]),
        col_layout(inp["final_norm_g"]),
        conv_layout(inp["sc_conv_w"][0], inp["sc_conv_b"][0]).reshape(128, -1),
        conv_layout(inp["ffn_conv_w"][0], inp["ffn_conv_b"][0]).reshape(128, -1),
        conv_layout(inp["ffn_conv_w"][1], inp["ffn_conv_b"][1]).reshape(128, -1),
    ]
    shared = {
        "w_in": np.ascontiguousarray(inp["sc_w_in"][0]),
        "w_out": np.ascontiguousarray(inp["sc_w_out"][0]),
        "w_up0": np.ascontiguousarray(inp["ffn_w_up"][0]),
        "w_down0": np.ascontiguousarray(inp["ffn_w_down"][0]),
        "w_qkv": np.ascontiguousarray(inp["attn_w_qkv"][0]),
        "w_o": np.ascontiguousarray(inp["attn_w_out"][0]),
        "w_up1": np.ascontiguousarray(inp["ffn_w_up"][1]),
        "w_down1": np.ascontiguousarray(inp["ffn_w_down"][1]),
        "etab": alibi_tables(),
    }
    in_maps = []
    for c in range(NCORES):
        b, a = core_tokens(c)
        m = dict(shared)
        m["xin"] = padded_slice_T(x, b, a - 2, a + T + 2)
        m["kbias"] = key_bias(a)
        vm = np.zeros((128, 2), np.float32)
        vm[:, 0] = 1.0 if a > 0 else 0.0
        vm[:, 1] = 1.0 if a + T < SEQ else 0.0
        m["cst"] = np.ascontiguousarray(np.concatenate(cparts + [vm], axis=1))
        in_maps.append(m)
    res = run_bass_kernel_spmd(nc, in_maps, core_ids=list(range(NCORES)))
    out = np.empty_like(x)
    for c in range(NCORES):
        b, a = core_tokens(c)
        out[b, a:a + T, :] = res.results[c]["outT"].T
    return out


FUSED = True


def kernel(**inputs):
    inp = {k: np.asarray(v) for k, v in inputs.items()}
    if FUSED:
        return run_fused(inp).astype(np.float32)
    x1 = run_A(inp)
    x1p = run_B(inp, x1)
    return run_C(inp, x1p).astype(np.float32)
```
